# Optimizing a Trainium2 kernel written in Bass

```python
import jax, jax.numpy as jnp
from jax import lax
import numpy as np

D_MODEL = 1024
BATCH = 8
SEQ = 2048
DEPTH = 4
DEC_BATCH = 128
DEC_SEQ = 4
PAST_LEN = 2048
PAGE_SIZE = 128

N_MIXERS = 2
N_A_LAYERS = (DEPTH + N_MIXERS - 1) // N_MIXERS
N_B_LAYERS = DEPTH // N_MIXERS
CHUNK = 128
D_A = D_MODEL
N_GROUPS_A = 8
GROUP_DIM_A = D_A // N_GROUPS_A
N_HEADS = 8
HEAD_DIM = D_MODEL // N_HEADS
N_KV_HEADS = 2
N_IDX_HEADS = 8
IDX_DIM = 64
TOPK_MAX = 256
QUERY_BLOCK = 128
ROPE_THETA = 500000.0
ROT_DIV = 4
IDX_SCALE = float((IDX_DIM * N_IDX_HEADS) ** -0.5)
ATTN_SCALE = float(HEAD_DIM ** -0.5)
Q_W = N_HEADS * HEAD_DIM
KV_W = N_KV_HEADS * HEAD_DIM
QI_W = N_IDX_HEADS * IDX_DIM
B_PROJ = Q_W + 2 * KV_W + QI_W + IDX_DIM + N_IDX_HEADS
D_FF = 4 * D_MODEL
EPS = 1e-6

kernel_name = 'dsa_chunkmlp_hybrid_step'


def rms_norm(x, g):
    xf = x.astype(jnp.float32)
    y = xf * lax.rsqrt(jnp.mean(xf * xf, axis=-1, keepdims=True) + EPS)
    return y.astype(x.dtype) * g


def rope(x, pos):
    dim = x.shape[-1]
    r = dim // ROT_DIV
    half = r // 2
    inv = ROPE_THETA ** (-jnp.arange(half, dtype=jnp.float32) * 2.0 / r)
    ang = pos.astype(jnp.float32)[:, None] * inv[None, :]
    cos = jnp.cos(ang)[None, :, None, :]
    sin = jnp.sin(ang)[None, :, None, :]
    xr = x[..., :r].astype(jnp.float32)
    x1, x2 = xr[..., :half], xr[..., half:]
    rot = jnp.concatenate([x1 * cos - x2 * sin, x2 * cos + x1 * sin], axis=-1).astype(x.dtype)
    return jnp.concatenate([rot, x[..., r:]], axis=-1)


def sqrelu_mlp(h, w1, w2):
    a = jax.nn.relu(h @ w1)
    return (a * a) @ w2


def chunk_mlp(h, w_in, v_gain, w_s, b_s, w_out):
    N, T, _ = h.shape
    uv = jax.nn.gelu(h @ w_in)
    u, v = uv[..., :D_A], uv[..., D_A:]
    vf = v.astype(jnp.float32)
    mu = jnp.mean(vf, axis=-1, keepdims=True)
    var = jnp.mean(jnp.square(vf - mu), axis=-1, keepdims=True)
    v = ((vf - mu) * lax.rsqrt(var + EPS)).astype(h.dtype) * v_gain
    c = CHUNK if T >= CHUNK else T
    n_chunks = -(-T // c)
    tp = n_chunks * c
    vp = jnp.pad(v, ((0, 0), (0, tp - T), (0, 0))).reshape(N, n_chunks, c, N_GROUPS_A, GROUP_DIM_A)
    mask = jnp.tril(jnp.ones((c, c), dtype=bool))
    ws = jnp.where(mask[None], w_s[:, :c, :c], jnp.zeros((), w_s.dtype))
    mixed = jnp.einsum('gts,ncsgd->nctgd', ws, vp) + b_s[:, :c].T[None, None, :, :, None]
    mixed = mixed.reshape(N, tp, D_A)[:, :T]
    out = (u * mixed) @ w_out
    start = ((T - 1) // CHUNK) * CHUNK
    return out, v[:, start:]


def dsa_project(h, w_in, pos):
    N, T, _ = h.shape
    p = h @ w_in
    o1 = Q_W
    o2 = o1 + KV_W
    o3 = o2 + KV_W
    o4 = o3 + QI_W
    o5 = o4 + IDX_DIM
    q = rope(p[..., :o1].reshape(N, T, N_HEADS, HEAD_DIM), pos)
    k = rope(p[..., o1:o2].reshape(N, T, N_KV_HEADS, HEAD_DIM), pos)
    v = p[..., o2:o3].reshape(N, T, N_KV_HEADS, HEAD_DIM)
    qi = rope(p[..., o3:o4].reshape(N, T, N_IDX_HEADS, IDX_DIM), pos)
    ki = rope(p[..., o4:o5].reshape(N, T, 1, IDX_DIM), pos)[:, :, 0]
    wi = p[..., o5:]
    return q, k, v, qi, ki, wi


def index_scores(qi, wi, ki):
    dots = jnp.einsum('nthd,nsd->nths', qi, ki, preferred_element_type=jnp.float32)
    return jnp.einsum('nths,nth->nts', jax.nn.relu(dots), wi.astype(jnp.float32)) * IDX_SCALE


def attend(q, kg, vg, valid):
    N, T = q.shape[0], q.shape[1]
    qg = q.reshape(N, T, N_KV_HEADS, N_HEADS // N_KV_HEADS, HEAD_DIM)
    s = jnp.einsum('ntkgd,ntjkd->ntkgj', qg, kg, preferred_element_type=jnp.float32) * ATTN_SCALE
    s = jnp.where(valid[:, :, None, None, :], s, -jnp.inf)
    p = jax.nn.softmax(s, axis=-1)
    o = jnp.einsum('ntkgj,ntjkd->ntkgd', p.astype(vg.dtype), vg)
    return o.reshape(N, T, N_HEADS * HEAD_DIM).astype(q.dtype)


def dsa_prompt(h, w_in, w_out):
    N, T, _ = h.shape
    pos = jnp.arange(T)
    q, k, v, qi, ki, wi = dsa_project(h, w_in, pos)
    qb = min(QUERY_BLOCK, T)
    nb = T // qb
    k_top = min(TOPK_MAX, T // 4)
    bi = jnp.arange(N)[:, None, None]

    def block(args):
        q_b, qi_b, wi_b, start = args
        tq = start + jnp.arange(qb)
        sc = index_scores(qi_b, wi_b, ki)
        sc = jnp.where(pos[None, None, :] <= tq[None, :, None], sc, -jnp.inf)
        _, idx = lax.top_k(sc, k_top)
        valid = idx <= tq[None, :, None]
        return attend(q_b, k[bi, idx], v[bi, idx], valid)

    def to_blocks(a):
        return a.reshape(N, nb, qb, *a.shape[2:]).swapaxes(0, 1)

    o = lax.map(block, (to_blocks(q), to_blocks(qi), to_blocks(wi), jnp.arange(nb) * qb))
    o = o.swapaxes(0, 1).reshape(N, T, Q_W)
    return o @ w_out, k, v, ki


def dsa_sample(h, w_in, w_out, cache_k, cache_v, cache_kidx, page_table, layer):
    N, T, _ = h.shape
    n_pages = page_table.shape[1]
    past = n_pages * PAGE_SIZE
    pos = past + jnp.arange(T)
    q, k, v, qi, ki, wi = dsa_project(h, w_in, pos)
    past_ki = cache_kidx[layer, page_table].reshape(N, past, IDX_DIM)
    all_ki = jnp.concatenate([past_ki, ki.astype(past_ki.dtype)], axis=1)
    L = past + T
    k_top = min(TOPK_MAX, L // 4)
    sc = index_scores(qi, wi, all_ki)
    key_pos = jnp.arange(L)
    sc = jnp.where(key_pos[None, None, :] <= pos[None, :, None], sc, -jnp.inf)
    _, idx = lax.top_k(sc, k_top)
    bi = jnp.arange(N)[:, None, None]
    in_past = (idx < past)[..., None, None]
    pidx = jnp.minimum(idx, past - 1)
    page = page_table[bi, pidx // PAGE_SIZE]
    off = pidx % PAGE_SIZE
    nidx = jnp.clip(idx - past, 0, T - 1)
    kg = jnp.where(in_past, cache_k[layer, page, off], k[bi, nidx].astype(cache_k.dtype))
    vg = jnp.where(in_past, cache_v[layer, page, off], v[bi, nidx].astype(cache_v.dtype))
    valid = idx <= pos[None, :, None]
    o = attend(q, kg, vg, valid)
    return o @ w_out, k, v, ki


def setup_inputs(seed: int = 0) -> dict:
    key = jax.random.key(seed)
    ks = jax.random.split(key, 20)
    f32 = jnp.float32
    n_pages = PAST_LEN // PAGE_SIZE
    used = DEC_BATCH * n_pages
    n_pool = used + max(1, used // 4)
    nrm = lambda k, shp, s: jax.random.normal(k, shp, f32) * s
    page_table = jax.random.permutation(ks[0], n_pool)[:used].reshape(DEC_BATCH, n_pages).astype(jnp.int32)
    return {
        'x_prompt': nrm(ks[1], (BATCH, SEQ, D_MODEL), 1.0),
        'x_sample': nrm(ks[2], (DEC_BATCH, DEC_SEQ, D_MODEL), 1.0),
        'cache_k': nrm(ks[3], (N_B_LAYERS, n_pool, PAGE_SIZE, N_KV_HEADS, HEAD_DIM), 1.0),
        'cache_v': nrm(ks[4], (N_B_LAYERS, n_pool, PAGE_SIZE, N_KV_HEADS, HEAD_DIM), 1.0),
        'cache_kidx': nrm(ks[5], (N_B_LAYERS, n_pool, PAGE_SIZE, IDX_DIM), 1.0),
        'page_table': page_table,
        'norm_mix': 1.0 + nrm(ks[6], (DEPTH, D_MODEL), 0.02),
        'norm_ffn': 1.0 + nrm(ks[7], (DEPTH, D_MODEL), 0.02),
        'a_w_in': nrm(ks[8], (N_A_LAYERS, D_MODEL, 2 * D_A), D_MODEL ** -0.5),
        'a_v_gain': 1.0 + nrm(ks[9], (N_A_LAYERS, D_A), 0.02),
        'a_w_s': nrm(ks[10], (N_A_LAYERS, N_GROUPS_A, CHUNK, CHUNK), 0.5 * CHUNK ** -0.5),
        'a_b_s': 1.0 + nrm(ks[11], (N_A_LAYERS, N_GROUPS_A, CHUNK), 0.02),
        'a_w_out': nrm(ks[12], (N_A_LAYERS, D_A, D_MODEL), D_A ** -0.5),
        'b_w_in': nrm(ks[13], (N_B_LAYERS, D_MODEL, B_PROJ), D_MODEL ** -0.5),
        'b_w_out': nrm(ks[14], (N_B_LAYERS, Q_W, D_MODEL), Q_W ** -0.5),
        'ffn_w1': nrm(ks[15], (DEPTH, D_MODEL, D_FF), D_MODEL ** -0.5),
        'ffn_w2': nrm(ks[16], (DEPTH, D_FF, D_MODEL), D_FF ** -0.5),
        'norm_final': 1.0 + nrm(ks[17], (D_MODEL,), 0.02),
    }


def reference(x_prompt, x_sample, cache_k, cache_v, cache_kidx, page_table, norm_mix, norm_ffn,
              a_w_in, a_v_gain, a_w_s, a_b_s, a_w_out, b_w_in, b_w_out, ffn_w1, ffn_w2, norm_final):
    xp, xs = x_prompt, x_sample
    kp_l, vp_l, kip_l, ks_l, vs_l, kis_l, cvp_l, cvs_l = [], [], [], [], [], [], [], []
    for i in range(DEPTH):
        j = i // N_MIXERS
        hp = rms_norm(xp, norm_mix[i])
        hs = rms_norm(xs, norm_mix[i])
        if i % N_MIXERS == 0:
            op, cvp = chunk_mlp(hp, a_w_in[j], a_v_gain[j], a_w_s[j], a_b_s[j], a_w_out[j])
            os_, cvs = chunk_mlp(hs, a_w_in[j], a_v_gain[j], a_w_s[j], a_b_s[j], a_w_out[j])
            cvp_l.append(cvp)
            cvs_l.append(cvs)
        else:
            op, kp, vp, kip = dsa_prompt(hp, b_w_in[j], b_w_out[j])
            os_, ks, vs, kis = dsa_sample(hs, b_w_in[j], b_w_out[j], cache_k, cache_v, cache_kidx, page_table, j)
            kp_l.append(kp)
            vp_l.append(vp)
            kip_l.append(kip)
            ks_l.append(ks)
            vs_l.append(vs)
            kis_l.append(kis)
        xp = xp + op
        xs = xs + os_
        xp = xp + sqrelu_mlp(rms_norm(xp, norm_ffn[i]), ffn_w1[i], ffn_w2[i])
        xs = xs + sqrelu_mlp(rms_norm(xs, norm_ffn[i]), ffn_w1[i], ffn_w2[i])
    y_prompt = rms_norm(xp, norm_final)
    y_sample = rms_norm(xs, norm_final)
    new_k_prompt = jnp.stack(kp_l)
    new_v_prompt = jnp.stack(vp_l)
    new_kidx_prompt = jnp.stack(kip_l)
    new_k_sample = jnp.stack(ks_l)
    new_v_sample = jnp.stack(vs_l)
    new_kidx_sample = jnp.stack(kis_l)
    chunk_v_prompt = jnp.stack(cvp_l)
    chunk_v_sample = jnp.stack(cvs_l)
    return (y_prompt, y_sample, new_k_prompt, new_v_prompt, new_kidx_prompt, new_k_sample, new_v_sample, new_kidx_sample, chunk_v_prompt, chunk_v_sample)
```

```python
import math
import numpy as np
from contextlib import ExitStack
import concourse.bass as bass
import concourse.mybir as mybir
from concourse.bass_utils import run_bass_kernel_spmd

F32 = mybir.dt.float32
BF16 = mybir.dt.bfloat16
I32 = mybir.dt.int32
AF = mybir.ActivationFunctionType
ALU = mybir.AluOpType
AX = mybir.AxisListType

D = 1024
KC = 8
SEQ = 2048
NS = 64
NT = SEQ + NS
NB = 16
BP = 2120
EPS = 1e-6
ATTN_SCALE = float(128 ** -0.5)
NEG = -1.0e30
NITER = 20
TOPK = 256
NPOOL = 2560
USE_AG = False

ENGS = ("tensor", "vector", "scalar", "gpsimd", "sync")
SEM_LIMIT = 24000
NDMA_SEM = 10


class Prog:
    def __init__(self, nc):
        self.nc = nc
        self.ops = []

    def op(self, eng, fn, reads=(), writes=(), dma=False, semkey=None, nobar=False, inc=16):
        self.ops.append(dict(eng=eng, fn=fn, reads=tuple(reads), writes=tuple(writes), dma=dma, semkey=semkey, nobar=nobar, inc=inc))

    def barrier(self):
        self.ops.append(dict(eng=None, barrier=True))

    def finalize(self, stack):
        nc = self.nc
        ops = self.ops
        n = len(ops)
        last_writer = {}
        readers = {}
        deps = [None] * n
        needed = [False] * n
        last_on_eng = {}
        outstanding_dma = []
        bar_deps = []
        fresh = {}
        for i, o in enumerate(ops):
            if o.get("barrier"):
                bar_deps = list(last_on_eng.values()) + list(outstanding_dma)
                outstanding_dma = []
                last_writer = {k: v for k, v in last_writer.items() if k.startswith("G:")}
                readers = {k: v for k, v in readers.items() if k.startswith("G:")}
                fresh = {e: True for e in ENGS}
                continue
            d = set()
            for k in o["reads"]:
                if k in last_writer:
                    d.add(last_writer[k])
            for k in o["writes"]:
                if k in last_writer:
                    d.add(last_writer[k])
                for r in readers.get(k, ()):
                    d.add(r)
            if bar_deps and fresh.get(o["eng"], False):
                d.update(bar_deps)
                fresh[o["eng"]] = False
            d.discard(i)
            if o["eng"] == "tensor" and not o["dma"]:
                d = {j for j in d if not (ops[j]["eng"] == "tensor" and not ops[j]["dma"])}
            deps[i] = d
            for j in d:
                needed[j] = True
            for k in o["reads"]:
                readers.setdefault(k, []).append(i)
            for k in o["writes"]:
                last_writer[k] = i
                readers[k] = []
            if o["dma"]:
                if not o.get("nobar"):
                    outstanding_dma.append(i)
                needed[i] = True
            else:
                last_on_eng[o["eng"]] = i

        def newsem(name):
            return stack.enter_context(nc.semaphore(name))
        eng_sems = {e: [newsem(f"s_{e}_0")] for e in ENGS}
        eng_cnt = {e: 0 for e in ENGS}
        dma_sems = {e: [newsem(f"d_{e}_{k}") for k in range(NDMA_SEM)] for e in ("sync", "gpsimd", "scalar")}
        dma_cnt = {e: [0] * NDMA_SEM for e in dma_sems}
        dma_rr = {e: 0 for e in dma_sems}
        dma_last = {e: [None] * NDMA_SEM for e in dma_sems}
        event = [None] * n
        extra_dep = [None] * n
        ded_sems = {}
        for i, o in enumerate(ops):
            if o.get("barrier"):
                continue
            e = o["eng"]
            if o["dma"] and o.get("semkey"):
                sk = o["semkey"]
                if sk not in ded_sems:
                    ded_sems[sk] = [newsem("g_" + sk.replace(".", "_")), 0]
                ded_sems[sk][1] += o["inc"]
                event[i] = (ded_sems[sk][0], ded_sems[sk][1])
                o["sem"] = ded_sems[sk][0]
            elif o["dma"]:
                k = dma_rr[e]
                dma_rr[e] = (k + 1) % NDMA_SEM
                if dma_last[e][k] is not None:
                    extra_dep[i] = dma_last[e][k]
                dma_cnt[e][k] += 16
                event[i] = (dma_sems[e][k], dma_cnt[e][k])
                dma_last[e][k] = event[i]
                o["sem"] = dma_sems[e][k]
            elif needed[i]:
                if eng_cnt[e] >= SEM_LIMIT:
                    eng_sems[e].append(newsem(f"s_{e}_{len(eng_sems[e])}"))
                    eng_cnt[e] = 0
                eng_cnt[e] += 1
                event[i] = (eng_sems[e][-1], eng_cnt[e])
                o["sem"] = eng_sems[e][-1]
        seen = {e: {} for e in ENGS}
        for i, o in enumerate(ops):
            if o.get("barrier"):
                continue
            e = o["eng"]
            evs = [event[j] for j in deps[i]]
            if extra_dep[i] is not None:
                evs.append(extra_dep[i])
            best = {}
            for (s, v) in evs:
                key = id(s)
                if v > seen[e].get(key, 0) and v > best.get(key, (None, 0))[1]:
                    best[key] = (s, v)
            o["waits"] = list(best.values())
            for key, (s, v) in best.items():
                seen[e][key] = v
        self.final_events = [(v[0], v[1]) for v in ded_sems.values()]
        for e in dma_sems:
            for k in range(NDMA_SEM):
                if dma_last[e][k] is not None:
                    self.final_events.append(dma_last[e][k])

    def emit(self, block):
        ops = self.ops
        final_events = self.final_events

        def run(engname, eng):
            for o in ops:
                if o.get("barrier") or o["eng"] != engname:
                    continue
                for (s, v) in o["waits"]:
                    eng.wait_ge(s, v)
                ins = o["fn"](eng)
                if o["dma"] and o["inc"] == 1:
                    ins.then_inc(o["sem"])
                elif o["dma"]:
                    ins.then_inc(o["sem"], 16)
                elif "sem" in o:
                    ins.then_inc(o["sem"], 1)
            if engname == "sync":
                for (s, v) in final_events:
                    eng.wait_ge(s, v)

        @block.tensor
        def _(t):
            run("tensor", t)

        @block.vector
        def _(v):
            run("vector", v)

        @block.scalar
        def _(a):
            run("scalar", a)

        @block.gpsimd
        def _(g):
            run("gpsimd", g)

        @block.sync
        def _(s):
            run("sync", s)


class Arena:
    def __init__(self, t, nwords, base=0):
        self.t = t
        self.n = nwords
        self.off = base
        self.hi = 0

    def alloc(self, shape, dt=F32):
        free = 1
        for s in shape[1:]:
            free *= s
        words = free if dt != BF16 else (free + 1) // 2
        v = self.t[0:shape[0], self.off:self.off + words]
        if dt == BF16:
            v = v.bitcast(BF16)
            if free != 2 * words:
                v = v[:, 0:free]
        elif dt == I32:
            v = v.bitcast(I32)
        self.off += words
        self.hi = max(self.hi, self.off)
        assert self.off <= self.n, f"arena overflow {self.off} > {self.n}"
        if len(shape) == 3:
            v = v.rearrange("p (a b) -> p a b", a=shape[1])
        elif len(shape) == 4:
            v = v.rearrange("p (a b c) -> p a b c", a=shape[1], b=shape[2])
        return v

    def mark(self):
        return self.off

    def reset(self, m):
        self.off = m


DEBUG = {}
GROUPS = [(0, 512), (512, 512), (1024, 512), (1536, 512), (2048, 64)]


def tkeys(prefix, t0, n):
    return [f"{prefix}.{tt}" for tt in range(t0 // 128, (t0 + n + 127) // 128)]


def build_nc(n_layers=4, do_b=True, do_samp=True):
    nc = bass.Bass("TRN2", target_bir_lowering=False)
    st = ExitStack()
    I = {}
    O = {}

    def inp(name, shape, dt=F32):
        I[name] = nc.dram_tensor(name, list(shape), dt, kind="ExternalInput").ap()
        return I[name]

    def outp(name, shape, dt=F32):
        O[name] = nc.dram_tensor(name, list(shape), dt, kind="ExternalOutput").ap()
        return O[name]

    xp = inp("xp", [SEQ, D]); xs = inp("xs", [NS, D])
    SHR = NPOOL * 8 // 8
    if do_b and do_samp:
        if USE_AG:
            ck_sh = inp("ck", [2 * SHR, 16 * 256]); cv_sh = inp("cv", [2 * SHR, 16 * 256]); cki_sh = inp("cki", [2 * SHR, 16 * 64])
            pool_in = {}; pool_full = {}
            POOLS = [("cki", 0, 1024, 0)] + [(nm, hf, 2048, hf * 2048) for nm in ("ck", "cv") for hf in range(2)]
            for nm, hf, w, c0 in POOLS:
                for jj in range(2):
                    pool_in[nm, hf, jj] = nc.dram_tensor(f"{nm}{hf}_in{jj}", [SHR, w], F32, kind="Internal").ap()
                    pool_full[nm, hf, jj] = nc.dram_tensor(f"{nm}{hf}_full{jj}", [NPOOL * 8, w], F32, kind="Internal").ap()
        else:
            ck = inp("ck", [2 * NPOOL * 8, 16 * 256]); cv = inp("cv", [2 * NPOOL * 8, 16 * 256]); cki = inp("cki", [2 * NPOOL * 8, 16 * 64])
    pt = inp("pt", [NB, 16], I32)
    gmix = inp("gmix", [128, 32]); gffn = inp("gffn", [128, 32]); gfin = inp("gfin", [128, 8])
    a_w_in = inp("a_w_in", [2, D, 2048]); a_v_gain = inp("a_v_gain", [2, D])
    a_wsT = inp("a_wsT", [2, 128, 8 * 128]); a_wsTs = inp("a_wsTs", [2, 64, 8 * 64])
    a_bs = inp("a_bs", [2, 8 * 128]); a_bss = inp("a_bss", [2, 8 * 64])
    a_w_out = inp("a_w_out", [2, D, D])
    b_w_in = inp("b_w_in", [2, D, BP]); b_w_out = inp("b_w_out", [2, D, D])
    ffn_w1 = inp("ffn_w1", [4, D, 4096]); ffn_w2 = inp("ffn_w2", [4, 4096, D])
    c_ident = inp("c_ident", [128, 128]); c_tri01 = inp("c_tri01", [128, 128]); c_tribias = inp("c_tribias", [128, 128])
    c_blk01 = inp("c_blk01", [64, 64]); c_blkbias = inp("c_blkbias", [64, 64])
    c_ropeP = inp("c_ropeP", [128, 16 * 48]); c_ropeS = inp("c_ropeS", [64, 48])
    c_q8 = inp("c_q8", [128, 1], I32); c_pow2 = inp("c_pow2", [128, NITER + 1])
    c_rowmask = inp("c_rowmask", [64, 16])

    y_p = outp("y_p", [SEQ, D]); y_s = outp("y_s", [NS, D])
    nk_p = outp("nk_p", [2, SEQ, 256]); nv_p = outp("nv_p", [2, SEQ, 256]); nki_p = outp("nki_p", [2, SEQ, 64])
    nk_s = outp("nk_s", [2, NS, 256]); nv_s = outp("nv_s", [2, NS, 256]); nki_s = outp("nki_s", [2, NS, 64])
    cv_p = outp("cv_p", [2, 128, D]); cv_s = outp("cv_s", [2, NS, D])

    with st:
        ARW = 53100
        arena_t = st.enter_context(nc.sbuf_tensor("arena", [128, ARW], F32))
        A = Arena(arena_t, ARW)
        psf = [st.enter_context(nc.psum_tensor(f"psf{i}", [128, 512], F32)) for i in range(6)]
        psb = [st.enter_context(nc.psum_tensor(f"psb{i}", [128, 1024], BF16)) for i in range(2)]
        P = Prog(nc)
        rr = {"f": 0, "b": 0, "eng": 0}

        def PS(pool=None):
            k = rr["f"]
            rr["f"] = (k + 1) % 4
            return psf[k], f"psf{k}"

        def PSB():
            k = rr["b"]
            rr["b"] = (k + 1) % DEBUG.get("npsb", 2)
            return psb[k], f"psb{k}"

        def alt(*engs):
            rr["eng"] += 1
            return engs[rr["eng"] % len(engs)]

        def copy_op(eng, out, in_, reads, writes):
            if eng == "scalar":
                P.op("scalar", lambda e: e.activation(out=out, in_=in_, func=AF.Copy), reads, writes)
            else:
                P.op(eng, lambda e: e.tensor_copy(out, in_), reads, writes)

        xT = A.alloc([128, KC, NT])
        xn_off = A.mark()
        xnT = A.alloc([128, KC, NT], BF16)
        xn_end = A.mark()
        identf = A.alloc([128, 128]); identb = A.alloc([128, 128], BF16); onesb = A.alloc([128, 128], BF16)
        tri01 = A.alloc([128, 128], BF16); tribias = A.alloc([128, 128])
        blk01 = A.alloc([64, 64], BF16); blkbias = A.alloc([64, 64])
        ropeP = A.alloc([128, 16, 48]); ropeS = A.alloc([64, 48])
        gmix_t = A.alloc([128, 32]); gffn_t = A.alloc([128, 32]); gfin_t = A.alloc([128, 8])
        q8 = A.alloc([128, 1], I32); pow2 = A.alloc([128, NITER + 1])
        rowmask = A.alloc([64, 16])
        stage = A.alloc([128, 128])

        def load(dst, src, key, eng="sync", **kw):
            P.op(eng, lambda e: e.dma_start(out=dst, in_=src, **kw), writes=[key], dma=True)

        load(identf, c_ident[:, :], "identf"); load(tribias, c_tribias[:, :], "tribias")
        load(blkbias, c_blkbias[:, :], "blkbias")
        load(ropeP, c_ropeP.rearrange("p (a b) -> p a b", a=16), "rope"); load(ropeS, c_ropeS[:, :], "rope")
        load(gmix_t, gmix[:, :], "gains"); load(gffn_t, gffn[:, :], "gains"); load(gfin_t, gfin[:, :], "gains")
        load(q8, c_q8[:, :], "q8"); load(pow2, c_pow2[:, :], "pow2")
        load(rowmask, c_rowmask[:, :], "rowmask")
        load(identb, c_ident[:, :], "identb", eng="gpsimd")
        load(tri01, c_tri01[:, :], "tri01", eng="gpsimd")
        load(blk01, c_blk01[:, :], "blk01", eng="gpsimd")
        P.op("vector", lambda e: e.memset(onesb, 1.0), writes=["onesb"])

        NCH = 4
        def pool_copies():
            if not (do_b and do_samp and USE_AG):
                return
            srcs = {"cki": cki_sh, "ck": ck_sh, "cv": cv_sh}
            for jj in range(2):
                for nm, hf, w, c0 in POOLS:
                    rows = SHR // NCH
                    for ch in range(NCH):
                        dst = pool_in[nm, hf, jj][ch * rows:(ch + 1) * rows, :]
                        src = srcs[nm][jj * SHR + ch * rows:jj * SHR + (ch + 1) * rows, c0:c0 + w]
                        P.op("sync", lambda e, dst=dst, src=src: e.dma_start(out=dst, in_=src), writes=[f"G:{nm}{hf}_in{jj}.{ch}"], dma=True, semkey=f"cp.{nm}{hf}.{jj}", nobar=True)

        def pool_allgather():
            if not (do_b and do_samp and USE_AG):
                return
            for jj in range(2):
                for nm, hf, w, c0 in POOLS:
                    pin = pool_in[nm, hf, jj]; pfull = pool_full[nm, hf, jj]
                    P.op("gpsimd", lambda e, pin=pin, pfull=pfull: e.collective_compute("AllGather", ALU.bypass, replica_groups=[list(range(8))], ins=[pin.opt()], outs=[pfull.opt()]),
                         reads=[f"G:{nm}{hf}_in{jj}.{ch}" for ch in range(NCH)], writes=[f"G:{nm}{hf}_full{jj}"], dma=True, semkey=f"ag.{nm}{hf}.{jj}", nobar=True, inc=1)

        pool_copies()
        m0 = A.mark()
        xtok = [A.alloc([128, D]) for _ in range(2)]
        for tt in range(17):
            n = 128 if tt < 16 else NS
            xb = xtok[tt % 2]
            src = xp[tt * 128:(tt + 1) * 128, :] if tt < 16 else xs[:, :]
            P.op("sync", lambda e, xb=xb, src=src, n=n: e.dma_start(out=xb[0:n, :], in_=src), writes=[f"xtok{tt % 2}"], dma=True)
            for half in range(2):
                ps, pk = PS()
                for cc in range(4):
                    c = half * 4 + cc
                    P.op("tensor", lambda e, ps=ps, xb=xb, c=c, cc=cc, n=n: e.transpose(ps[:, cc * 128:cc * 128 + n], xb[0:n, c * 128:(c + 1) * 128], identf[0:n, 0:n]),
                         reads=[f"xtok{tt % 2}", "identf"], writes=[pk])
                dst = xT[:, half * 4:half * 4 + 4, tt * 128:tt * 128 + n]
                srcp = ps[:, :].rearrange("p (a b) -> p a b", a=4)[:, :, 0:n]
                copy_op(alt("vector", "scalar"), dst, srcp, [pk], [f"xT.{tt}"])
        A.reset(m0)
        P.barrier()

        def rmsnorm(gain, gcol0, dst_key="xn"):
            m = A.mark()
            sq = [A.alloc([128, KC, 512], BF16) for _ in range(1)] * 2
            rs = [A.alloc([128, 512]) for _ in range(1)] * 2
            for gi, (t0, n) in enumerate(GROUPS):
                gi = 0
                s_ = sq[gi % 2]; r_ = rs[gi % 2]
                xk = tkeys("xT", t0, n)
                P.op("scalar", lambda e, s_=s_, t0=t0, n=n: e.activation(out=s_[:, :, 0:n], in_=xT[:, :, t0:t0 + n], func=AF.Square),
                     reads=xk, writes=[f"sq{gi % 2}"])
                ps, pk = PS()
                for c in range(KC):
                    P.op("tensor", lambda e, ps=ps, s_=s_, c=c, n=n: e.matmul(ps[:, 0:n], onesb, s_[:, c, 0:n], start=(c == 0), stop=(c == KC - 1)),
                         reads=[f"sq{gi % 2}", "onesb"], writes=[pk])
                P.op("scalar", lambda e, ps=ps, r_=r_, n=n: e.activation(out=r_[:, 0:n], in_=ps[:, 0:n], func=AF.Sqrt, scale=1.0 / D, bias=epsb[:, 0:1]),
                     reads=[pk, "epsb"], writes=[f"rs{gi % 2}"])
                P.op("vector", lambda e, r_=r_, n=n: e.reciprocal(r_[:, 0:n], r_[:, 0:n]), reads=[f"rs{gi % 2}"], writes=[f"rs{gi % 2}"])
                for c in range(KC):
                    eng = "vector"
                    P.op(eng, lambda e, c=c, t0=t0, n=n, r_=r_: e.scalar_tensor_tensor(out=xnT[:, c, t0:t0 + n], in0=xT[:, c, t0:t0 + n], scalar=gain[:, gcol0 + c:gcol0 + c + 1], in1=r_[:, 0:n], op0=ALU.mult, op1=ALU.mult),
                         reads=xk + [f"rs{gi % 2}", "gains"], writes=tkeys(dst_key, t0, n))
            A.reset(m)

        epsb = A.alloc([128, 1])
        P.op("vector", lambda e: e.memset(epsb, EPS), writes=["epsb"])

        wload_eng = "gpsimd"

        def ffn(l):
            m = A.mark()
            w1g = [A.alloc([128, KC, 1024], BF16) for _ in range(2)]
            w2g = [A.alloc([128, KC, 1024], BF16) for _ in range(2)]
            hT = [A.alloc([128, KC, 512], BF16) for _ in range(2)]
            rl = [A.alloc([128, 512], BF16) for _ in range(2)]
            rmsnorm(gffn_t, l * 8)
            cnt = 0
            for fg in range(4):
                sl = fg % 2
                for kc in range(KC):
                    P.op(wload_eng, lambda e, sl=sl, kc=kc, fg=fg: e.dma_start(out=w1g[sl][:, kc, :], in_=ffn_w1[l, kc * 128:(kc + 1) * 128, fg * 1024:(fg + 1) * 1024]),
                         writes=[f"w1g{sl}"], dma=True)
                for fc in range(KC):
                    P.op(wload_eng, lambda e, sl=sl, fc=fc, fg=fg: e.dma_start(out=w2g[sl][:, fc, :], in_=ffn_w2[l, fg * 1024 + fc * 128:fg * 1024 + (fc + 1) * 128, :]),
                         writes=[f"w2g{sl}"], dma=True)
                for (t0, n) in GROUPS:
                    hs = cnt % 2
                    cnt += 1
                    for fc in range(KC):
                        ps, pk = PS()
                        for kc in range(KC):
                            P.op("tensor", lambda e, ps=ps, sl=sl, kc=kc, fc=fc, t0=t0, n=n: e.matmul(ps[:, 0:n], w1g[sl][:, kc, fc * 128:(fc + 1) * 128], xnT[:, kc, t0:t0 + n], start=(kc == 0), stop=(kc == KC - 1)),
                                 reads=[f"w1g{sl}"] + tkeys("xn", t0, n), writes=[pk])
                        rb = rl[fc % 2]
                        P.op("scalar", lambda e, ps=ps, rb=rb, n=n: e.activation(out=rb[:, 0:n], in_=ps[:, 0:n], func=AF.Relu), reads=[pk], writes=[f"rl{fc % 2}"])
                        P.op("gpsimd", lambda e, rb=rb, hs=hs, fc=fc, n=n: e.tensor_tensor(out=hT[hs][:, fc, 0:n], in0=rb[:, 0:n], in1=rb[:, 0:n], op=ALU.mult),
                             reads=[f"rl{fc % 2}"], writes=[f"hT{hs}.{fc}"])
                    for dmc in range(KC):
                        ps, pk = PS()
                        for fc in range(KC):
                            P.op("tensor", lambda e, ps=ps, sl=sl, fc=fc, dmc=dmc, hs=hs, n=n: e.matmul(ps[:, 0:n], w2g[sl][:, fc, dmc * 128:(dmc + 1) * 128], hT[hs][:, fc, 0:n], start=(fc == 0), stop=(fc == KC - 1)),
                                 reads=[f"w2g{sl}", f"hT{hs}.{fc}"], writes=[pk])
                        P.op("vector", lambda e, ps=ps, dmc=dmc, t0=t0, n=n: e.tensor_tensor(out=xT[:, dmc, t0:t0 + n], in0=xT[:, dmc, t0:t0 + n], in1=ps[:, 0:n], op=ALU.add),
                             reads=[pk] + tkeys("xT", t0, n), writes=tkeys("xT", t0, n))
            A.reset(m)
            P.barrier()

        def layer_a(j, l):
            m = A.mark()
            w_in = A.alloc([128, KC, 2048], BF16)
            w_out = A.alloc([128, KC, D], BF16)
            wsT = A.alloc([128, 8, 128], BF16); wsTs = A.alloc([64, 8, 64], BF16)
            bsb = A.alloc([128, 8, 128]); bsbs = A.alloc([128, 8, 64])
            vgain = A.alloc([128, D])
            uT = A.alloc([128, KC, 512], BF16)
            umT = uT
            vg = A.alloc([128, D]); vn = vg; vnb = A.alloc([128, D], BF16)
            junk = A.alloc([128, D], BF16)
            stat = A.alloc([128, 8])
            mixs = [A.alloc([128, 128]) for _ in range(2)]
            for kc in range(KC):
                P.op(wload_eng, lambda e, kc=kc: e.dma_start(out=w_in[:, kc, :], in_=a_w_in[j, kc * 128:(kc + 1) * 128, :]), writes=["a_w_in"], dma=True)
            for kc in range(KC):
                P.op(wload_eng, lambda e, kc=kc: e.dma_start(out=w_out[:, kc, :], in_=a_w_out[j, kc * 128:(kc + 1) * 128, :]), writes=["a_w_out"], dma=True)
            load(wsT, a_wsT[j].rearrange("p (a b) -> p a b", a=8), "wsT", eng="gpsimd")
            load(wsTs, a_wsTs[j].rearrange("p (a b) -> p a b", a=8), "wsTs", eng="gpsimd")
            load(bsb, a_bs[j].partition_broadcast(128).rearrange("p (a b) -> p a b", a=8), "bsb")
            load(bsbs, a_bss[j].partition_broadcast(128).rearrange("p (a b) -> p a b", a=8), "bsbs")
            load(vgain, a_v_gain[j].partition_broadcast(128), "vgain")
            P.op("vector", lambda e: e.tensor_tensor(out=wsT, in0=wsT, in1=tri01.unsqueeze(1).to_broadcast([128, 8, 128]), op=ALU.mult), reads=["wsT", "tri01"], writes=["wsT"])
            P.op("vector", lambda e: e.tensor_tensor(out=wsTs, in0=wsTs, in1=blk01.unsqueeze(1).to_broadcast([64, 8, 64]), op=ALU.mult), reads=["wsTs", "blk01"], writes=["wsTs"])
            rmsnorm(gmix_t, l * 8)
            for (t0, n) in GROUPS:
                for gc in range(8):
                    ps, pk = PS()
                    for kc in range(KC):
                        P.op("tensor", lambda e, ps=ps, kc=kc, gc=gc, t0=t0, n=n: e.matmul(ps[:, 0:n], w_in[:, kc, gc * 128:(gc + 1) * 128], xnT[:, kc, t0:t0 + n], start=(kc == 0), stop=(kc == KC - 1)),
                             reads=["a_w_in"] + tkeys("xn", t0, n), writes=[pk])
                    P.op("scalar", lambda e, ps=ps, gc=gc, n=n: e.activation(out=uT[:, gc, 0:n], in_=ps[:, 0:n], func=AF.Gelu_apprx_tanh), reads=[pk], writes=[f"uT.{gc}"])
                ntile = (n + 127) // 128
                for ti in range(ntile):
                    tt = t0 // 128 + ti
                    tn = min(128, n)
                    c0 = ti * 128
                    P.op("vector", lambda e: e.memset(stat, 0.0), writes=["stat"])
                    for half in range(2):
                        ps, pk = PS()
                        for kc in range(KC):
                            P.op("tensor", lambda e, ps=ps, kc=kc, half=half, tt=tt, tn=tn: e.matmul(ps[0:tn, :], xnT[:, kc, tt * 128:tt * 128 + tn], w_in[:, kc, 1024 + half * 512:1024 + (half + 1) * 512], start=(kc == 0), stop=(kc == KC - 1)),
                                 reads=["a_w_in", f"xn.{tt}"], writes=[pk])
                        P.op("scalar", lambda e, ps=ps, half=half, tn=tn: e.activation(out=vg[0:tn, half * 512:(half + 1) * 512], in_=ps[0:tn, :], func=AF.Gelu_apprx_tanh, accum_out=stat[0:tn, half:half + 1]),
                             reads=[pk, "stat"], writes=[f"vg.{half}", "stat"])
                    P.op("scalar", lambda e, tn=tn: e.activation(out=junk[0:tn, :], in_=vg[0:tn, :], func=AF.Square, accum_out=stat[0:tn, 2:3]), reads=["vg.0", "vg.1", "stat"], writes=["junk", "stat"])
                    P.op("vector", lambda e, tn=tn: e.tensor_tensor(out=stat[0:tn, 3:4], in0=stat[0:tn, 0:1], in1=stat[0:tn, 1:2], op=ALU.add), reads=["stat"], writes=["stat"])
                    P.op("vector", lambda e, tn=tn: e.tensor_scalar(stat[0:tn, 3:4], stat[0:tn, 3:4], 1.0 / D, None, ALU.mult), reads=["stat"], writes=["stat"])
                    P.op("vector", lambda e, tn=tn: e.tensor_tensor(out=stat[0:tn, 4:5], in0=stat[0:tn, 3:4], in1=stat[0:tn, 3:4], op=ALU.mult), reads=["stat"], writes=["stat"])
                    P.op("vector", lambda e, tn=tn: e.scalar_tensor_tensor(out=stat[0:tn, 5:6], in0=stat[0:tn, 2:3], scalar=1.0 / D, in1=stat[0:tn, 4:5], op0=ALU.mult, op1=ALU.subtract), reads=["stat"], writes=["stat"])
                    P.op("scalar", lambda e, tn=tn: e.activation(out=stat[0:tn, 6:7], in_=stat[0:tn, 5:6], func=AF.Sqrt, bias=epsb[0:tn, 0:1]), reads=["stat", "epsb"], writes=["stat"])
                    P.op("vector", lambda e, tn=tn: e.reciprocal(stat[0:tn, 6:7], stat[0:tn, 6:7]), reads=["stat"], writes=["stat"])
                    P.op("vector", lambda e, tn=tn: e.tensor_scalar(vn[0:tn, :], vg[0:tn, :], stat[0:tn, 3:4], stat[0:tn, 6:7], ALU.subtract, ALU.mult), reads=["vg.0", "vg.1", "stat"], writes=["vg.0", "vg.1"])
                    P.op("gpsimd", lambda e, tn=tn: e.tensor_tensor(out=vn[0:tn, :], in0=vn[0:tn, :], in1=vgain[0:tn, :], op=ALU.mult), reads=["vg.0", "vg.1", "vgain"], writes=["vg.0", "vg.1"])
                    P.op("vector", lambda e, tn=tn: e.tensor_copy(vnb[0:tn, :], vn[0:tn, :]), reads=["vg.0", "vg.1"], writes=["vnb"])
                    if tt == 15:
                        P.op("sync", lambda e: e.dma_start(out=cv_p[j, :, :], in_=vn[:, :]), reads=["vg.0", "vg.1"], dma=True)
                    if tt == 16:
                        P.op("sync", lambda e: e.dma_start(out=cv_s[j, :, :], in_=vn[0:NS, :]), reads=["vg.0", "vg.1"], dma=True)
                    for gc in range(8):
                        ps, pk = PS()
                        if tt < 16:
                            P.op("tensor", lambda e, ps=ps, gc=gc: e.matmul(ps[:, 0:128], vnb[:, gc * 128:(gc + 1) * 128], wsT[:, gc, :], start=True, stop=True), reads=["vnb", "wsT"], writes=[pk])
                            bias = bsb[:, gc, :]
                        else:
                            P.op("tensor", lambda e, ps=ps, gc=gc: e.matmul(ps[:, 0:NS], vnb[0:NS, gc * 128:(gc + 1) * 128], wsTs[:, gc, :], start=True, stop=True), reads=["vnb", "wsTs"], writes=[pk])
                            bias = bsbs[:, gc, :]
                        mx = mixs[gc % 2]
                        P.op("vector", lambda e, ps=ps, mx=mx, bias=bias, tn=tn: e.tensor_tensor(out=mx[:, 0:tn], in0=ps[:, 0:tn], in1=bias, op=ALU.add), reads=[pk, "bsb", "bsbs"], writes=[f"mix{gc % 2}"])
                        P.op("gpsimd", lambda e, mx=mx, gc=gc, c0=c0, tn=tn: e.tensor_tensor(out=umT[:, gc, c0:c0 + tn], in0=uT[:, gc, c0:c0 + tn], in1=mx[:, 0:tn], op=ALU.mult), reads=[f"mix{gc % 2}", f"uT.{gc}"], writes=[f"uT.{gc}"])
                for dmc in range(KC):
                    ps, pk = PS()
                    for gc in range(8):
                        P.op("tensor", lambda e, ps=ps, gc=gc, dmc=dmc, n=n: e.matmul(ps[:, 0:n], w_out[:, gc, dmc * 128:(dmc + 1) * 128], umT[:, gc, 0:n], start=(gc == 0), stop=(gc == 7)),
                             reads=["a_w_out", f"uT.{gc}"], writes=[pk])
                    P.op("vector", lambda e, ps=ps, dmc=dmc, t0=t0, n=n: e.tensor_tensor(out=xT[:, dmc, t0:t0 + n], in0=xT[:, dmc, t0:t0 + n], in1=ps[:, 0:n], op=ALU.add),
                         reads=[pk] + tkeys("xT", t0, n), writes=tkeys("xT", t0, n))
            A.reset(m)
            P.barrier()

        def final_norm_out():
            m = A.mark()
            sq = [A.alloc([128, KC, 512], BF16) for _ in range(2)]
            rs = [A.alloc([128, 512]) for _ in range(2)]
            yT = [A.alloc([128, KC, 512]) for _ in range(2)]
            ytok = [A.alloc([128, D]) for _ in range(2)]
            cnt = 0
            for gi, (t0, n) in enumerate(GROUPS):
                s_ = sq[gi % 2]; r_ = rs[gi % 2]; y_ = yT[gi % 2]
                xk = tkeys("xT", t0, n)
                P.op("scalar", lambda e, s_=s_, t0=t0, n=n: e.activation(out=s_[:, :, 0:n], in_=xT[:, :, t0:t0 + n], func=AF.Square), reads=xk, writes=[f"sq{gi % 2}"])
                ps, pk = PS()
                for c in range(KC):
                    P.op("tensor", lambda e, ps=ps, s_=s_, c=c, n=n: e.matmul(ps[:, 0:n], onesb, s_[:, c, 0:n], start=(c == 0), stop=(c == KC - 1)), reads=[f"sq{gi % 2}", "onesb"], writes=[pk])
                P.op("scalar", lambda e, ps=ps, r_=r_, n=n: e.activation(out=r_[:, 0:n], in_=ps[:, 0:n], func=AF.Sqrt, scale=1.0 / D, bias=epsb[:, 0:1]), reads=[pk, "epsb"], writes=[f"rs{gi % 2}"])
                P.op("vector", lambda e, r_=r_, n=n: e.reciprocal(r_[:, 0:n], r_[:, 0:n]), reads=[f"rs{gi % 2}"], writes=[f"rs{gi % 2}"])
                for c in range(KC):
                    eng = "vector"
                    P.op(eng, lambda e, c=c, t0=t0, n=n, r_=r_, y_=y_: e.scalar_tensor_tensor(out=y_[:, c, 0:n], in0=xT[:, c, t0:t0 + n], scalar=gfin_t[:, c:c + 1], in1=r_[:, 0:n], op0=ALU.mult, op1=ALU.mult),
                         reads=xk + [f"rs{gi % 2}", "gains"], writes=[f"yT{gi % 2}"])
                for ti in range((n + 127) // 128):
                    tt = t0 // 128 + ti
                    tn = min(128, n)
                    yb = ytok[cnt % 2]
                    yk = f"ytok{cnt % 2}"
                    cnt += 1
                    for half in range(2):
                        ps, pk = PS()
                        for cc in range(4):
                            c = half * 4 + cc
                            P.op("tensor", lambda e, ps=ps, y_=y_, c=c, cc=cc, ti=ti, tn=tn: e.transpose(ps[0:tn, cc * 128:(cc + 1) * 128], y_[:, c, ti * 128:ti * 128 + tn], identf[:, :]),
                                 reads=[f"yT{gi % 2}", "identf"], writes=[pk])
                        copy_op(alt("vector", "scalar"), yb[0:tn, half * 512:(half + 1) * 512], ps[0:tn, :], [pk], [yk])
                    dst = y_p[tt * 128:(tt + 1) * 128, :] if tt < 16 else y_s[:, :]
                    P.op("sync", lambda e, dst=dst, yb=yb, tn=tn: e.dma_start(out=dst, in_=yb[0:tn, :]), reads=[yk], dma=True)
            A.reset(m)


        def layer_b(j, l):
            m = A.mark()
            S = Arena(arena_t, xn_end, base=xn_off)
            w_in = A.alloc([128, KC, BP], BF16)
            w_out = A.alloc([128, KC, D], BF16)
            proj = A.alloc([128, BP])
            Isc = A.alloc([128, SEQ + NS])
            Rr = A.alloc([128, 8, 512], BF16)
            junk = Rr.rearrange("p a b -> p (a b)")
            kT = S.alloc([128, 2, SEQ], BF16)
            vtok = S.alloc([128, 16, 256], BF16)
            kiT2 = S.alloc([128, SEQ], BF16)
            mask = S.alloc([128, SEQ + NS], BF16)
            maskT = S.alloc([128, 17, 128], BF16)
            xn_t = A.alloc([128, KC, 128], BF16)
            sq = A.alloc([128, KC, 128], BF16)
            rs = A.alloc([128, 128])
            rt = [A.alloc([128, 8, 16]) for _ in range(4)]
            qb = A.alloc([128, D], BF16)
            kb = A.alloc([128, 256], BF16)
            qib = A.alloc([128, 640], BF16)
            bis = A.alloc([128, 4 * (NITER + 2)])
            steps = bis[:, 0:NITER + 1]; mids = bis[:, NITER + 2:2 * NITER + 3]; cnts = bis[:, 2 * NITER + 4:3 * NITER + 4]
            incs = bis[:, 3 * NITER + 5:4 * NITER + 5]
            sc = A.alloc([128, 8])
            mB = A.mark()
            qT = A.alloc([128, 8, 128], BF16)
            qiT2 = A.alloc([128, 4, 128], BF16)
            Dw = A.alloc([128, 8, 128], BF16)
            Eb = [A.alloc([128, 512], BF16) for _ in range(2)]
            PTb = [A.alloc([128, 512], BF16) for _ in range(2)]
            rden = A.alloc([128, 512])
            oTn = A.alloc([128, 8, 128], BF16)
            for kc in range(KC):
                P.op(wload_eng, lambda e, kc=kc: e.dma_start(out=w_in[:, kc, :], in_=b_w_in[j, kc * 128:(kc + 1) * 128, :]), writes=["b_w_in"], dma=True)
            for kc in range(KC):
                P.op(wload_eng, lambda e, kc=kc: e.dma_start(out=w_out[:, kc, :], in_=b_w_out[j, kc * 128:(kc + 1) * 128, :]), writes=["b_w_out"], dma=True)
            P.op("vector", lambda e: e.memset(sc[:, 3:4], -1.0e29), writes=["thr0"])

            def norm_tile(t0, n):
                xk = tkeys("xT", t0, n)
                P.op("scalar", lambda e: e.activation(out=sq[:, :, 0:n], in_=xT[:, :, t0:t0 + n], func=AF.Square), reads=xk, writes=["sqb"])
                ps, pk = PS()
                for c in range(KC):
                    P.op("tensor", lambda e, ps=ps, c=c: e.matmul(ps[:, 0:n], onesb, sq[:, c, 0:n], start=(c == 0), stop=(c == KC - 1)), reads=["sqb", "onesb"], writes=[pk])
                P.op("scalar", lambda e, ps=ps: e.activation(out=rs[:, 0:n], in_=ps[:, 0:n], func=AF.Sqrt, scale=1.0 / D, bias=epsb[:, 0:1]), reads=[pk, "epsb"], writes=["rsb"])
                P.op("vector", lambda e: e.reciprocal(rs[:, 0:n], rs[:, 0:n]), reads=["rsb"], writes=["rsb"])
                for c in range(KC):
                    P.op("vector", lambda e, c=c: e.scalar_tensor_tensor(out=xn_t[:, c, 0:n], in0=xT[:, c, t0:t0 + n], scalar=gmix_t[:, l * 8 + c:l * 8 + c + 1], in1=rs[:, 0:n], op0=ALU.mult, op1=ALU.mult),
                         reads=xk + ["rsb", "gains"], writes=["xn_t"])

            def project(n):
                for c0 in range(0, BP, 512):
                    w = min(512, BP - c0)
                    ps, pk = PS()
                    for kc in range(KC):
                        P.op("tensor", lambda e, ps=ps, kc=kc, c0=c0, w=w: e.matmul(ps[0:n, 0:w], xn_t[:, kc, 0:n], w_in[:, kc, c0:c0 + w], start=(kc == 0), stop=(kc == KC - 1)),
                             reads=["xn_t", "b_w_in"], writes=[pk])
                    copy_op(alt("vector", "scalar"), proj[0:n, c0:c0 + w], ps[0:n, 0:w], [pk], ["proj"])

            def rope(n, tab):
                for (o, H, Dh, half, co, so) in ((0, 8, 128, 16, 0, 16), (1024, 2, 128, 16, 0, 16), (1536, 8, 64, 8, 32, 40), (2048, 1, 64, 8, 32, 40)):
                    sec = proj[0:n, o:o + H * Dh].rearrange("p (h d) -> p h d", h=H)
                    x1 = sec[:, :, 0:half]; x2 = sec[:, :, half:2 * half]
                    cb = tab[:, co:co + half].unsqueeze(1).to_broadcast([n, H, half])
                    sb_ = tab[:, so:so + half].unsqueeze(1).to_broadcast([n, H, half])
                    t = [r[0:n, 0:H, 0:half] for r in rt]
                    P.op("vector", lambda e, t=t, x1=x1, cb=cb: e.tensor_tensor(out=t[0], in0=x1, in1=cb, op=ALU.mult), reads=["proj", "rope"], writes=["rt0"])
                    P.op("gpsimd", lambda e, t=t, x2=x2, sb_=sb_: e.tensor_tensor(out=t[1], in0=x2, in1=sb_, op=ALU.mult), reads=["proj", "rope"], writes=["rt1"])
                    P.op("vector", lambda e, t=t, x2=x2, cb=cb: e.tensor_tensor(out=t[2], in0=x2, in1=cb, op=ALU.mult), reads=["proj", "rope"], writes=["rt2"])
                    P.op("gpsimd", lambda e, t=t, x1=x1, sb_=sb_: e.tensor_tensor(out=t[3], in0=x1, in1=sb_, op=ALU.mult), reads=["proj", "rope"], writes=["rt3"])
                    P.op("vector", lambda e, t=t, x1=x1: e.tensor_tensor(out=x1, in0=t[0], in1=t[1], op=ALU.subtract), reads=["rt0", "rt1", "rt2", "rt3"], writes=["proj"])
                    P.op("vector", lambda e, t=t, x2=x2: e.tensor_tensor(out=x2, in0=t[2], in1=t[3], op=ALU.add), reads=["rt0", "rt1", "rt2", "rt3"], writes=["proj"])

            def bisect(np_, L, lo_cols, rkeys=("R",)):
                rkeys = list(rkeys)
                P.op("vector", lambda e: e.tensor_reduce(out=sc[0:np_, 0:1], in_=Isc[0:np_, 0:L], axis=AX.X, op=ALU.max), reads=["I"], writes=["sc"])
                P.op("vector", lambda e: e.tensor_reduce(out=sc[0:np_, 1:2], in_=Isc[0:np_, 0:lo_cols], axis=AX.X, op=ALU.min), reads=["I"], writes=["sc"])
                P.op("vector", lambda e: e.tensor_tensor(out=sc[0:np_, 2:3], in0=sc[0:np_, 0:1], in1=sc[0:np_, 1:2], op=ALU.subtract), reads=["sc"], writes=["sc"])
                P.op("vector", lambda e: e.tensor_scalar(steps[0:np_, :], pow2[0:np_, :], sc[0:np_, 2:3], None, ALU.mult), reads=["sc", "pow2"], writes=["steps"])
                P.op("vector", lambda e: e.tensor_tensor(out=mids[0:np_, 0:1], in0=sc[0:np_, 1:2], in1=steps[0:np_, 0:1], op=ALU.add), reads=["sc", "steps"], writes=["mids"])
                P.op("vector", lambda e: e.memset(cnts[0:np_, :], 0.0), writes=["cnts"])
                for k in range(NITER):
                    P.op("vector", lambda e, k=k: e.tensor_scalar(junk[0:np_, 0:L], Isc[0:np_, 0:L], mids[0:np_, k:k + 1], None, ALU.is_ge, ALU.add, accum_out=cnts[0:np_, k:k + 1]),
                         reads=["I", "mids", "cnts"], writes=rkeys + ["cnts"])
                    P.op("vector", lambda e, k=k: e.tensor_scalar(incs[0:np_, k:k + 1], cnts[0:np_, k:k + 1], float(TOPK), steps[0:np_, k:k + 1], ALU.is_ge, ALU.mult), reads=["cnts", "steps"], writes=["incs"])
                    P.op("vector", lambda e, k=k: e.scalar_tensor_tensor(out=mids[0:np_, k + 1:k + 2], in0=mids[0:np_, k:k + 1], scalar=steps[0:np_, k + 1:k + 2], in1=incs[0:np_, k:k + 1], op0=ALU.subtract, op1=ALU.add),
                         reads=["mids", "steps", "incs"], writes=["mids"])

            ACC0, ACC1 = psf[4], psf[5]
            for qt in range(DEBUG.get('nqt', 16)):
                t0 = qt * 128
                L = (qt + 1) * 128
                norm_tile(t0, 128)
                project(128)
                rope(128, ropeP[:, qt, :])
                P.op("sync", lambda e, t0=t0: e.dma_start(out=nk_p[j, t0:t0 + 128, :], in_=proj[:, 1024:1280]), reads=["proj"], dma=True)
                P.op("sync", lambda e, t0=t0: e.dma_start(out=nv_p[j, t0:t0 + 128, :], in_=proj[:, 1280:1536]), reads=["proj"], dma=True)
                P.op("sync", lambda e, t0=t0: e.dma_start(out=nki_p[j, t0:t0 + 128, :], in_=proj[:, 2048:2112]), reads=["proj"], dma=True)
                if DEBUG.get('bstage', 99) <= 1:
                    continue
                P.op("scalar", lambda e: e.activation(out=qb, in_=proj[:, 0:1024], func=AF.Copy), reads=["proj"], writes=["qb"])
                P.op("gpsimd", lambda e: e.tensor_copy(kb, proj[:, 1024:1280]), reads=["proj"], writes=["kb"])
                P.op("gpsimd", lambda e, qt=qt: e.tensor_copy(vtok[:, qt, :], proj[:, 1280:1536]), reads=["proj"], writes=[f"vtok.{qt}"])
                P.op("vector", lambda e: e.tensor_copy(qib[:, 0:576], proj[:, 1536:2112]), reads=["proj"], writes=["qib"])
                P.op("vector", lambda e: e.tensor_copy(qib[:, 576:640], proj[:, 2048:2112]), reads=["proj"], writes=["qib"])
                if DEBUG.get('sub') == 'a':
                    continue
                for h in range(8):
                    P.op("gpsimd", lambda e, h=h: e.tensor_scalar(Dw[:, h, :], identb, proj[:, 2112 + h:2113 + h], None, ALU.mult), reads=["proj", "identb"], writes=["Dw"])
                if DEBUG.get('sub') == 'b':
                    continue
                pb, pbk = PSB()
                for h in range(8):
                    P.op("tensor", lambda e, pb=pb, h=h: e.transpose(pb[:, h * 128:(h + 1) * 128], qb[:, h * 128:(h + 1) * 128], identb), reads=["qb", "identb"], writes=[pbk])
                copy_op("vector", qT.rearrange("p a b -> p (a b)"), pb[:, :], [pbk], ["qT"])
                if DEBUG.get('sub') == 'c':
                    continue
                pb, pbk = PSB()
                for g in range(DEBUG.get("nk", 2)):
                    P.op("tensor", lambda e, pb=pb, g=g: e.transpose(pb[:, g * 128:(g + 1) * 128], kb[:, g * 128:(g + 1) * 128], identb), reads=["kb", "identb"], writes=[pbk])
                for pr in range(DEBUG.get("nqi", 5)):
                    P.op("tensor", lambda e, pb=pb, pr=pr: e.transpose(pb[:, (2 + pr) * 128:(3 + pr) * 128], qib[:, pr * 128:(pr + 1) * 128], identb), reads=["qib", "identb"], writes=[pbk])
                if DEBUG.get('sub') != 'd':
                    copy_op("vector", kT[:, :, t0:t0 + 128], pb[:, 0:256].rearrange("p (a b) -> p a b", a=2), [pbk], [f"kT.{qt}"])
                if DEBUG.get('sub') != 'e':
                    copy_op("vector", qiT2.rearrange("p a b -> p (a b)"), pb[:, 256:768], [pbk], ["qiT2"])
                if DEBUG.get('sub') != 'f':
                    copy_op("vector", kiT2[:, t0:t0 + 128], pb[:, 768:896], [pbk], [f"kiT2.{qt}"])
                if DEBUG.get('bstage', 99) <= 2:
                    continue
                for s0 in range(0, L, 512):
                    w = min(512, L - s0)
                    kk = [f"kiT2.{q}" for q in range(s0 // 128, (s0 + w) // 128)]
                    for h in range(8):
                        pr, hh = h // 2, h % 2
                        ps, pk = PS()
                        P.op("tensor", lambda e, ps=ps, pr=pr, hh=hh, s0=s0, w=w: e.matmul(ps[:, 0:w], qiT2[hh * 64:(hh + 1) * 64, pr, :], kiT2[hh * 64:(hh + 1) * 64, s0:s0 + w], start=True, stop=True),
                             reads=["qiT2"] + kk, writes=[pk])
                        if h % 2 == 0:
                            P.op("scalar", lambda e, ps=ps, h=h, w=w: e.activation(out=Rr[:, h, 0:w], in_=ps[:, 0:w], func=AF.Relu), reads=[pk], writes=["R"])
                        else:
                            P.op("vector", lambda e, ps=ps, h=h, w=w: e.tensor_scalar_max(Rr[:, h, 0:w], ps[:, 0:w], 0.0), reads=[pk], writes=["R"])
                    ps, pk = PS()
                    for h in range(8):
                        P.op("tensor", lambda e, ps=ps, h=h, w=w: e.matmul(ps[:, 0:w], Dw[:, h, :], Rr[:, h, 0:w], start=(h == 0), stop=(h == 7)), reads=["Dw", "R"], writes=[pk])
                    copy_op("scalar", Isc[:, s0:s0 + w], ps[:, 0:w], [pk], ["I"])
                P.op("vector", lambda e, t0=t0: e.tensor_tensor(out=Isc[:, t0:t0 + 128], in0=Isc[:, t0:t0 + 128], in1=tribias, op=ALU.add), reads=["I", "tribias"], writes=["I"])
                if DEBUG.get('bstage', 99) <= 3:
                    continue
                if qt >= 2:
                    bisect(128, L, qt * 128)
                    thr = mids[:, NITER:NITER + 1]
                else:
                    thr = sc[:, 3:4]
                P.op("vector", lambda e, L=L, thr=thr: e.tensor_scalar(mask[:, 0:L], Isc[:, 0:L], thr, None, ALU.is_ge), reads=["I", "mids", "thr0"], writes=["mask"])
                for k0 in range(0, qt + 1, 8):
                    nk = min(8, qt + 1 - k0)
                    pb, pbk = PSB()
                    for kt in range(k0, k0 + nk):
                        P.op("tensor", lambda e, pb=pb, kt=kt, k0=k0: e.transpose(pb[:, (kt - k0) * 128:(kt - k0 + 1) * 128], mask[:, kt * 128:(kt + 1) * 128], identb), reads=["mask", "identb"], writes=[pbk])
                    copy_op("vector", maskT[:, k0:k0 + nk, :].rearrange("p a b -> p (a b)"), pb[:, 0:nk * 128], [pbk], ["maskT"])
                if DEBUG.get('bstage', 99) <= 4:
                    continue
                for g in range(2):
                    for kt in range(qt + 1):
                        ps, pk = PS()
                        P.op("tensor", lambda e, ps=ps, g=g, kt=kt: e.matmul(ps[:, 0:512], kT[:, g, kt * 128:(kt + 1) * 128], qT[:, g * 4:(g + 1) * 4, :].rearrange("p a b -> p (a b)"), start=True, stop=True),
                             reads=["qT", f"kT.{kt}"], writes=[pk])
                        E = Eb[kt % 2]; PT = PTb[kt % 2]
                        P.op("scalar", lambda e, ps=ps, E=E: e.activation(out=E, in_=ps[:, 0:512], func=AF.Exp, scale=ATTN_SCALE), reads=[pk], writes=[f"E{kt % 2}"])
                        P.op(alt("gpsimd", "vector"), lambda e, E=E, PT=PT, kt=kt: e.tensor_tensor(out=PT.rearrange("p (a b) -> p a b", a=4), in0=E.rearrange("p (a b) -> p a b", a=4), in1=maskT[:, kt, :].unsqueeze(1).to_broadcast([128, 4, 128]), op=ALU.mult),
                             reads=[f"E{kt % 2}", "maskT"], writes=[f"PT{kt % 2}"])
                        P.op("tensor", lambda e, g=g, kt=kt, PT=PT, qt=qt: e.matmul(ACC0[:, 0:512], vtok[:, kt, g * 128:(g + 1) * 128], PT, start=(kt == 0), stop=(kt == qt)), reads=[f"PT{kt % 2}", f"vtok.{kt}"], writes=["acc0"])
                        P.op("tensor", lambda e, kt=kt, PT=PT, qt=qt: e.matmul(ACC1[:, 0:512], onesb, PT, start=(kt == 0), stop=(kt == qt)), reads=[f"PT{kt % 2}", "onesb"], writes=["acc1"])
                    P.op("vector", lambda e: e.reciprocal(rden, ACC1[:, 0:512]), reads=["acc1"], writes=["rden"])
                    P.op("vector", lambda e, g=g: e.tensor_tensor(out=oTn[:, g * 4:(g + 1) * 4, :].rearrange("p a b -> p (a b)"), in0=ACC0[:, 0:512], in1=rden, op=ALU.mult), reads=["acc0", "rden"], writes=["oTn"])
                if DEBUG.get('bstage', 99) <= 5:
                    continue
                for dmc in range(KC):
                    ps, pk = PS()
                    for h in range(8):
                        P.op("tensor", lambda e, ps=ps, h=h, dmc=dmc: e.matmul(ps[:, 0:128], w_out[:, h, dmc * 128:(dmc + 1) * 128], oTn[:, h, :], start=(h == 0), stop=(h == 7)), reads=["b_w_out", "oTn"], writes=[pk])
                    P.op("vector", lambda e, ps=ps, dmc=dmc, t0=t0: e.tensor_tensor(out=xT[:, dmc, t0:t0 + 128], in0=xT[:, dmc, t0:t0 + 128], in1=ps[:, 0:128], op=ALU.add), reads=[pk, f"xT.{qt}"], writes=[f"xT.{qt}"])
            if do_samp:
                P.barrier()
                A.reset(mB)
                S2 = Arena(arena_t, xn_end, base=xn_off)
                k_b = S2.alloc([128, 16, 256], BF16)
                v_b = S2.alloc([128, 16, 256], BF16)
                kT_b = S2.alloc([128, 2, 16, 128], BF16)
                kiT_b = S2.alloc([64, 16, 128], BF16)
                kg = [S2.alloc([128, 1024], BF16) for _ in range(2)]
                mask_s = A.alloc([64, SEQ + NS], BF16)
                maskT_all = A.alloc([128, 16, 64], BF16)
                maskT_new = A.alloc([64, 64], BF16)
                qT_s = A.alloc([128, 2, 64, 4], BF16)
                kT_new = A.alloc([128, 2, 64], BF16)
                qiT_s = A.alloc([64, 8, 64], BF16)
                kiT_new = A.alloc([64, 64], BF16)
                Dw_s = A.alloc([64, 8, 64], BF16)
                vb_new = A.alloc([64, 256], BF16)
                E_s = [A.alloc([128, 256], BF16) for _ in range(2)]
                PT_s = [A.alloc([128, 256], BF16) for _ in range(2)]
                rden_s = A.alloc([128, 256])
                oT_s = A.alloc([128, 8, 64], BF16)
                idr = A.alloc([128, 16], I32)
                idx = A.alloc([128, 16], I32)
                rr2 = {"k": 0}

                def PS2():
                    k = 4 + rr2["k"]
                    rr2["k"] = (rr2["k"] + 1) % 2
                    return psf[k], f"psf{k}"

                RK = [f"R.{h}" for h in range(8)]
                norm_tile(SEQ, NS)
                project(NS)
                rope(NS, ropeS[0:NS, :])
                P.op("sync", lambda e: e.dma_start(out=nk_s[j, :, :], in_=proj[0:NS, 1024:1280]), reads=["proj"], dma=True)
                P.op("sync", lambda e: e.dma_start(out=nv_s[j, :, :], in_=proj[0:NS, 1280:1536]), reads=["proj"], dma=True)
                P.op("sync", lambda e: e.dma_start(out=nki_s[j, :, :], in_=proj[0:NS, 2048:2112]), reads=["proj"], dma=True)
                P.op("scalar", lambda e: e.activation(out=qb[0:NS, :], in_=proj[0:NS, 0:1024], func=AF.Copy), reads=["proj"], writes=["qb"])
                P.op("gpsimd", lambda e: e.tensor_copy(kb[0:NS, :], proj[0:NS, 1024:1280]), reads=["proj"], writes=["kb"])
                P.op("gpsimd", lambda e: e.tensor_copy(vb_new[0:NS, :], proj[0:NS, 1280:1536]), reads=["proj"], writes=["vb_new"])
                P.op("vector", lambda e: e.tensor_copy(qib[0:NS, 0:576], proj[0:NS, 1536:2112]), reads=["proj"], writes=["qib"])
                for h in range(8):
                    P.op("gpsimd", lambda e, h=h: e.tensor_scalar(Dw_s[:, h, :], identb[0:NS, 0:NS], proj[0:NS, 2112 + h:2113 + h], None, ALU.mult), reads=["proj", "identb"], writes=["Dw_s"])
                pb, pbk = PSB()
                for h in range(8):
                    P.op("tensor", lambda e, pb=pb, h=h: e.transpose(pb[:, h * 64:(h + 1) * 64], qb[0:NS, h * 128:(h + 1) * 128], identb[0:NS, 0:NS]), reads=["qb", "identb"], writes=[pbk])
                for g in range(2):
                    P.op("vector", lambda e, pb=pb, g=g: e.tensor_copy(qT_s[:, g, :, :].rearrange("p t h -> p h t"), pb[:, g * 256:(g + 1) * 256].rearrange("p (h t) -> p h t", h=4)), reads=[pbk], writes=["qT_s"])
                pb, pbk = PSB()
                for g in range(2):
                    P.op("tensor", lambda e, pb=pb, g=g: e.transpose(pb[:, g * 64:(g + 1) * 64], kb[0:NS, g * 128:(g + 1) * 128], identb[0:NS, 0:NS]), reads=["kb", "identb"], writes=[pbk])
                for h in range(9):
                    P.op("tensor", lambda e, pb=pb, h=h: e.transpose(pb[0:64, 128 + h * 64:128 + (h + 1) * 64], qib[0:NS, h * 64:(h + 1) * 64], identb[0:NS, 0:NS]), reads=["qib", "identb"], writes=[pbk])
                P.op("vector", lambda e, pb=pb: e.tensor_copy(kT_new.rearrange("p a b -> p (a b)"), pb[:, 0:128]), reads=[pbk], writes=["kT_new"])
                P.op("vector", lambda e, pb=pb: e.tensor_copy(qiT_s.rearrange("p a b -> p (a b)"), pb[0:64, 128:640]), reads=[pbk], writes=["qiT_s"])
                P.op("vector", lambda e, pb=pb: e.tensor_copy(kiT_new, pb[0:64, 640:704]), reads=[pbk], writes=["kiT_new"])
                for jj in range(16):
                    src = bass.AP(pt.tensor, jj, [[0, 8], [16, 16]])
                    P.op("sync", lambda e, src=src, jj=jj: e.dma_start(out=idr[jj * 8:(jj + 1) * 8, :], in_=src, allow_slow_non_contiguous=True), writes=["idr"], dma=True)
                P.op("vector", lambda e: e.tensor_scalar(idx, idr, 8, q8[:, 0:1], ALU.mult, ALU.add), reads=["idr", "q8"], writes=["idx"])
                if USE_AG:
                    r0 = 0
                    cki_src = pool_full["cki", 0, j]; ck_src = [pool_full["ck", hf, j] for hf in range(2)]; cv_src = [pool_full["cv", hf, j] for hf in range(2)]
                else:
                    r0 = j * NPOOL * 8
                    cki_src = cki; ck_src = [ck, ck]; cv_src = [cv, cv]
                IACC = [(psf[i], f"psf{i}") for i in range(4)]
                for b in range(NB):
                    g_ = kg[b % 2]; gk = f"kg{b % 2}"
                    P.op("gpsimd", lambda e, g_=g_, b=b: e.indirect_dma_start(out=g_, out_offset=None, in_=cki_src, in_offset=bass.IndirectOffsetOnAxis(ap=idx[:, b:b + 1], axis=0), element_offset=r0 * 1024), reads=["idx", f"G:cki0_full{j}"], writes=[gk], dma=True)
                    for r8 in range(2):
                        pb, pbk = PSB()
                        for rr_ in range(8):
                            r = r8 * 8 + rr_
                            P.op("tensor", lambda e, pb=pb, g_=g_, r=r, rr_=rr_: e.transpose(pb[0:64, rr_ * 128:(rr_ + 1) * 128], g_[:, r * 64:(r + 1) * 64], identb), reads=[gk, "identb"], writes=[pbk])
                        P.op("vector", lambda e, pb=pb, r8=r8: e.tensor_copy(kiT_b[:, r8 * 8:(r8 + 1) * 8, :].rearrange("p a b -> p (a b)"), pb[0:64, :]), reads=[pbk], writes=[f"kiT_b.{r8}"])
                    for blk in range(4):
                        for h in range(8):
                            ps, pk = PS2()
                            P.op("tensor", lambda e, ps=ps, h=h, blk=blk: e.matmul(ps[0:64, 0:512], qiT_s[:, h, :], kiT_b[:, blk * 4:(blk + 1) * 4, :].rearrange("p a b -> p (a b)"), start=True, stop=True),
                                 reads=["qiT_s", f"kiT_b.{blk // 2}"], writes=[pk])
                            if h % 2 == 0:
                                P.op("scalar", lambda e, ps=ps, h=h, b=b: e.activation(out=Rr[0:64, h, :], in_=ps[0:64, 0:512], func=AF.Relu, scale=rowmask[:, b:b + 1]), reads=[pk, "rowmask"], writes=[RK[h]])
                            else:
                                P.op("vector", lambda e, ps=ps, h=h, b=b: e.tensor_scalar(Rr[0:64, h, :], ps[0:64, 0:512], rowmask[:, b:b + 1], 0.0, ALU.mult, ALU.max), reads=[pk, "rowmask"], writes=[RK[h]])
                        ia, iak = IACC[blk]
                        for h in range(8):
                            P.op("tensor", lambda e, ia=ia, h=h, b=b: e.matmul(ia[0:64, 0:512], Dw_s[:, h, :], Rr[0:64, h, :], start=(b == 0 and h == 0), stop=(b == NB - 1 and h == 7)), reads=["Dw_s", RK[h]], writes=[iak])
                ps, pk = PS2()
                for h in range(8):
                    P.op("tensor", lambda e, ps=ps, h=h: e.matmul(ps[0:64, h * 64:(h + 1) * 64], qiT_s[:, h, :], kiT_new, start=True, stop=True), reads=["qiT_s", "kiT_new"], writes=[pk])
                P.op("scalar", lambda e, ps=ps: e.activation(out=Rr[0:64, :, 0:64], in_=ps[0:64, 0:512].rearrange("p (h s) -> p h s", h=8), func=AF.Relu), reads=[pk], writes=RK)
                ps2, pk2 = PS2()
                for h in range(8):
                    P.op("tensor", lambda e, ps2=ps2, h=h: e.matmul(ps2[0:64, 0:64], Dw_s[:, h, :], Rr[0:64, h, 0:64], start=(h == 0), stop=(h == 7)), reads=["Dw_s"] + RK, writes=[pk2])
                P.op("vector", lambda e, ps2=ps2: e.tensor_tensor(out=Isc[0:64, SEQ:SEQ + NS], in0=ps2[0:64, 0:64], in1=blkbias, op=ALU.add), reads=[pk2, "blkbias"], writes=["I"])
                for blk in range(4):
                    ia, iak = IACC[blk]
                    copy_op("scalar" if blk % 2 == 0 else "vector", Isc[0:64, blk * 512:(blk + 1) * 512], ia[0:64, 0:512], [iak], ["I"])
                bisect(64, SEQ + NS, SEQ, rkeys=RK)
                P.op("vector", lambda e: e.tensor_scalar(mask_s[:, :], Isc[0:64, :], mids[0:64, NITER:NITER + 1], None, ALU.is_ge), reads=["I", "mids"], writes=["mask_s"])
                for r8 in range(2):
                    pb, pbk = PSB()
                    for rr_ in range(8):
                        r = r8 * 8 + rr_
                        P.op("tensor", lambda e, pb=pb, r=r, rr_=rr_: e.transpose(pb[:, rr_ * 64:(rr_ + 1) * 64], mask_s[:, r * 128:(r + 1) * 128], identb[0:64, 0:64]), reads=["mask_s", "identb"], writes=[pbk])
                    P.op("vector", lambda e, pb=pb, r8=r8: e.tensor_copy(maskT_all[:, r8 * 8:(r8 + 1) * 8, :].rearrange("p a b -> p (a b)"), pb[:, 0:512]), reads=[pbk], writes=["maskT_all"])
                pb, pbk = PSB()
                P.op("tensor", lambda e, pb=pb: e.transpose(pb[0:64, 0:64], mask_s[:, SEQ:SEQ + NS], identb[0:64, 0:64]), reads=["mask_s", "identb"], writes=[pbk])
                P.op("vector", lambda e, pb=pb: e.tensor_copy(maskT_new, pb[0:64, 0:64]), reads=[pbk], writes=["maskT_new"])
                OACC = [(psf[0], "psf0"), (psf[2], "psf2")]
                DACC = [(psf[1], "psf1"), (psf[3], "psf3")]
                ecnt = 0
                for g in range(2):
                    ps, pk = PS2()
                    P.op("tensor", lambda e, ps=ps, g=g: e.matmul(ps[0:64, 0:256], kT_new[:, g, :], qT_s[:, g, :, :].rearrange("p t h -> p (t h)"), start=True, stop=True), reads=["kT_new", "qT_s"], writes=[pk])
                    E = E_s[ecnt % 2]; PT = PT_s[ecnt % 2]; ek = f"E_s{ecnt % 2}"; ptk = f"PT_s{ecnt % 2}"
                    ecnt += 1
                    P.op("scalar", lambda e, ps=ps, E=E: e.activation(out=E[0:64, :], in_=ps[0:64, 0:256], func=AF.Exp, scale=ATTN_SCALE), reads=[pk], writes=[ek])
                    P.op("vector", lambda e, E=E, PT=PT: e.tensor_tensor(out=PT[0:64, :].rearrange("p (t h) -> p t h", h=4), in0=E[0:64, :].rearrange("p (t h) -> p t h", h=4), in1=maskT_new.unsqueeze(2).to_broadcast([64, 64, 4]), op=ALU.mult),
                         reads=[ek, "maskT_new"], writes=[ptk])
                    oa, oak = OACC[g]; da, dak = DACC[g]
                    P.op("tensor", lambda e, oa=oa, g=g, PT=PT: e.matmul(oa[:, 0:256], vb_new[:, g * 128:(g + 1) * 128], PT[0:64, :], start=True, stop=False), reads=["vb_new", ptk], writes=[oak])
                    P.op("tensor", lambda e, da=da, PT=PT: e.matmul(da[:, 0:256], onesb[0:64, :], PT[0:64, :], start=True, stop=False), reads=["onesb", ptk], writes=[dak])
                for b in range(NB):
                    for hf in range(2):
                        P.op("gpsimd", lambda e, b=b, hf=hf: e.indirect_dma_start(out=k_b[:, hf * 8:(hf + 1) * 8, :].rearrange("p a b -> p (a b)"), out_offset=None, in_=ck_src[hf], in_offset=bass.IndirectOffsetOnAxis(ap=idx[:, b:b + 1], axis=0), element_offset=(0 if USE_AG else r0 * 4096 + hf * 2048)),
                             reads=["idx", f"G:ck{hf}_full{j}"], writes=[f"k_b.{hf}"], dma=True)
                    for hf in range(2):
                        P.op("gpsimd", lambda e, b=b, hf=hf: e.indirect_dma_start(out=v_b[:, hf * 8:(hf + 1) * 8, :].rearrange("p a b -> p (a b)"), out_offset=None, in_=cv_src[hf], in_offset=bass.IndirectOffsetOnAxis(ap=idx[:, b:b + 1], axis=0), element_offset=(0 if USE_AG else r0 * 4096 + hf * 2048)),
                             reads=["idx", f"G:cv{hf}_full{j}"], writes=[f"v_b.{hf}"], dma=True)
                    for g in range(2):
                        for r8 in range(2):
                            pb, pbk = PSB()
                            for rr_ in range(8):
                                r = r8 * 8 + rr_
                                P.op("tensor", lambda e, pb=pb, g=g, r=r, rr_=rr_: e.transpose(pb[:, rr_ * 128:(rr_ + 1) * 128], k_b[:, r, g * 128:(g + 1) * 128], identb), reads=[f"k_b.{r8}", "identb"], writes=[pbk])
                            P.op("vector", lambda e, pb=pb, g=g, r8=r8: e.tensor_copy(kT_b[:, g, r8 * 8:(r8 + 1) * 8, :].rearrange("p a b -> p (a b)"), pb[:, :]), reads=[pbk], writes=[f"kT_b.{g}.{r8}"])
                    for g in range(2):
                        ps, pk = PS2()
                        for r in range(16):
                            P.op("tensor", lambda e, ps=ps, g=g, r=r, b=b: e.matmul(ps[:, r * 16:(r + 1) * 16], kT_b[:, g, r, :], qT_s[:, g, b * 4:(b + 1) * 4, :].rearrange("p t h -> p (t h)"), start=True, stop=True),
                                 reads=[f"kT_b.{g}.{r // 8}", "qT_s"], writes=[pk])
                        E = E_s[ecnt % 2]; PT = PT_s[ecnt % 2]; ek = f"E_s{ecnt % 2}"; ptk = f"PT_s{ecnt % 2}"
                        ecnt += 1
                        P.op("scalar", lambda e, ps=ps, E=E: e.activation(out=E, in_=ps[:, 0:256], func=AF.Exp, scale=ATTN_SCALE), reads=[pk], writes=[ek])
                        P.op("vector", lambda e, E=E, PT=PT, b=b: e.tensor_tensor(out=PT.rearrange("p (r t h) -> p r t h", r=16, t=4), in0=E.rearrange("p (r t h) -> p r t h", r=16, t=4), in1=maskT_all[:, :, b * 4:(b + 1) * 4].unsqueeze(3).to_broadcast([128, 16, 4, 4]), op=ALU.mult),
                             reads=[ek, "maskT_all"], writes=[ptk])
                        oa, oak = OACC[g]; da, dak = DACC[g]
                        for r in range(16):
                            last = (b == NB - 1 and r == 15)
                            P.op("tensor", lambda e, oa=oa, g=g, r=r, b=b, PT=PT, last=last: e.matmul(oa[:, b * 16:(b + 1) * 16], v_b[:, r, g * 128:(g + 1) * 128], PT[:, r * 16:(r + 1) * 16], start=False, stop=last), reads=[f"v_b.{r // 8}", ptk], writes=[oak])
                            P.op("tensor", lambda e, da=da, r=r, b=b, PT=PT, last=last: e.matmul(da[:, b * 16:(b + 1) * 16], onesb, PT[:, r * 16:(r + 1) * 16], start=False, stop=last), reads=["onesb", ptk], writes=[dak])
                for g in range(2):
                    oa, oak = OACC[g]; da, dak = DACC[g]
                    P.op("vector", lambda e, da=da: e.reciprocal(rden_s, da[:, 0:256]), reads=[dak], writes=["rden_s"])
                    P.op("vector", lambda e, oa=oa, g=g: e.tensor_tensor(out=oT_s[:, g * 4:(g + 1) * 4, :], in0=oa[:, 0:256].rearrange("p (t h) -> p h t", h=4), in1=rden_s.rearrange("p (t h) -> p h t", h=4), op=ALU.mult), reads=[oak, "rden_s"], writes=["oT_s"])
                for dmc in range(KC):
                    ps, pk = PS2()
                    for h in range(8):
                        P.op("tensor", lambda e, ps=ps, h=h, dmc=dmc: e.matmul(ps[:, 0:NS], w_out[:, h, dmc * 128:(dmc + 1) * 128], oT_s[:, h, :], start=(h == 0), stop=(h == 7)), reads=["b_w_out", "oT_s"], writes=[pk])
                    P.op("vector", lambda e, ps=ps, dmc=dmc: e.tensor_tensor(out=xT[:, dmc, SEQ:SEQ + NS], in0=xT[:, dmc, SEQ:SEQ + NS], in1=ps[:, 0:NS], op=ALU.add), reads=[pk, "xT.16"], writes=["xT.16"])
            A.reset(m)
            P.barrier()

        for l in range(n_layers):
            j = l // 2
            if l % 2 == 0:
                if not DEBUG.get("skip_a"):
                    layer_a(j, l)
                if l == 0:
                    pool_allgather()
            else:
                if do_b:
                    layer_b(j, l)
            if not DEBUG.get("skip_ffn"):
                ffn(l)
        final_norm_out()
        print("arena hi words", A.hi, "ops", len(P.ops))
        P.finalize(st)
        with nc.Block() as block:
            P.emit(block)
    return nc


def host_consts():
    c = {}
    c["c_ident"] = np.eye(128, dtype=np.float32)
    s = np.arange(128)[:, None]; t = np.arange(128)[None, :]
    c["c_tri01"] = (s <= t).astype(np.float32)
    c["c_tribias"] = np.where(t <= s, 0.0, NEG).astype(np.float32)
    b = np.arange(64) // 4; o = np.arange(64) % 4
    same = b[:, None] == b[None, :]
    c["c_blk01"] = (same & (o[:, None] <= o[None, :])).astype(np.float32)
    c["c_blkbias"] = np.where(same & (o[None, :] <= o[:, None]), 0.0, NEG).astype(np.float32)
    def tab(pos):
        inv16 = 500000.0 ** (-np.arange(16, dtype=np.float32) * 2.0 / 32)
        inv8 = 500000.0 ** (-np.arange(8, dtype=np.float32) * 2.0 / 16)
        a16 = pos.astype(np.float32)[:, None] * inv16[None, :]
        a8 = pos.astype(np.float32)[:, None] * inv8[None, :]
        return np.concatenate([np.cos(a16), np.sin(a16), np.cos(a8), np.sin(a8)], axis=1).astype(np.float32)
    tp = tab(np.arange(SEQ))
    c["c_ropeP"] = np.ascontiguousarray(tp.reshape(16, 128, 48).transpose(1, 0, 2).reshape(128, 16 * 48))
    c["c_ropeS"] = tab(2048 + (np.arange(64) % 4))
    c["c_q8"] = (np.arange(128) % 8).astype(np.int32).reshape(128, 1)
    c["c_rowmask"] = (np.arange(64)[:, None] // 4 == np.arange(16)[None, :]).astype(np.float32)
    p2 = (0.5 ** (np.arange(NITER + 1) + 1)).astype(np.float32)
    p2[NITER] = p2[NITER - 1]
    c["c_pow2"] = np.tile(p2[None, :], (128, 1))
    return c


_NC_CACHE = {}


def kernel(x_prompt, x_sample, cache_k, cache_v, cache_kidx, page_table, norm_mix, norm_ffn,
           a_w_in, a_v_gain, a_w_s, a_b_s, a_w_out, b_w_in, b_w_out, ffn_w1, ffn_w2, norm_final,
           _n_layers=4, _do_b=True, _do_samp=True):
    f = lambda a: np.ascontiguousarray(np.asarray(a, dtype=np.float32))
    x_prompt = f(x_prompt); x_sample = f(x_sample)
    consts = host_consts()
    shared = dict(consts)
    gl = lambda g: np.ascontiguousarray(f(g).reshape(4, 8, 128).transpose(2, 0, 1).reshape(128, 32))
    shared["gmix"] = gl(norm_mix); shared["gffn"] = gl(norm_ffn)
    shared["gfin"] = np.ascontiguousarray(f(norm_final).reshape(8, 128).T)
    shared["a_w_in"] = f(a_w_in); shared["a_v_gain"] = f(a_v_gain); shared["a_w_out"] = f(a_w_out)
    ws = f(a_w_s)
    shared["a_wsT"] = np.ascontiguousarray(ws.transpose(0, 3, 1, 2).reshape(2, 128, 8 * 128))
    w4 = ws[:, :, :4, :4]
    blk = np.zeros((2, 64, 8, 64), np.float32)
    for b in range(16):
        blk[:, b * 4:(b + 1) * 4, :, b * 4:(b + 1) * 4] = w4.transpose(0, 3, 1, 2)
    shared["a_wsTs"] = blk.reshape(2, 64, 8 * 64)
    bs = f(a_b_s)
    shared["a_bs"] = np.ascontiguousarray(bs.reshape(2, 8 * 128))
    shared["a_bss"] = np.ascontiguousarray(np.tile(bs[:, :, :4], (1, 1, 16)).reshape(2, 8 * 64))
    shared["b_w_in"] = f(b_w_in); shared["b_w_out"] = f(b_w_out)
    shared["ffn_w1"] = f(ffn_w1); shared["ffn_w2"] = f(ffn_w2)
    percore = [dict() for _ in range(8)]
    if _do_b and _do_samp:
        ckv = f(cache_k).reshape(2, NPOOL * 8, 16 * 256); cvv = f(cache_v).reshape(2, NPOOL * 8, 16 * 256); ckiv = f(cache_kidx).reshape(2, NPOOL * 8, 16 * 64)
        if USE_AG:
            SHR = NPOOL * 8 // 8
            for c in range(8):
                percore[c]["ck"] = np.ascontiguousarray(ckv[:, c * SHR:(c + 1) * SHR]).reshape(2 * SHR, 16 * 256)
                percore[c]["cv"] = np.ascontiguousarray(cvv[:, c * SHR:(c + 1) * SHR]).reshape(2 * SHR, 16 * 256)
                percore[c]["cki"] = np.ascontiguousarray(ckiv[:, c * SHR:(c + 1) * SHR]).reshape(2 * SHR, 16 * 64)
        else:
            shared["ck"] = ckv.reshape(2 * NPOOL * 8, 16 * 256)
            shared["cv"] = cvv.reshape(2 * NPOOL * 8, 16 * 256)
            shared["cki"] = ckiv.reshape(2 * NPOOL * 8, 16 * 64)
    ptab = np.asarray(page_table).astype(np.int32)
    in_maps = []
    for c in range(8):
        m = dict(shared)
        m.update(percore[c])
        m["xp"] = x_prompt[c]
        m["xs"] = np.ascontiguousarray(x_sample[c * NB:(c + 1) * NB].reshape(NS, D))
        m["pt"] = np.ascontiguousarray(ptab[c * NB:(c + 1) * NB])
        in_maps.append(m)
    key = (_n_layers, _do_b, _do_samp)
    if key not in _NC_CACHE:
        _NC_CACHE[key] = build_nc(_n_layers, _do_b, _do_samp)
    nc = _NC_CACHE[key]
    res = run_bass_kernel_spmd(nc, in_maps, core_ids=list(range(8)))
    R = res.results
    cat = lambda name: np.stack([R[c][name] for c in range(8)])
    y_prompt = cat("y_p")
    y_sample = cat("y_s").reshape(128, 4, D)
    nk_p = cat("nk_p").transpose(1, 0, 2, 3).reshape(2, 8, SEQ, 2, 128)
    nv_p = cat("nv_p").transpose(1, 0, 2, 3).reshape(2, 8, SEQ, 2, 128)
    nki_p = cat("nki_p").transpose(1, 0, 2, 3).reshape(2, 8, SEQ, 64)
    nk_s = cat("nk_s").transpose(1, 0, 2, 3).reshape(2, 128, 4, 2, 128)
    nv_s = cat("nv_s").transpose(1, 0, 2, 3).reshape(2, 128, 4, 2, 128)
    nki_s = cat("nki_s").transpose(1, 0, 2, 3).reshape(2, 128, 4, 64)
    cvp = cat("cv_p").transpose(1, 0, 2, 3).reshape(2, 8, 128, D)
    cvs = cat("cv_s").transpose(1, 0, 2, 3).reshape(2, 128, 4, D)
    return (y_prompt, y_sample, np.ascontiguousarray(nk_p), np.ascontiguousarray(nv_p), np.ascontiguousarray(nki_p),
            np.ascontiguousarray(nk_s), np.ascontiguousarray(nv_s), np.ascontiguousarray(nki_s),
            np.ascontiguousarray(cvp), np.ascontiguousarray(cvs))
```

```python
import math
import numpy as np
from contextlib import ExitStack
import concourse.bass as bass
import concourse.mybir as mybir
from concourse.bass_utils import run_bass_kernel_spmd

F32 = mybir.dt.float32
BF16 = mybir.dt.bfloat16
I32 = mybir.dt.int32
AF = mybir.ActivationFunctionType
ALU = mybir.AluOpType
AX = mybir.AxisListType

D = 1024
KC = 8
SEQ = 2048
NS = 64
NT = SEQ + NS
NB = 16
BP = 2120
EPS = 1e-6
ATTN_SCALE = float(128 ** -0.5)
NEG = -1.0e30
NITER = 20
TOPK = 256
NPOOL = 2560
USE_AG = False

ENGS = ("tensor", "vector", "scalar", "gpsimd", "sync")
SEM_LIMIT = 24000
NDMA_SEM = 10


class Prog:
    def __init__(self, nc):
        self.nc = nc
        self.ops = []

    def op(self, eng, fn, reads=(), writes=(), dma=False, semkey=None, nobar=False, inc=16):
        self.ops.append(dict(eng=eng, fn=fn, reads=tuple(reads), writes=tuple(writes), dma=dma, semkey=semkey, nobar=nobar, inc=inc))

    def barrier(self):
        self.ops.append(dict(eng=None, barrier=True))

    def finalize(self, stack):
        nc = self.nc
        ops = self.ops
        n = len(ops)
        last_writer = {}
        readers = {}
        deps = [None] * n
        needed = [False] * n
        last_on_eng = {}
        outstanding_dma = []
        bar_deps = []
        fresh = {}
        for i, o in enumerate(ops):
            if o.get("barrier"):
                bar_deps = list(last_on_eng.values()) + list(outstanding_dma)
                outstanding_dma = []
                last_writer = {k: v for k, v in last_writer.items() if k.startswith("G:")}
                readers = {k: v for k, v in readers.items() if k.startswith("G:")}
                fresh = {e: True for e in ENGS}
                continue
            d = set()
            for k in o["reads"]:
                if k in last_writer:
                    d.add(last_writer[k])
            for k in o["writes"]:
                if k in last_writer:
                    d.add(last_writer[k])
                for r in readers.get(k, ()):
                    d.add(r)
            if bar_deps and fresh.get(o["eng"], False):
                d.update(bar_deps)
                fresh[o["eng"]] = False
            d.discard(i)
            if o["eng"] == "tensor" and not o["dma"]:
                d = {j for j in d if not (ops[j]["eng"] == "tensor" and not ops[j]["dma"])}
            deps[i] = d
            for j in d:
                needed[j] = True
            for k in o["reads"]:
                readers.setdefault(k, []).append(i)
            for k in o["writes"]:
                last_writer[k] = i
                readers[k] = []
            if o["dma"]:
                if not o.get("nobar"):
                    outstanding_dma.append(i)
                needed[i] = True
            else:
                last_on_eng[o["eng"]] = i

        def newsem(name):
            return stack.enter_context(nc.semaphore(name))
        eng_sems = {e: [newsem(f"s_{e}_0")] for e in ENGS}
        eng_cnt = {e: 0 for e in ENGS}
        dma_sems = {e: [newsem(f"d_{e}_{k}") for k in range(NDMA_SEM)] for e in ("sync", "gpsimd", "scalar")}
        dma_cnt = {e: [0] * NDMA_SEM for e in dma_sems}
        dma_rr = {e: 0 for e in dma_sems}
        dma_last = {e: [None] * NDMA_SEM for e in dma_sems}
        event = [None] * n
        extra_dep = [None] * n
        ded_sems = {}
        for i, o in enumerate(ops):
            if o.get("barrier"):
                continue
            e = o["eng"]
            if o["dma"] and o.get("semkey"):
                sk = o["semkey"]
                if sk not in ded_sems:
                    ded_sems[sk] = [newsem("g_" + sk.replace(".", "_")), 0]
                ded_sems[sk][1] += o["inc"]
                event[i] = (ded_sems[sk][0], ded_sems[sk][1])
                o["sem"] = ded_sems[sk][0]
            elif o["dma"]:
                k = dma_rr[e]
                dma_rr[e] = (k + 1) % NDMA_SEM
                if dma_last[e][k] is not None:
                    extra_dep[i] = dma_last[e][k]
                dma_cnt[e][k] += 16
                event[i] = (dma_sems[e][k], dma_cnt[e][k])
                dma_last[e][k] = event[i]
                o["sem"] = dma_sems[e][k]
            elif needed[i]:
                if eng_cnt[e] >= SEM_LIMIT:
                    eng_sems[e].append(newsem(f"s_{e}_{len(eng_sems[e])}"))
                    eng_cnt[e] = 0
                eng_cnt[e] += 1
                event[i] = (eng_sems[e][-1], eng_cnt[e])
                o["sem"] = eng_sems[e][-1]
        seen = {e: {} for e in ENGS}
        for i, o in enumerate(ops):
            if o.get("barrier"):
                continue
            e = o["eng"]
            evs = [event[j] for j in deps[i]]
            if extra_dep[i] is not None:
                evs.append(extra_dep[i])
            best = {}
            for (s, v) in evs:
                key = id(s)
                if v > seen[e].get(key, 0) and v > best.get(key, (None, 0))[1]:
                    best[key] = (s, v)
            o["waits"] = list(best.values())
            for key, (s, v) in best.items():
                seen[e][key] = v
        self.final_events = [(v[0], v[1]) for v in ded_sems.values()]
        for e in dma_sems:
            for k in range(NDMA_SEM):
                if dma_last[e][k] is not None:
                    self.final_events.append(dma_last[e][k])

    def emit(self, block):
        ops = self.ops
        final_events = self.final_events

        def run(engname, eng):
            for o in ops:
                if o.get("barrier") or o["eng"] != engname:
                    continue
                for (s, v) in o["waits"]:
                    eng.wait_ge(s, v)
                ins = o["fn"](eng)
                if o["dma"] and o["inc"] == 1:
                    ins.then_inc(o["sem"])
                elif o["dma"]:
                    ins.then_inc(o["sem"], 16)
                elif "sem" in o:
                    ins.then_inc(o["sem"], 1)
            if engname == "sync":
                for (s, v) in final_events:
                    eng.wait_ge(s, v)

        @block.tensor
        def _(t):
            run("tensor", t)

        @block.vector
        def _(v):
            run("vector", v)

        @block.scalar
        def _(a):
            run("scalar", a)

        @block.gpsimd
        def _(g):
            run("gpsimd", g)

        @block.sync
        def _(s):
            run("sync", s)


class Arena:
    def __init__(self, t, nwords, base=0):
        self.t = t
        self.n = nwords
        self.off = base
        self.hi = 0

    def alloc(self, shape, dt=F32):
        free = 1
        for s in shape[1:]:
            free *= s
        words = free if dt != BF16 else (free + 1) // 2
        v = self.t[0:shape[0], self.off:self.off + words]
        if dt == BF16:
            v = v.bitcast(BF16)
            if free != 2 * words:
                v = v[:, 0:free]
        elif dt == I32:
            v = v.bitcast(I32)
        self.off += words
        self.hi = max(self.hi, self.off)
        assert self.off <= self.n, f"arena overflow {self.off} > {self.n}"
        if len(shape) == 3:
            v = v.rearrange("p (a b) -> p a b", a=shape[1])
        elif len(shape) == 4:
            v = v.rearrange("p (a b c) -> p a b c", a=shape[1], b=shape[2])
        return v

    def mark(self):
        return self.off

    def reset(self, m):
        self.off = m


DEBUG = {}
GROUPS = [(0, 512), (512, 512), (1024, 512), (1536, 512), (2048, 64)]


def tkeys(prefix, t0, n):
    return [f"{prefix}.{tt}" for tt in range(t0 // 128, (t0 + n + 127) // 128)]


def build_nc(n_layers=4, do_b=True, do_samp=True):
    nc = bass.Bass("TRN2", target_bir_lowering=False)
    st = ExitStack()
    I = {}
    O = {}

    def inp(name, shape, dt=F32):
        I[name] = nc.dram_tensor(name, list(shape), dt, kind="ExternalInput").ap()
        return I[name]

    def outp(name, shape, dt=F32):
        O[name] = nc.dram_tensor(name, list(shape), dt, kind="ExternalOutput").ap()
        return O[name]

    xp = inp("xp", [SEQ, D]); xs = inp("xs", [NS, D])
    SHR = NPOOL * 8 // 8
    if do_b and do_samp:
        if USE_AG:
            ck_sh = inp("ck", [2 * SHR, 16 * 256]); cv_sh = inp("cv", [2 * SHR, 16 * 256]); cki_sh = inp("cki", [2 * SHR, 16 * 64])
            pool_in = {}; pool_full = {}
            POOLS = [("cki", 0, 1024, 0)] + [(nm, hf, 2048, hf * 2048) for nm in ("ck", "cv") for hf in range(2)]
            for nm, hf, w, c0 in POOLS:
                for jj in range(2):
                    pool_in[nm, hf, jj] = nc.dram_tensor(f"{nm}{hf}_in{jj}", [SHR, w], F32, kind="Internal").ap()
                    pool_full[nm, hf, jj] = nc.dram_tensor(f"{nm}{hf}_full{jj}", [NPOOL * 8, w], F32, kind="Internal").ap()
        else:
            ck = inp("ck", [2 * NPOOL * 8, 16 * 256]); cv = inp("cv", [2 * NPOOL * 8, 16 * 256]); cki = inp("cki", [2 * NPOOL * 8, 16 * 64])
    pt = inp("pt", [NB, 16], I32)
    gmix = inp("gmix", [128, 32]); gffn = inp("gffn", [128, 32]); gfin = inp("gfin", [128, 8])
    a_w_in = inp("a_w_in", [2, D, 2048]); a_v_gain = inp("a_v_gain", [2, D])
    a_wsT = inp("a_wsT", [2, 128, 8 * 128]); a_wsTs = inp("a_wsTs", [2, 64, 8 * 64])
    a_bs = inp("a_bs", [2, 8 * 128]); a_bss = inp("a_bss", [2, 8 * 64])
    a_w_out = inp("a_w_out", [2, D, D])
    b_w_in = inp("b_w_in", [2, D, BP]); b_w_out = inp("b_w_out", [2, D, D])
    ffn_w1 = inp("ffn_w1", [4, D, 4096]); ffn_w2 = inp("ffn_w2", [4, 4096, D])
    c_ident = inp("c_ident", [128, 128]); c_tri01 = inp("c_tri01", [128, 128]); c_tribias = inp("c_tribias", [128, 128])
    c_blk01 = inp("c_blk01", [64, 64]); c_blkbias = inp("c_blkbias", [64, 64])
    c_ropeP = inp("c_ropeP", [128, 16 * 48]); c_ropeS = inp("c_ropeS", [64, 48])
    c_q8 = inp("c_q8", [128, 1], I32); c_pow2 = inp("c_pow2", [128, NITER + 1])
    c_rowmask = inp("c_rowmask", [64, 16])

    y_p = outp("y_p", [SEQ, D]); y_s = outp("y_s", [NS, D])
    nk_p = outp("nk_p", [2, SEQ, 256]); nv_p = outp("nv_p", [2, SEQ, 256]); nki_p = outp("nki_p", [2, SEQ, 64])
    nk_s = outp("nk_s", [2, NS, 256]); nv_s = outp("nv_s", [2, NS, 256]); nki_s = outp("nki_s", [2, NS, 64])
    cv_p = outp("cv_p", [2, 128, D]); cv_s = outp("cv_s", [2, NS, D])

    with st:
        ARW = 53100
        arena_t = st.enter_context(nc.sbuf_tensor("arena", [128, ARW], F32))
        A = Arena(arena_t, ARW)
        psf = [st.enter_context(nc.psum_tensor(f"psf{i}", [128, 512], F32)) for i in range(6)]
        psb = [st.enter_context(nc.psum_tensor(f"psb{i}", [128, 1024], BF16)) for i in range(2)]
        P = Prog(nc)
        rr = {"f": 0, "b": 0, "eng": 0}

        def PS(pool=None):
            k = rr["f"]
            rr["f"] = (k + 1) % 4
            return psf[k], f"psf{k}"

        def PSB():
            k = rr["b"]
            rr["b"] = (k + 1) % DEBUG.get("npsb", 2)
            return psb[k], f"psb{k}"

        def alt(*engs):
            rr["eng"] += 1
            return engs[rr["eng"] % len(engs)]

        def copy_op(eng, out, in_, reads, writes):
            if eng == "scalar":
                P.op("scalar", lambda e: e.activation(out=out, in_=in_, func=AF.Copy), reads, writes)
            else:
                P.op(eng, lambda e: e.tensor_copy(out, in_), reads, writes)

        xT = A.alloc([128, KC, NT])
        xn_off = A.mark()
        xnT = A.alloc([128, KC, NT], BF16)
        xn_end = A.mark()
        identf = A.alloc([128, 128]); identb = A.alloc([128, 128], BF16); onesb = A.alloc([128, 128], BF16)
        tri01 = A.alloc([128, 128], BF16); tribias = A.alloc([128, 128])
        blk01 = A.alloc([64, 64], BF16); blkbias = A.alloc([64, 64])
        ropeP = A.alloc([128, 16, 48]); ropeS = A.alloc([64, 48])
        gmix_t = A.alloc([128, 32]); gffn_t = A.alloc([128, 32]); gfin_t = A.alloc([128, 8])
        q8 = A.alloc([128, 1], I32); pow2 = A.alloc([128, NITER + 1])
        rowmask = A.alloc([64, 16])
        stage = A.alloc([128, 128])

        def load(dst, src, key, eng="sync", **kw):
            P.op(eng, lambda e: e.dma_start(out=dst, in_=src, **kw), writes=[key], dma=True)

        load(identf, c_ident[:, :], "identf"); load(tribias, c_tribias[:, :], "tribias")
        load(blkbias, c_blkbias[:, :], "blkbias")
        load(ropeP, c_ropeP.rearrange("p (a b) -> p a b", a=16), "rope"); load(ropeS, c_ropeS[:, :], "rope")
        load(gmix_t, gmix[:, :], "gains"); load(gffn_t, gffn[:, :], "gains"); load(gfin_t, gfin[:, :], "gains")
        load(q8, c_q8[:, :], "q8"); load(pow2, c_pow2[:, :], "pow2")
        load(rowmask, c_rowmask[:, :], "rowmask")
        load(identb, c_ident[:, :], "identb", eng="gpsimd")
        load(tri01, c_tri01[:, :], "tri01", eng="gpsimd")
        load(blk01, c_blk01[:, :], "blk01", eng="gpsimd")
        P.op("vector", lambda e: e.memset(onesb, 1.0), writes=["onesb"])

        NCH = 4
        def pool_copies():
            if not (do_b and do_samp and USE_AG):
                return
            srcs = {"cki": cki_sh, "ck": ck_sh, "cv": cv_sh}
            for jj in range(2):
                for nm, hf, w, c0 in POOLS:
                    rows = SHR // NCH
                    for ch in range(NCH):
                        dst = pool_in[nm, hf, jj][ch * rows:(ch + 1) * rows, :]
                        src = srcs[nm][jj * SHR + ch * rows:jj * SHR + (ch + 1) * rows, c0:c0 + w]
                        P.op("sync", lambda e, dst=dst, src=src: e.dma_start(out=dst, in_=src), writes=[f"G:{nm}{hf}_in{jj}.{ch}"], dma=True, semkey=f"cp.{nm}{hf}.{jj}", nobar=True)

        def pool_allgather():
            if not (do_b and do_samp and USE_AG):
                return
            for jj in range(2):
                for nm, hf, w, c0 in POOLS:
                    pin = pool_in[nm, hf, jj]; pfull = pool_full[nm, hf, jj]
                    P.op("gpsimd", lambda e, pin=pin, pfull=pfull: e.collective_compute("AllGather", ALU.bypass, replica_groups=[list(range(8))], ins=[pin.opt()], outs=[pfull.opt()]),
                         reads=[f"G:{nm}{hf}_in{jj}.{ch}" for ch in range(NCH)], writes=[f"G:{nm}{hf}_full{jj}"], dma=True, semkey=f"ag.{nm}{hf}.{jj}", nobar=True, inc=1)

        pool_copies()
        m0 = A.mark()
        xtok = [A.alloc([128, D]) for _ in range(2)]
        for tt in range(17):
            n = 128 if tt < 16 else NS
            xb = xtok[tt % 2]
            src = xp[tt * 128:(tt + 1) * 128, :] if tt < 16 else xs[:, :]
            P.op("sync", lambda e, xb=xb, src=src, n=n: e.dma_start(out=xb[0:n, :], in_=src), writes=[f"xtok{tt % 2}"], dma=True)
            for half in range(2):
                ps, pk = PS()
                for cc in range(4):
                    c = half * 4 + cc
                    P.op("tensor", lambda e, ps=ps, xb=xb, c=c, cc=cc, n=n: e.transpose(ps[:, cc * 128:cc * 128 + n], xb[0:n, c * 128:(c + 1) * 128], identf[0:n, 0:n]),
                         reads=[f"xtok{tt % 2}", "identf"], writes=[pk])
                dst = xT[:, half * 4:half * 4 + 4, tt * 128:tt * 128 + n]
                srcp = ps[:, :].rearrange("p (a b) -> p a b", a=4)[:, :, 0:n]
                copy_op(alt("vector", "scalar"), dst, srcp, [pk], [f"xT.{tt}"])
        A.reset(m0)
        P.barrier()

        def rmsnorm(gain, gcol0, dst_key="xn"):
            m = A.mark()
            sq = [A.alloc([128, KC, 512], BF16) for _ in range(1)] * 2
            rs = [A.alloc([128, 512]) for _ in range(1)] * 2
            for gi, (t0, n) in enumerate(GROUPS):
                gi = 0
                s_ = sq[gi % 2]; r_ = rs[gi % 2]
                xk = tkeys("xT", t0, n)
                P.op("scalar", lambda e, s_=s_, t0=t0, n=n: e.activation(out=s_[:, :, 0:n], in_=xT[:, :, t0:t0 + n], func=AF.Square),
                     reads=xk, writes=[f"sq{gi % 2}"])
                ps, pk = PS()
                for c in range(KC):
                    P.op("tensor", lambda e, ps=ps, s_=s_, c=c, n=n: e.matmul(ps[:, 0:n], onesb, s_[:, c, 0:n], start=(c == 0), stop=(c == KC - 1)),
                         reads=[f"sq{gi % 2}", "onesb"], writes=[pk])
                P.op("scalar", lambda e, ps=ps, r_=r_, n=n: e.activation(out=r_[:, 0:n], in_=ps[:, 0:n], func=AF.Sqrt, scale=1.0 / D, bias=epsb[:, 0:1]),
                     reads=[pk, "epsb"], writes=[f"rs{gi % 2}"])
                P.op("vector", lambda e, r_=r_, n=n: e.reciprocal(r_[:, 0:n], r_[:, 0:n]), reads=[f"rs{gi % 2}"], writes=[f"rs{gi % 2}"])
                for c in range(KC):
                    eng = "vector"
                    P.op(eng, lambda e, c=c, t0=t0, n=n, r_=r_: e.scalar_tensor_tensor(out=xnT[:, c, t0:t0 + n], in0=xT[:, c, t0:t0 + n], scalar=gain[:, gcol0 + c:gcol0 + c + 1], in1=r_[:, 0:n], op0=ALU.mult, op1=ALU.mult),
                         reads=xk + [f"rs{gi % 2}", "gains"], writes=tkeys(dst_key, t0, n))
            A.reset(m)

        epsb = A.alloc([128, 1])
        P.op("vector", lambda e: e.memset(epsb, EPS), writes=["epsb"])

        wload_eng = "gpsimd"

        def ffn(l):
            m = A.mark()
            w1g = [A.alloc([128, KC, 1024], BF16) for _ in range(2)]
            w2g = [A.alloc([128, KC, 1024], BF16) for _ in range(2)]
            hT = [A.alloc([128, KC, 512], BF16) for _ in range(2)]
            rl = [A.alloc([128, 512], BF16) for _ in range(2)]
            cnt = 0

            def wload(fg):
                sl = fg % 2
                for kc in range(KC):
                    P.op(wload_eng, lambda e, sl=sl, kc=kc, fg=fg: e.dma_start(out=w1g[sl][:, kc, :], in_=ffn_w1[l, kc * 128:(kc + 1) * 128, fg * 1024:(fg + 1) * 1024]),
                         writes=[f"w1g{sl}"], dma=True)
                for fc in range(KC):
                    P.op(wload_eng, lambda e, sl=sl, fc=fc, fg=fg: e.dma_start(out=w2g[sl][:, fc, :], in_=ffn_w2[l, fg * 1024 + fc * 128:fg * 1024 + (fc + 1) * 128, :]),
                         writes=[f"w2g{sl}"], dma=True)

            wload(0)
            wload(1)
            rmsnorm(gffn_t, l * 8)
            for fg in range(4):
                sl = fg % 2
                if fg >= 1 and fg + 1 < 4:
                    wload(fg + 1)
                for (t0, n) in GROUPS:
                    hs = cnt % 2
                    cnt += 1
                    for fc in range(KC):
                        ps, pk = PS()
                        for kc in range(KC):
                            P.op("tensor", lambda e, ps=ps, sl=sl, kc=kc, fc=fc, t0=t0, n=n: e.matmul(ps[:, 0:n], w1g[sl][:, kc, fc * 128:(fc + 1) * 128], xnT[:, kc, t0:t0 + n], start=(kc == 0), stop=(kc == KC - 1)),
                                 reads=[f"w1g{sl}"] + tkeys("xn", t0, n), writes=[pk])
                        rb = rl[fc % 2]
                        P.op("scalar", lambda e, ps=ps, rb=rb, n=n: e.activation(out=rb[:, 0:n], in_=ps[:, 0:n], func=AF.Relu), reads=[pk], writes=[f"rl{fc % 2}"])
                        P.op("gpsimd", lambda e, rb=rb, hs=hs, fc=fc, n=n: e.tensor_tensor(out=hT[hs][:, fc, 0:n], in0=rb[:, 0:n], in1=rb[:, 0:n], op=ALU.mult),
                             reads=[f"rl{fc % 2}"], writes=[f"hT{hs}.{fc}"])
                    for dmc in range(KC):
                        ps, pk = PS()
                        for fc in range(KC):
                            P.op("tensor", lambda e, ps=ps, sl=sl, fc=fc, dmc=dmc, hs=hs, n=n: e.matmul(ps[:, 0:n], w2g[sl][:, fc, dmc * 128:(dmc + 1) * 128], hT[hs][:, fc, 0:n], start=(fc == 0), stop=(fc == KC - 1)),
                                 reads=[f"w2g{sl}", f"hT{hs}.{fc}"], writes=[pk])
                        P.op("vector", lambda e, ps=ps, dmc=dmc, t0=t0, n=n: e.tensor_tensor(out=xT[:, dmc, t0:t0 + n], in0=xT[:, dmc, t0:t0 + n], in1=ps[:, 0:n], op=ALU.add),
                             reads=[pk] + tkeys("xT", t0, n), writes=tkeys("xT", t0, n))
            A.reset(m)
            P.barrier()

        def layer_a(j, l):
            m = A.mark()
            w_in = A.alloc([128, KC, 2048], BF16)
            w_out = A.alloc([128, KC, D], BF16)
            wsT = A.alloc([128, 8, 128], BF16); wsTs = A.alloc([64, 8, 64], BF16)
            bsb = A.alloc([128, 8, 128]); bsbs = A.alloc([128, 8, 64])
            vgain = A.alloc([128, D])
            uT = A.alloc([128, KC, 512], BF16)
            umT = uT
            vg = A.alloc([128, D]); vn = vg; vnb = A.alloc([128, D], BF16)
            junk = A.alloc([128, D], BF16)
            stat = A.alloc([128, 8])
            mixs = [A.alloc([128, 128]) for _ in range(2)]
            for kc in range(KC):
                P.op(wload_eng, lambda e, kc=kc: e.dma_start(out=w_in[:, kc, :], in_=a_w_in[j, kc * 128:(kc + 1) * 128, :]), writes=["a_w_in"], dma=True)
            for kc in range(KC):
                P.op(wload_eng, lambda e, kc=kc: e.dma_start(out=w_out[:, kc, :], in_=a_w_out[j, kc * 128:(kc + 1) * 128, :]), writes=["a_w_out"], dma=True)
            load(wsT, a_wsT[j].rearrange("p (a b) -> p a b", a=8), "wsT", eng="gpsimd")
            load(wsTs, a_wsTs[j].rearrange("p (a b) -> p a b", a=8), "wsTs", eng="gpsimd")
            load(bsb, a_bs[j].partition_broadcast(128).rearrange("p (a b) -> p a b", a=8), "bsb")
            load(bsbs, a_bss[j].partition_broadcast(128).rearrange("p (a b) -> p a b", a=8), "bsbs")
            load(vgain, a_v_gain[j].partition_broadcast(128), "vgain")
            P.op("vector", lambda e: e.tensor_tensor(out=wsT, in0=wsT, in1=tri01.unsqueeze(1).to_broadcast([128, 8, 128]), op=ALU.mult), reads=["wsT", "tri01"], writes=["wsT"])
            P.op("vector", lambda e: e.tensor_tensor(out=wsTs, in0=wsTs, in1=blk01.unsqueeze(1).to_broadcast([64, 8, 64]), op=ALU.mult), reads=["wsTs", "blk01"], writes=["wsTs"])
            rmsnorm(gmix_t, l * 8)
            for (t0, n) in GROUPS:
                for gc in range(8):
                    ps, pk = PS()
                    for kc in range(KC):
                        P.op("tensor", lambda e, ps=ps, kc=kc, gc=gc, t0=t0, n=n: e.matmul(ps[:, 0:n], w_in[:, kc, gc * 128:(gc + 1) * 128], xnT[:, kc, t0:t0 + n], start=(kc == 0), stop=(kc == KC - 1)),
                             reads=["a_w_in"] + tkeys("xn", t0, n), writes=[pk])
                    P.op("scalar", lambda e, ps=ps, gc=gc, n=n: e.activation(out=uT[:, gc, 0:n], in_=ps[:, 0:n], func=AF.Gelu_apprx_tanh), reads=[pk], writes=[f"uT.{gc}"])
                ntile = (n + 127) // 128
                for ti in range(ntile):
                    tt = t0 // 128 + ti
                    tn = min(128, n)
                    c0 = ti * 128
                    P.op("vector", lambda e: e.memset(stat, 0.0), writes=["stat"])
                    for half in range(2):
                        ps, pk = PS()
                        for kc in range(KC):
                            P.op("tensor", lambda e, ps=ps, kc=kc, half=half, tt=tt, tn=tn: e.matmul(ps[0:tn, :], xnT[:, kc, tt * 128:tt * 128 + tn], w_in[:, kc, 1024 + half * 512:1024 + (half + 1) * 512], start=(kc == 0), stop=(kc == KC - 1)),
                                 reads=["a_w_in", f"xn.{tt}"], writes=[pk])
                        P.op("scalar", lambda e, ps=ps, half=half, tn=tn: e.activation(out=vg[0:tn, half * 512:(half + 1) * 512], in_=ps[0:tn, :], func=AF.Gelu_apprx_tanh, accum_out=stat[0:tn, half:half + 1]),
                             reads=[pk, "stat"], writes=[f"vg.{half}", "stat"])
                    P.op("scalar", lambda e, tn=tn: e.activation(out=junk[0:tn, :], in_=vg[0:tn, :], func=AF.Square, accum_out=stat[0:tn, 2:3]), reads=["vg.0", "vg.1", "stat"], writes=["junk", "stat"])
                    P.op("vector", lambda e, tn=tn: e.tensor_tensor(out=stat[0:tn, 3:4], in0=stat[0:tn, 0:1], in1=stat[0:tn, 1:2], op=ALU.add), reads=["stat"], writes=["stat"])
                    P.op("vector", lambda e, tn=tn: e.tensor_scalar(stat[0:tn, 3:4], stat[0:tn, 3:4], 1.0 / D, None, ALU.mult), reads=["stat"], writes=["stat"])
                    P.op("vector", lambda e, tn=tn: e.tensor_tensor(out=stat[0:tn, 4:5], in0=stat[0:tn, 3:4], in1=stat[0:tn, 3:4], op=ALU.mult), reads=["stat"], writes=["stat"])
                    P.op("vector", lambda e, tn=tn: e.scalar_tensor_tensor(out=stat[0:tn, 5:6], in0=stat[0:tn, 2:3], scalar=1.0 / D, in1=stat[0:tn, 4:5], op0=ALU.mult, op1=ALU.subtract), reads=["stat"], writes=["stat"])
                    P.op("scalar", lambda e, tn=tn: e.activation(out=stat[0:tn, 6:7], in_=stat[0:tn, 5:6], func=AF.Sqrt, bias=epsb[0:tn, 0:1]), reads=["stat", "epsb"], writes=["stat"])
                    P.op("vector", lambda e, tn=tn: e.reciprocal(stat[0:tn, 6:7], stat[0:tn, 6:7]), reads=["stat"], writes=["stat"])
                    P.op("vector", lambda e, tn=tn: e.tensor_scalar(vn[0:tn, :], vg[0:tn, :], stat[0:tn, 3:4], stat[0:tn, 6:7], ALU.subtract, ALU.mult), reads=["vg.0", "vg.1", "stat"], writes=["vg.0", "vg.1"])
                    P.op("gpsimd", lambda e, tn=tn: e.tensor_tensor(out=vn[0:tn, :], in0=vn[0:tn, :], in1=vgain[0:tn, :], op=ALU.mult), reads=["vg.0", "vg.1", "vgain"], writes=["vg.0", "vg.1"])
                    P.op("vector", lambda e, tn=tn: e.tensor_copy(vnb[0:tn, :], vn[0:tn, :]), reads=["vg.0", "vg.1"], writes=["vnb"])
                    if tt == 15:
                        P.op("sync", lambda e: e.dma_start(out=cv_p[j, :, :], in_=vn[:, :]), reads=["vg.0", "vg.1"], dma=True)
                    if tt == 16:
                        P.op("sync", lambda e: e.dma_start(out=cv_s[j, :, :], in_=vn[0:NS, :]), reads=["vg.0", "vg.1"], dma=True)
                    for gc in range(8):
                        ps, pk = PS()
                        if tt < 16:
                            P.op("tensor", lambda e, ps=ps, gc=gc: e.matmul(ps[:, 0:128], vnb[:, gc * 128:(gc + 1) * 128], wsT[:, gc, :], start=True, stop=True), reads=["vnb", "wsT"], writes=[pk])
                            bias = bsb[:, gc, :]
                        else:
                            P.op("tensor", lambda e, ps=ps, gc=gc: e.matmul(ps[:, 0:NS], vnb[0:NS, gc * 128:(gc + 1) * 128], wsTs[:, gc, :], start=True, stop=True), reads=["vnb", "wsTs"], writes=[pk])
                            bias = bsbs[:, gc, :]
                        mx = mixs[gc % 2]
                        P.op("vector", lambda e, ps=ps, mx=mx, bias=bias, tn=tn: e.tensor_tensor(out=mx[:, 0:tn], in0=ps[:, 0:tn], in1=bias, op=ALU.add), reads=[pk, "bsb", "bsbs"], writes=[f"mix{gc % 2}"])
                        P.op("gpsimd", lambda e, mx=mx, gc=gc, c0=c0, tn=tn: e.tensor_tensor(out=umT[:, gc, c0:c0 + tn], in0=uT[:, gc, c0:c0 + tn], in1=mx[:, 0:tn], op=ALU.mult), reads=[f"mix{gc % 2}", f"uT.{gc}"], writes=[f"uT.{gc}"])
                for dmc in range(KC):
                    ps, pk = PS()
                    for gc in range(8):
                        P.op("tensor", lambda e, ps=ps, gc=gc, dmc=dmc, n=n: e.matmul(ps[:, 0:n], w_out[:, gc, dmc * 128:(dmc + 1) * 128], umT[:, gc, 0:n], start=(gc == 0), stop=(gc == 7)),
                             reads=["a_w_out", f"uT.{gc}"], writes=[pk])
                    P.op("vector", lambda e, ps=ps, dmc=dmc, t0=t0, n=n: e.tensor_tensor(out=xT[:, dmc, t0:t0 + n], in0=xT[:, dmc, t0:t0 + n], in1=ps[:, 0:n], op=ALU.add),
                         reads=[pk] + tkeys("xT", t0, n), writes=tkeys("xT", t0, n))
            A.reset(m)
            P.barrier()

        def final_norm_out():
            m = A.mark()
            sq = [A.alloc([128, KC, 512], BF16) for _ in range(2)]
            rs = [A.alloc([128, 512]) for _ in range(2)]
            yT = [A.alloc([128, KC, 512]) for _ in range(2)]
            ytok = [A.alloc([128, D]) for _ in range(2)]
            cnt = 0
            for gi, (t0, n) in enumerate(GROUPS):
                s_ = sq[gi % 2]; r_ = rs[gi % 2]; y_ = yT[gi % 2]
                xk = tkeys("xT", t0, n)
                P.op("scalar", lambda e, s_=s_, t0=t0, n=n: e.activation(out=s_[:, :, 0:n], in_=xT[:, :, t0:t0 + n], func=AF.Square), reads=xk, writes=[f"sq{gi % 2}"])
                ps, pk = PS()
                for c in range(KC):
                    P.op("tensor", lambda e, ps=ps, s_=s_, c=c, n=n: e.matmul(ps[:, 0:n], onesb, s_[:, c, 0:n], start=(c == 0), stop=(c == KC - 1)), reads=[f"sq{gi % 2}", "onesb"], writes=[pk])
                P.op("scalar", lambda e, ps=ps, r_=r_, n=n: e.activation(out=r_[:, 0:n], in_=ps[:, 0:n], func=AF.Sqrt, scale=1.0 / D, bias=epsb[:, 0:1]), reads=[pk, "epsb"], writes=[f"rs{gi % 2}"])
                P.op("vector", lambda e, r_=r_, n=n: e.reciprocal(r_[:, 0:n], r_[:, 0:n]), reads=[f"rs{gi % 2}"], writes=[f"rs{gi % 2}"])
                for c in range(KC):
                    eng = "vector"
                    P.op(eng, lambda e, c=c, t0=t0, n=n, r_=r_, y_=y_: e.scalar_tensor_tensor(out=y_[:, c, 0:n], in0=xT[:, c, t0:t0 + n], scalar=gfin_t[:, c:c + 1], in1=r_[:, 0:n], op0=ALU.mult, op1=ALU.mult),
                         reads=xk + [f"rs{gi % 2}", "gains"], writes=[f"yT{gi % 2}"])
                for ti in range((n + 127) // 128):
                    tt = t0 // 128 + ti
                    tn = min(128, n)
                    yb = ytok[cnt % 2]
                    yk = f"ytok{cnt % 2}"
                    cnt += 1
                    for half in range(2):
                        ps, pk = PS()
                        for cc in range(4):
                            c = half * 4 + cc
                            P.op("tensor", lambda e, ps=ps, y_=y_, c=c, cc=cc, ti=ti, tn=tn: e.transpose(ps[0:tn, cc * 128:(cc + 1) * 128], y_[:, c, ti * 128:ti * 128 + tn], identf[:, :]),
                                 reads=[f"yT{gi % 2}", "identf"], writes=[pk])
                        copy_op(alt("vector", "scalar"), yb[0:tn, half * 512:(half + 1) * 512], ps[0:tn, :], [pk], [yk])
                    dst = y_p[tt * 128:(tt + 1) * 128, :] if tt < 16 else y_s[:, :]
                    P.op("sync", lambda e, dst=dst, yb=yb, tn=tn: e.dma_start(out=dst, in_=yb[0:tn, :]), reads=[yk], dma=True)
            A.reset(m)


        def layer_b(j, l):
            m = A.mark()
            S = Arena(arena_t, xn_end, base=xn_off)
            w_in = A.alloc([128, KC, BP], BF16)
            w_out = A.alloc([128, KC, D], BF16)
            proj = A.alloc([128, BP])
            Isc = A.alloc([128, SEQ + NS])
            Rr = A.alloc([128, 8, 512], BF16)
            junk = Rr.rearrange("p a b -> p (a b)")
            kT = S.alloc([128, 2, SEQ], BF16)
            vtok = S.alloc([128, 16, 256], BF16)
            kiT2 = S.alloc([128, SEQ], BF16)
            mask = S.alloc([128, SEQ + NS], BF16)
            maskT = S.alloc([128, 17, 128], BF16)
            xn_t = A.alloc([128, KC, 128], BF16)
            sq = A.alloc([128, KC, 128], BF16)
            rs = A.alloc([128, 128])
            qb = A.alloc([128, D], BF16)
            kb = A.alloc([128, 256], BF16)
            qib = A.alloc([128, 640], BF16)
            bis = A.alloc([128, 4 * (NITER + 2)])
            steps = bis[:, 0:NITER + 1]; mids = bis[:, NITER + 2:2 * NITER + 3]; cnts = bis[:, 2 * NITER + 4:3 * NITER + 4]
            incs = bis[:, 3 * NITER + 5:4 * NITER + 5]
            sc = A.alloc([128, 8])
            mB = A.mark()
            qT = A.alloc([128, 8, 128], BF16)
            qiT2 = A.alloc([128, 4, 128], BF16)
            Dw = A.alloc([128, 8, 128], BF16)
            Eb = [A.alloc([128, 512], BF16) for _ in range(2)]
            PTb = [A.alloc([128, 512], BF16) for _ in range(2)]
            rden = A.alloc([128, 512])
            oTn = A.alloc([128, 8, 128], BF16)
            for kc in range(KC):
                P.op(wload_eng, lambda e, kc=kc: e.dma_start(out=w_in[:, kc, :], in_=b_w_in[j, kc * 128:(kc + 1) * 128, :]), writes=["b_w_in"], dma=True)
            for kc in range(KC):
                P.op(wload_eng, lambda e, kc=kc: e.dma_start(out=w_out[:, kc, :], in_=b_w_out[j, kc * 128:(kc + 1) * 128, :]), writes=["b_w_out"], dma=True)
            P.op("vector", lambda e: e.memset(sc[:, 3:4], -1.0e29), writes=["thr0"])

            def norm_tile(t0, n):
                xk = tkeys("xT", t0, n)
                P.op("scalar", lambda e: e.activation(out=sq[:, :, 0:n], in_=xT[:, :, t0:t0 + n], func=AF.Square), reads=xk, writes=["sqb"])
                ps, pk = PS()
                for c in range(KC):
                    P.op("tensor", lambda e, ps=ps, c=c: e.matmul(ps[:, 0:n], onesb, sq[:, c, 0:n], start=(c == 0), stop=(c == KC - 1)), reads=["sqb", "onesb"], writes=[pk])
                P.op("scalar", lambda e, ps=ps: e.activation(out=rs[:, 0:n], in_=ps[:, 0:n], func=AF.Sqrt, scale=1.0 / D, bias=epsb[:, 0:1]), reads=[pk, "epsb"], writes=["rsb"])
                P.op("vector", lambda e: e.reciprocal(rs[:, 0:n], rs[:, 0:n]), reads=["rsb"], writes=["rsb"])
                for c in range(KC):
                    P.op("vector", lambda e, c=c: e.scalar_tensor_tensor(out=xn_t[:, c, 0:n], in0=xT[:, c, t0:t0 + n], scalar=gmix_t[:, l * 8 + c:l * 8 + c + 1], in1=rs[:, 0:n], op0=ALU.mult, op1=ALU.mult),
                         reads=xk + ["rsb", "gains"], writes=["xn_t"])

            def project(n):
                for c0 in range(0, BP, 512):
                    w = min(512, BP - c0)
                    ps, pk = PS()
                    for kc in range(KC):
                        P.op("tensor", lambda e, ps=ps, kc=kc, c0=c0, w=w: e.matmul(ps[0:n, 0:w], xn_t[:, kc, 0:n], w_in[:, kc, c0:c0 + w], start=(kc == 0), stop=(kc == KC - 1)),
                             reads=["xn_t", "b_w_in"], writes=[pk])
                    copy_op(alt("vector", "scalar"), proj[0:n, c0:c0 + w], ps[0:n, 0:w], [pk], ["proj"])

            PROJK = ["proj", "projr.0", "projr.1", "projr.2", "projr.3"]
            rts = {}
            for si, (H_, hf_) in enumerate(((8, 16), (2, 16), (8, 8), (1, 8))):
                rts[si] = [A.alloc([128, H_, hf_]) for _ in range(4)]

            def rope(n, tab):
                secs = []
                for si, (o, H, Dh, half, co, so, eng) in enumerate(((0, 8, 128, 16, 0, 16, "vector"), (1024, 2, 128, 16, 0, 16, "gpsimd"), (1536, 8, 64, 8, 32, 40, "gpsimd"), (2048, 1, 64, 8, 32, 40, "vector"))):
                    sec = proj[0:n, o:o + H * Dh].rearrange("p (h d) -> p h d", h=H)
                    x1 = sec[:, :, 0:half]; x2 = sec[:, :, half:2 * half]
                    cb = tab[:, co:co + half].unsqueeze(1).to_broadcast([n, H, half])
                    sb_ = tab[:, so:so + half].unsqueeze(1).to_broadcast([n, H, half])
                    t = [r[0:n, :, :] for r in rts[si]]
                    secs.append((si, eng, x1, x2, cb, sb_, t))
                for (si, eng, x1, x2, cb, sb_, t) in secs:
                    P.op(eng, lambda e, t=t, x1=x1, cb=cb: e.tensor_tensor(out=t[0], in0=x1, in1=cb, op=ALU.mult), reads=["proj", "rope"], writes=[f"rt{si}.0"])
                    P.op(eng, lambda e, t=t, x2=x2, sb_=sb_: e.tensor_tensor(out=t[1], in0=x2, in1=sb_, op=ALU.mult), reads=["proj", "rope"], writes=[f"rt{si}.1"])
                    P.op(eng, lambda e, t=t, x2=x2, cb=cb: e.tensor_tensor(out=t[2], in0=x2, in1=cb, op=ALU.mult), reads=["proj", "rope"], writes=[f"rt{si}.2"])
                    P.op(eng, lambda e, t=t, x1=x1, sb_=sb_: e.tensor_tensor(out=t[3], in0=x1, in1=sb_, op=ALU.mult), reads=["proj", "rope"], writes=[f"rt{si}.3"])
                for (si, eng, x1, x2, cb, sb_, t) in secs:
                    rk = [f"rt{si}.{i}" for i in range(4)]
                    P.op(eng, lambda e, t=t, x1=x1: e.tensor_tensor(out=x1, in0=t[0], in1=t[1], op=ALU.subtract), reads=rk, writes=[f"projr.{si}"])
                    P.op(eng, lambda e, t=t, x2=x2: e.tensor_tensor(out=x2, in0=t[2], in1=t[3], op=ALU.add), reads=rk, writes=[f"projr.{si}"])

            def bisect(np_, L, lo_cols, rkeys=("R",)):
                rkeys = list(rkeys)
                P.op("vector", lambda e: e.tensor_reduce(out=sc[0:np_, 0:1], in_=Isc[0:np_, 0:L], axis=AX.X, op=ALU.max), reads=["I"], writes=["sc"])
                P.op("vector", lambda e: e.tensor_reduce(out=sc[0:np_, 1:2], in_=Isc[0:np_, 0:lo_cols], axis=AX.X, op=ALU.min), reads=["I"], writes=["sc"])
                P.op("vector", lambda e: e.tensor_tensor(out=sc[0:np_, 2:3], in0=sc[0:np_, 0:1], in1=sc[0:np_, 1:2], op=ALU.subtract), reads=["sc"], writes=["sc"])
                P.op("vector", lambda e: e.tensor_scalar(steps[0:np_, :], pow2[0:np_, :], sc[0:np_, 2:3], None, ALU.mult), reads=["sc", "pow2"], writes=["steps"])
                P.op("vector", lambda e: e.tensor_tensor(out=mids[0:np_, 0:1], in0=sc[0:np_, 1:2], in1=steps[0:np_, 0:1], op=ALU.add), reads=["sc", "steps"], writes=["mids"])
                P.op("vector", lambda e: e.memset(cnts[0:np_, :], 0.0), writes=["cnts"])
                for k in range(NITER):
                    P.op("vector", lambda e, k=k: e.tensor_scalar(junk[0:np_, 0:L], Isc[0:np_, 0:L], mids[0:np_, k:k + 1], None, ALU.is_ge, ALU.add, accum_out=cnts[0:np_, k:k + 1]),
                         reads=["I", "mids", "cnts"], writes=rkeys + ["cnts"])
                    P.op("vector", lambda e, k=k: e.tensor_scalar(incs[0:np_, k:k + 1], cnts[0:np_, k:k + 1], float(TOPK), steps[0:np_, k:k + 1], ALU.is_ge, ALU.mult), reads=["cnts", "steps"], writes=["incs"])
                    P.op("vector", lambda e, k=k: e.scalar_tensor_tensor(out=mids[0:np_, k + 1:k + 2], in0=mids[0:np_, k:k + 1], scalar=steps[0:np_, k + 1:k + 2], in1=incs[0:np_, k:k + 1], op0=ALU.subtract, op1=ALU.add),
                         reads=["mids", "steps", "incs"], writes=["mids"])

            ACC0, ACC1 = psf[4], psf[5]
            qT2 = A.alloc([128, 8, 128], BF16)
            qTs = [qT, qT2]
            RKP = [f"R.{h}" for h in range(8)]

            def front(qt):
                t0 = qt * 128
                L = (qt + 1) * 128
                qTc = qTs[qt % 2]; qk = f"qT{qt % 2}"
                norm_tile(t0, 128)
                project(128)
                rope(128, ropeP[:, qt, :])
                P.op("sync", lambda e: e.dma_start(out=nk_p[j, t0:t0 + 128, :], in_=proj[:, 1024:1280]), reads=PROJK, dma=True)
                P.op("sync", lambda e: e.dma_start(out=nv_p[j, t0:t0 + 128, :], in_=proj[:, 1280:1536]), reads=PROJK, dma=True)
                P.op("sync", lambda e: e.dma_start(out=nki_p[j, t0:t0 + 128, :], in_=proj[:, 2048:2112]), reads=PROJK, dma=True)
                P.op("scalar", lambda e: e.activation(out=qb, in_=proj[:, 0:1024], func=AF.Copy), reads=PROJK, writes=["qb"])
                P.op("gpsimd", lambda e: e.tensor_copy(kb, proj[:, 1024:1280]), reads=PROJK, writes=["kb"])
                P.op("gpsimd", lambda e: e.tensor_copy(vtok[:, qt, :], proj[:, 1280:1536]), reads=PROJK, writes=[f"vtok.{qt}"])
                P.op("scalar", lambda e: e.activation(out=qib[:, 0:576], in_=proj[:, 1536:2112], func=AF.Copy), reads=PROJK, writes=["qib"])
                P.op("gpsimd", lambda e: e.tensor_copy(qib[:, 576:640], proj[:, 2048:2112]), reads=PROJK, writes=["qib"])
                for h in range(8):
                    P.op("gpsimd", lambda e, h=h: e.tensor_scalar(Dw[:, h, :], identb, proj[:, 2112 + h:2113 + h], None, ALU.mult), reads=PROJK + ["identb"], writes=["Dw"])
                pb, pbk = PSB()
                for h in range(8):
                    P.op("tensor", lambda e, pb=pb, h=h: e.transpose(pb[:, h * 128:(h + 1) * 128], qb[:, h * 128:(h + 1) * 128], identb), reads=["qb", "identb"], writes=[pbk])
                copy_op("vector", qTc.rearrange("p a b -> p (a b)"), pb[:, :], [pbk], [qk])
                pb, pbk = PSB()
                for g in range(2):
                    P.op("tensor", lambda e, pb=pb, g=g: e.transpose(pb[:, g * 128:(g + 1) * 128], kb[:, g * 128:(g + 1) * 128], identb), reads=["kb", "identb"], writes=[pbk])
                for pr in range(5):
                    P.op("tensor", lambda e, pb=pb, pr=pr: e.transpose(pb[:, (2 + pr) * 128:(3 + pr) * 128], qib[:, pr * 128:(pr + 1) * 128], identb), reads=["qib", "identb"], writes=[pbk])
                copy_op("vector", kT[:, :, t0:t0 + 128], pb[:, 0:256].rearrange("p (a b) -> p a b", a=2), [pbk], [f"kT.{qt}"])
                copy_op("vector", qiT2.rearrange("p a b -> p (a b)"), pb[:, 256:768], [pbk], ["qiT2"])
                copy_op("vector", kiT2[:, t0:t0 + 128], pb[:, 768:896], [pbk], [f"kiT2.{qt}"])
                for s0 in range(0, L, 512):
                    w = min(512, L - s0)
                    kk = [f"kiT2.{q}" for q in range(s0 // 128, (s0 + w) // 128)]
                    for h in range(8):
                        pr, hh = h // 2, h % 2
                        ps, pk = PS()
                        P.op("tensor", lambda e, ps=ps, pr=pr, hh=hh, s0=s0, w=w: e.matmul(ps[:, 0:w], qiT2[hh * 64:(hh + 1) * 64, pr, :], kiT2[hh * 64:(hh + 1) * 64, s0:s0 + w], start=True, stop=True),
                             reads=["qiT2"] + kk, writes=[pk])
                        if h % 4 != 3:
                            P.op("scalar", lambda e, ps=ps, h=h, w=w: e.activation(out=Rr[:, h, 0:w], in_=ps[:, 0:w], func=AF.Relu), reads=[pk], writes=[RKP[h]])
                        else:
                            P.op("vector", lambda e, ps=ps, h=h, w=w: e.tensor_scalar_max(Rr[:, h, 0:w], ps[:, 0:w], 0.0), reads=[pk], writes=[RKP[h]])
                    ps, pk = PS()
                    for h in range(8):
                        P.op("tensor", lambda e, ps=ps, h=h, w=w: e.matmul(ps[:, 0:w], Dw[:, h, :], Rr[:, h, 0:w], start=(h == 0), stop=(h == 7)), reads=["Dw", RKP[h]], writes=[pk])
                    copy_op("scalar", Isc[:, s0:s0 + w], ps[:, 0:w], [pk], ["I"])
                P.op("gpsimd", lambda e: e.tensor_tensor(out=Isc[:, t0:t0 + 128], in0=Isc[:, t0:t0 + 128], in1=tribias, op=ALU.add), reads=["I", "tribias"], writes=["I"])

            def bis(qt):
                if qt >= 2:
                    bisect(128, (qt + 1) * 128, qt * 128, rkeys=RKP)

            def msk_a(qt):
                L = (qt + 1) * 128
                thr = mids[:, NITER:NITER + 1] if qt >= 2 else sc[:, 3:4]
                P.op("vector", lambda e: e.tensor_scalar(mask[:, 0:L], Isc[:, 0:L], thr, None, ALU.is_ge), reads=["I", "mids", "thr0"], writes=["mask"])

            def msk_b(qt):
                for k0 in range(0, qt + 1, 8):
                    nk = min(8, qt + 1 - k0)
                    pb, pbk = PSB()
                    for kt in range(k0, k0 + nk):
                        P.op("tensor", lambda e, pb=pb, kt=kt, k0=k0: e.transpose(pb[:, (kt - k0) * 128:(kt - k0 + 1) * 128], mask[:, kt * 128:(kt + 1) * 128], identb), reads=["mask", "identb"], writes=[pbk])
                    copy_op("vector", maskT[:, k0:k0 + nk, :].rearrange("p a b -> p (a b)"), pb[:, 0:nk * 128], [pbk], ["maskT"])

            def back(qt):
                t0 = qt * 128
                qTc = qTs[qt % 2]; qk = f"qT{qt % 2}"
                items = [(g, kt) for g in range(2) for kt in range(qt + 1)]

                def s_mm(i):
                    g, kt = items[i]
                    ps, pk = PS()
                    P.op("tensor", lambda e, ps=ps, g=g, kt=kt: e.matmul(ps[:, 0:512], kT[:, g, kt * 128:(kt + 1) * 128], qTc[:, g * 4:(g + 1) * 4, :].rearrange("p a b -> p (a b)"), start=True, stop=True),
                         reads=[qk, f"kT.{kt}"], writes=[pk])
                    return ps, pk

                nxt = s_mm(0)
                for i, (g, kt) in enumerate(items):
                    ps, pk = nxt
                    if i + 1 < len(items):
                        nxt = s_mm(i + 1)
                    E = Eb[i % 2]; PT = PTb[i % 2]
                    P.op("scalar", lambda e, ps=ps, E=E: e.activation(out=E, in_=ps[:, 0:512], func=AF.Exp, scale=ATTN_SCALE), reads=[pk], writes=[f"E{i % 2}"])
                    P.op("gpsimd", lambda e, E=E, PT=PT, kt=kt: e.tensor_tensor(out=PT.rearrange("p (a b) -> p a b", a=4), in0=E.rearrange("p (a b) -> p a b", a=4), in1=maskT[:, kt, :].unsqueeze(1).to_broadcast([128, 4, 128]), op=ALU.mult),
                         reads=[f"E{i % 2}", "maskT"], writes=[f"PT{i % 2}"])
                    P.op("tensor", lambda e, g=g, kt=kt, PT=PT: e.matmul(ACC0[:, 0:512], vtok[:, kt, g * 128:(g + 1) * 128], PT, start=(kt == 0), stop=(kt == qt)), reads=[f"PT{i % 2}", f"vtok.{kt}"], writes=["acc0"])
                    P.op("tensor", lambda e, kt=kt, PT=PT: e.matmul(ACC1[:, 0:512], onesb, PT, start=(kt == 0), stop=(kt == qt)), reads=[f"PT{i % 2}", "onesb"], writes=["acc1"])
                    if kt == qt:
                        P.op("vector", lambda e: e.reciprocal(rden, ACC1[:, 0:512]), reads=["acc1"], writes=["rden"])
                        P.op("vector", lambda e, g=g: e.tensor_tensor(out=oTn[:, g * 4:(g + 1) * 4, :].rearrange("p a b -> p (a b)"), in0=ACC0[:, 0:512], in1=rden, op=ALU.mult), reads=["acc0", "rden"], writes=["oTn"])
                for dmc in range(KC):
                    ps, pk = PS()
                    for h in range(8):
                        P.op("tensor", lambda e, ps=ps, h=h, dmc=dmc: e.matmul(ps[:, 0:128], w_out[:, h, dmc * 128:(dmc + 1) * 128], oTn[:, h, :], start=(h == 0), stop=(h == 7)), reads=["b_w_out", "oTn"], writes=[pk])
                    P.op("vector", lambda e, ps=ps, dmc=dmc: e.tensor_tensor(out=xT[:, dmc, t0:t0 + 128], in0=xT[:, dmc, t0:t0 + 128], in1=ps[:, 0:128], op=ALU.add), reads=[pk, f"xT.{qt}"], writes=[f"xT.{qt}"])

            NQT = DEBUG.get('nqt', 16)
            if DEBUG.get("nopipe"):
                for qt in range(NQT):
                    front(qt); bis(qt); msk_a(qt); msk_b(qt); back(qt)
            else:
                for qt in range(NQT):
                    front(qt)
                    if qt > 0:
                        msk_b(qt - 1)
                    bis(qt)
                    if qt > 0:
                        back(qt - 1)
                    msk_a(qt)
                msk_b(NQT - 1)
                back(NQT - 1)
            if do_samp:
                P.barrier()
                A.reset(mB)
                S2 = Arena(arena_t, xn_end, base=xn_off)
                k_b = S2.alloc([128, 16, 256], BF16)
                v_b = S2.alloc([128, 16, 256], BF16)
                kT_b = S2.alloc([128, 2, 16, 128], BF16)
                kiT_b = S2.alloc([64, 16, 128], BF16)
                kg = [S2.alloc([128, 1024], BF16) for _ in range(2)]
                mask_s = A.alloc([64, SEQ + NS], BF16)
                maskT_all = A.alloc([128, 16, 64], BF16)
                maskT_new = A.alloc([64, 64], BF16)
                qT_s = A.alloc([128, 2, 64, 4], BF16)
                kT_new = A.alloc([128, 2, 64], BF16)
                qiT_s = A.alloc([64, 8, 64], BF16)
                kiT_new = A.alloc([64, 64], BF16)
                Dw_s = A.alloc([64, 8, 64], BF16)
                vb_new = A.alloc([64, 256], BF16)
                E_s = [A.alloc([128, 256], BF16) for _ in range(2)]
                PT_s = [A.alloc([128, 256], BF16) for _ in range(2)]
                rden_s = A.alloc([128, 256])
                oT_s = A.alloc([128, 8, 64], BF16)
                idr = A.alloc([128, 16], I32)
                idx = A.alloc([128, 16], I32)
                rr2 = {"k": 0}

                def PS2():
                    k = 4 + rr2["k"]
                    rr2["k"] = (rr2["k"] + 1) % 2
                    return psf[k], f"psf{k}"

                RK = [f"R.{h}" for h in range(8)]
                norm_tile(SEQ, NS)
                project(NS)
                rope(NS, ropeS[0:NS, :])
                P.op("sync", lambda e: e.dma_start(out=nk_s[j, :, :], in_=proj[0:NS, 1024:1280]), reads=PROJK, dma=True)
                P.op("sync", lambda e: e.dma_start(out=nv_s[j, :, :], in_=proj[0:NS, 1280:1536]), reads=PROJK, dma=True)
                P.op("sync", lambda e: e.dma_start(out=nki_s[j, :, :], in_=proj[0:NS, 2048:2112]), reads=PROJK, dma=True)
                P.op("scalar", lambda e: e.activation(out=qb[0:NS, :], in_=proj[0:NS, 0:1024], func=AF.Copy), reads=PROJK, writes=["qb"])
                P.op("gpsimd", lambda e: e.tensor_copy(kb[0:NS, :], proj[0:NS, 1024:1280]), reads=PROJK, writes=["kb"])
                P.op("gpsimd", lambda e: e.tensor_copy(vb_new[0:NS, :], proj[0:NS, 1280:1536]), reads=PROJK, writes=["vb_new"])
                P.op("vector", lambda e: e.tensor_copy(qib[0:NS, 0:576], proj[0:NS, 1536:2112]), reads=PROJK, writes=["qib"])
                for h in range(8):
                    P.op("gpsimd", lambda e, h=h: e.tensor_scalar(Dw_s[:, h, :], identb[0:NS, 0:NS], proj[0:NS, 2112 + h:2113 + h], None, ALU.mult), reads=PROJK + ["identb"], writes=["Dw_s"])
                pb, pbk = PSB()
                for h in range(8):
                    P.op("tensor", lambda e, pb=pb, h=h: e.transpose(pb[:, h * 64:(h + 1) * 64], qb[0:NS, h * 128:(h + 1) * 128], identb[0:NS, 0:NS]), reads=["qb", "identb"], writes=[pbk])
                for g in range(2):
                    P.op("vector", lambda e, pb=pb, g=g: e.tensor_copy(qT_s[:, g, :, :].rearrange("p t h -> p h t"), pb[:, g * 256:(g + 1) * 256].rearrange("p (h t) -> p h t", h=4)), reads=[pbk], writes=["qT_s"])
                pb, pbk = PSB()
                for g in range(2):
                    P.op("tensor", lambda e, pb=pb, g=g: e.transpose(pb[:, g * 64:(g + 1) * 64], kb[0:NS, g * 128:(g + 1) * 128], identb[0:NS, 0:NS]), reads=["kb", "identb"], writes=[pbk])
                for h in range(9):
                    P.op("tensor", lambda e, pb=pb, h=h: e.transpose(pb[0:64, 128 + h * 64:128 + (h + 1) * 64], qib[0:NS, h * 64:(h + 1) * 64], identb[0:NS, 0:NS]), reads=["qib", "identb"], writes=[pbk])
                P.op("vector", lambda e, pb=pb: e.tensor_copy(kT_new.rearrange("p a b -> p (a b)"), pb[:, 0:128]), reads=[pbk], writes=["kT_new"])
                P.op("vector", lambda e, pb=pb: e.tensor_copy(qiT_s.rearrange("p a b -> p (a b)"), pb[0:64, 128:640]), reads=[pbk], writes=["qiT_s"])
                P.op("vector", lambda e, pb=pb: e.tensor_copy(kiT_new, pb[0:64, 640:704]), reads=[pbk], writes=["kiT_new"])
                for jj in range(16):
                    src = bass.AP(pt.tensor, jj, [[0, 8], [16, 16]])
                    P.op("sync", lambda e, src=src, jj=jj: e.dma_start(out=idr[jj * 8:(jj + 1) * 8, :], in_=src, allow_slow_non_contiguous=True), writes=["idr"], dma=True)
                P.op("vector", lambda e: e.tensor_scalar(idx, idr, 8, q8[:, 0:1], ALU.mult, ALU.add), reads=["idr", "q8"], writes=["idx"])
                if USE_AG:
                    r0 = 0
                    cki_src = pool_full["cki", 0, j]; ck_src = [pool_full["ck", hf, j] for hf in range(2)]; cv_src = [pool_full["cv", hf, j] for hf in range(2)]
                else:
                    r0 = j * NPOOL * 8
                    cki_src = cki; ck_src = [ck, ck]; cv_src = [cv, cv]
                IACC = [(psf[i], f"psf{i}") for i in range(4)]
                for b in range(NB):
                    g_ = kg[b % 2]; gk = f"kg{b % 2}"
                    P.op("gpsimd", lambda e, g_=g_, b=b: e.indirect_dma_start(out=g_, out_offset=None, in_=cki_src, in_offset=bass.IndirectOffsetOnAxis(ap=idx[:, b:b + 1], axis=0), element_offset=r0 * 1024), reads=["idx", f"G:cki0_full{j}"], writes=[gk], dma=True)
                    for r8 in range(2):
                        pb, pbk = PSB()
                        for rr_ in range(8):
                            r = r8 * 8 + rr_
                            P.op("tensor", lambda e, pb=pb, g_=g_, r=r, rr_=rr_: e.transpose(pb[0:64, rr_ * 128:(rr_ + 1) * 128], g_[:, r * 64:(r + 1) * 64], identb), reads=[gk, "identb"], writes=[pbk])
                        P.op("vector", lambda e, pb=pb, r8=r8: e.tensor_copy(kiT_b[:, r8 * 8:(r8 + 1) * 8, :].rearrange("p a b -> p (a b)"), pb[0:64, :]), reads=[pbk], writes=[f"kiT_b.{r8}"])
                    for blk in range(4):
                        for h in range(8):
                            ps, pk = PS2()
                            P.op("tensor", lambda e, ps=ps, h=h, blk=blk: e.matmul(ps[0:64, 0:512], qiT_s[:, h, :], kiT_b[:, blk * 4:(blk + 1) * 4, :].rearrange("p a b -> p (a b)"), start=True, stop=True),
                                 reads=["qiT_s", f"kiT_b.{blk // 2}"], writes=[pk])
                            if h % 2 == 0:
                                P.op("scalar", lambda e, ps=ps, h=h, b=b: e.activation(out=Rr[0:64, h, :], in_=ps[0:64, 0:512], func=AF.Relu, scale=rowmask[:, b:b + 1]), reads=[pk, "rowmask"], writes=[RK[h]])
                            else:
                                P.op("vector", lambda e, ps=ps, h=h, b=b: e.tensor_scalar(Rr[0:64, h, :], ps[0:64, 0:512], rowmask[:, b:b + 1], 0.0, ALU.mult, ALU.max), reads=[pk, "rowmask"], writes=[RK[h]])
                        ia, iak = IACC[blk]
                        for h in range(8):
                            P.op("tensor", lambda e, ia=ia, h=h, b=b: e.matmul(ia[0:64, 0:512], Dw_s[:, h, :], Rr[0:64, h, :], start=(b == 0 and h == 0), stop=(b == NB - 1 and h == 7)), reads=["Dw_s", RK[h]], writes=[iak])
                ps, pk = PS2()
                for h in range(8):
                    P.op("tensor", lambda e, ps=ps, h=h: e.matmul(ps[0:64, h * 64:(h + 1) * 64], qiT_s[:, h, :], kiT_new, start=True, stop=True), reads=["qiT_s", "kiT_new"], writes=[pk])
                P.op("scalar", lambda e, ps=ps: e.activation(out=Rr[0:64, :, 0:64], in_=ps[0:64, 0:512].rearrange("p (h s) -> p h s", h=8), func=AF.Relu), reads=[pk], writes=RK)
                ps2, pk2 = PS2()
                for h in range(8):
                    P.op("tensor", lambda e, ps2=ps2, h=h: e.matmul(ps2[0:64, 0:64], Dw_s[:, h, :], Rr[0:64, h, 0:64], start=(h == 0), stop=(h == 7)), reads=["Dw_s"] + RK, writes=[pk2])
                P.op("vector", lambda e, ps2=ps2: e.tensor_tensor(out=Isc[0:64, SEQ:SEQ + NS], in0=ps2[0:64, 0:64], in1=blkbias, op=ALU.add), reads=[pk2, "blkbias"], writes=["I"])
                for blk in range(4):
                    ia, iak = IACC[blk]
                    copy_op("scalar" if blk % 2 == 0 else "vector", Isc[0:64, blk * 512:(blk + 1) * 512], ia[0:64, 0:512], [iak], ["I"])
                bisect(64, SEQ + NS, SEQ, rkeys=RK)
                P.op("vector", lambda e: e.tensor_scalar(mask_s[:, :], Isc[0:64, :], mids[0:64, NITER:NITER + 1], None, ALU.is_ge), reads=["I", "mids"], writes=["mask_s"])
                for r8 in range(2):
                    pb, pbk = PSB()
                    for rr_ in range(8):
                        r = r8 * 8 + rr_
                        P.op("tensor", lambda e, pb=pb, r=r, rr_=rr_: e.transpose(pb[:, rr_ * 64:(rr_ + 1) * 64], mask_s[:, r * 128:(r + 1) * 128], identb[0:64, 0:64]), reads=["mask_s", "identb"], writes=[pbk])
                    P.op("vector", lambda e, pb=pb, r8=r8: e.tensor_copy(maskT_all[:, r8 * 8:(r8 + 1) * 8, :].rearrange("p a b -> p (a b)"), pb[:, 0:512]), reads=[pbk], writes=["maskT_all"])
                pb, pbk = PSB()
                P.op("tensor", lambda e, pb=pb: e.transpose(pb[0:64, 0:64], mask_s[:, SEQ:SEQ + NS], identb[0:64, 0:64]), reads=["mask_s", "identb"], writes=[pbk])
                P.op("vector", lambda e, pb=pb: e.tensor_copy(maskT_new, pb[0:64, 0:64]), reads=[pbk], writes=["maskT_new"])
                OACC = [(psf[0], "psf0"), (psf[2], "psf2")]
                DACC = [(psf[1], "psf1"), (psf[3], "psf3")]
                ecnt = 0
                for g in range(2):
                    ps, pk = PS2()
                    P.op("tensor", lambda e, ps=ps, g=g: e.matmul(ps[0:64, 0:256], kT_new[:, g, :], qT_s[:, g, :, :].rearrange("p t h -> p (t h)"), start=True, stop=True), reads=["kT_new", "qT_s"], writes=[pk])
                    E = E_s[ecnt % 2]; PT = PT_s[ecnt % 2]; ek = f"E_s{ecnt % 2}"; ptk = f"PT_s{ecnt % 2}"
                    ecnt += 1
                    P.op("scalar", lambda e, ps=ps, E=E: e.activation(out=E[0:64, :], in_=ps[0:64, 0:256], func=AF.Exp, scale=ATTN_SCALE), reads=[pk], writes=[ek])
                    P.op("vector", lambda e, E=E, PT=PT: e.tensor_tensor(out=PT[0:64, :].rearrange("p (t h) -> p t h", h=4), in0=E[0:64, :].rearrange("p (t h) -> p t h", h=4), in1=maskT_new.unsqueeze(2).to_broadcast([64, 64, 4]), op=ALU.mult),
                         reads=[ek, "maskT_new"], writes=[ptk])
                    oa, oak = OACC[g]; da, dak = DACC[g]
                    P.op("tensor", lambda e, oa=oa, g=g, PT=PT: e.matmul(oa[:, 0:256], vb_new[:, g * 128:(g + 1) * 128], PT[0:64, :], start=True, stop=False), reads=["vb_new", ptk], writes=[oak])
                    P.op("tensor", lambda e, da=da, PT=PT: e.matmul(da[:, 0:256], onesb[0:64, :], PT[0:64, :], start=True, stop=False), reads=["onesb", ptk], writes=[dak])
                for b in range(NB):
                    for hf in range(2):
                        P.op("gpsimd", lambda e, b=b, hf=hf: e.indirect_dma_start(out=k_b[:, hf * 8:(hf + 1) * 8, :].rearrange("p a b -> p (a b)"), out_offset=None, in_=ck_src[hf], in_offset=bass.IndirectOffsetOnAxis(ap=idx[:, b:b + 1], axis=0), element_offset=(0 if USE_AG else r0 * 4096 + hf * 2048)),
                             reads=["idx", f"G:ck{hf}_full{j}"], writes=[f"k_b.{hf}"], dma=True)
                    for hf in range(2):
                        P.op("gpsimd", lambda e, b=b, hf=hf: e.indirect_dma_start(out=v_b[:, hf * 8:(hf + 1) * 8, :].rearrange("p a b -> p (a b)"), out_offset=None, in_=cv_src[hf], in_offset=bass.IndirectOffsetOnAxis(ap=idx[:, b:b + 1], axis=0), element_offset=(0 if USE_AG else r0 * 4096 + hf * 2048)),
                             reads=["idx", f"G:cv{hf}_full{j}"], writes=[f"v_b.{hf}"], dma=True)
                    for g in range(2):
                        for r8 in range(2):
                            pb, pbk = PSB()
                            for rr_ in range(8):
                                r = r8 * 8 + rr_
                                P.op("tensor", lambda e, pb=pb, g=g, r=r, rr_=rr_: e.transpose(pb[:, rr_ * 128:(rr_ + 1) * 128], k_b[:, r, g * 128:(g + 1) * 128], identb), reads=[f"k_b.{r8}", "identb"], writes=[pbk])
                            P.op("vector", lambda e, pb=pb, g=g, r8=r8: e.tensor_copy(kT_b[:, g, r8 * 8:(r8 + 1) * 8, :].rearrange("p a b -> p (a b)"), pb[:, :]), reads=[pbk], writes=[f"kT_b.{g}.{r8}"])
                    for g in range(2):
                        ps, pk = PS2()
                        for r in range(16):
                            P.op("tensor", lambda e, ps=ps, g=g, r=r, b=b: e.matmul(ps[:, r * 16:(r + 1) * 16], kT_b[:, g, r, :], qT_s[:, g, b * 4:(b + 1) * 4, :].rearrange("p t h -> p (t h)"), start=True, stop=True),
                                 reads=[f"kT_b.{g}.{r // 8}", "qT_s"], writes=[pk])
                        E = E_s[ecnt % 2]; PT = PT_s[ecnt % 2]; ek = f"E_s{ecnt % 2}"; ptk = f"PT_s{ecnt % 2}"
                        ecnt += 1
                        P.op("scalar", lambda e, ps=ps, E=E: e.activation(out=E, in_=ps[:, 0:256], func=AF.Exp, scale=ATTN_SCALE), reads=[pk], writes=[ek])
                        P.op("vector", lambda e, E=E, PT=PT, b=b: e.tensor_tensor(out=PT.rearrange("p (r t h) -> p r t h", r=16, t=4), in0=E.rearrange("p (r t h) -> p r t h", r=16, t=4), in1=maskT_all[:, :, b * 4:(b + 1) * 4].unsqueeze(3).to_broadcast([128, 16, 4, 4]), op=ALU.mult),
                             reads=[ek, "maskT_all"], writes=[ptk])
                        oa, oak = OACC[g]; da, dak = DACC[g]
                        for r in range(16):
                            last = (b == NB - 1 and r == 15)
                            P.op("tensor", lambda e, oa=oa, g=g, r=r, b=b, PT=PT, last=last: e.matmul(oa[:, b * 16:(b + 1) * 16], v_b[:, r, g * 128:(g + 1) * 128], PT[:, r * 16:(r + 1) * 16], start=False, stop=last), reads=[f"v_b.{r // 8}", ptk], writes=[oak])
                            P.op("tensor", lambda e, da=da, r=r, b=b, PT=PT, last=last: e.matmul(da[:, b * 16:(b + 1) * 16], onesb, PT[:, r * 16:(r + 1) * 16], start=False, stop=last), reads=["onesb", ptk], writes=[dak])
                for g in range(2):
                    oa, oak = OACC[g]; da, dak = DACC[g]
                    P.op("vector", lambda e, da=da: e.reciprocal(rden_s, da[:, 0:256]), reads=[dak], writes=["rden_s"])
                    P.op("vector", lambda e, oa=oa, g=g: e.tensor_tensor(out=oT_s[:, g * 4:(g + 1) * 4, :], in0=oa[:, 0:256].rearrange("p (t h) -> p h t", h=4), in1=rden_s.rearrange("p (t h) -> p h t", h=4), op=ALU.mult), reads=[oak, "rden_s"], writes=["oT_s"])
                for dmc in range(KC):
                    ps, pk = PS2()
                    for h in range(8):
                        P.op("tensor", lambda e, ps=ps, h=h, dmc=dmc: e.matmul(ps[:, 0:NS], w_out[:, h, dmc * 128:(dmc + 1) * 128], oT_s[:, h, :], start=(h == 0), stop=(h == 7)), reads=["b_w_out", "oT_s"], writes=[pk])
                    P.op("vector", lambda e, ps=ps, dmc=dmc: e.tensor_tensor(out=xT[:, dmc, SEQ:SEQ + NS], in0=xT[:, dmc, SEQ:SEQ + NS], in1=ps[:, 0:NS], op=ALU.add), reads=[pk, "xT.16"], writes=["xT.16"])
            A.reset(m)
            P.barrier()

        for l in range(n_layers):
            j = l // 2
            if l % 2 == 0:
                if not DEBUG.get("skip_a"):
                    layer_a(j, l)
                if l == 0:
                    pool_allgather()
            else:
                if do_b:
                    layer_b(j, l)
            if not DEBUG.get("skip_ffn"):
                ffn(l)
        final_norm_out()
        print("arena hi words", A.hi, "ops", len(P.ops))
        P.finalize(st)
        with nc.Block() as block:
            P.emit(block)
    return nc


def host_consts():
    c = {}
    c["c_ident"] = np.eye(128, dtype=np.float32)
    s = np.arange(128)[:, None]; t = np.arange(128)[None, :]
    c["c_tri01"] = (s <= t).astype(np.float32)
    c["c_tribias"] = np.where(t <= s, 0.0, NEG).astype(np.float32)
    b = np.arange(64) // 4; o = np.arange(64) % 4
    same = b[:, None] == b[None, :]
    c["c_blk01"] = (same & (o[:, None] <= o[None, :])).astype(np.float32)
    c["c_blkbias"] = np.where(same & (o[None, :] <= o[:, None]), 0.0, NEG).astype(np.float32)
    def tab(pos):
        inv16 = 500000.0 ** (-np.arange(16, dtype=np.float32) * 2.0 / 32)
        inv8 = 500000.0 ** (-np.arange(8, dtype=np.float32) * 2.0 / 16)
        a16 = pos.astype(np.float32)[:, None] * inv16[None, :]
        a8 = pos.astype(np.float32)[:, None] * inv8[None, :]
        return np.concatenate([np.cos(a16), np.sin(a16), np.cos(a8), np.sin(a8)], axis=1).astype(np.float32)
    tp = tab(np.arange(SEQ))
    c["c_ropeP"] = np.ascontiguousarray(tp.reshape(16, 128, 48).transpose(1, 0, 2).reshape(128, 16 * 48))
    c["c_ropeS"] = tab(2048 + (np.arange(64) % 4))
    c["c_q8"] = (np.arange(128) % 8).astype(np.int32).reshape(128, 1)
    c["c_rowmask"] = (np.arange(64)[:, None] // 4 == np.arange(16)[None, :]).astype(np.float32)
    p2 = (0.5 ** (np.arange(NITER + 1) + 1)).astype(np.float32)
    p2[NITER] = p2[NITER - 1]
    c["c_pow2"] = np.tile(p2[None, :], (128, 1))
    return c


_NC_CACHE = {}


def kernel(x_prompt, x_sample, cache_k, cache_v, cache_kidx, page_table, norm_mix, norm_ffn,
           a_w_in, a_v_gain, a_w_s, a_b_s, a_w_out, b_w_in, b_w_out, ffn_w1, ffn_w2, norm_final,
           _n_layers=4, _do_b=True, _do_samp=True):
    f = lambda a: np.ascontiguousarray(np.asarray(a, dtype=np.float32))
    x_prompt = f(x_prompt); x_sample = f(x_sample)
    consts = host_consts()
    shared = dict(consts)
    gl = lambda g: np.ascontiguousarray(f(g).reshape(4, 8, 128).transpose(2, 0, 1).reshape(128, 32))
    shared["gmix"] = gl(norm_mix); shared["gffn"] = gl(norm_ffn)
    shared["gfin"] = np.ascontiguousarray(f(norm_final).reshape(8, 128).T)
    shared["a_w_in"] = f(a_w_in); shared["a_v_gain"] = f(a_v_gain); shared["a_w_out"] = f(a_w_out)
    ws = f(a_w_s)
    shared["a_wsT"] = np.ascontiguousarray(ws.transpose(0, 3, 1, 2).reshape(2, 128, 8 * 128))
    w4 = ws[:, :, :4, :4]
    blk = np.zeros((2, 64, 8, 64), np.float32)
    for b in range(16):
        blk[:, b * 4:(b + 1) * 4, :, b * 4:(b + 1) * 4] = w4.transpose(0, 3, 1, 2)
    shared["a_wsTs"] = blk.reshape(2, 64, 8 * 64)
    bs = f(a_b_s)
    shared["a_bs"] = np.ascontiguousarray(bs.reshape(2, 8 * 128))
    shared["a_bss"] = np.ascontiguousarray(np.tile(bs[:, :, :4], (1, 1, 16)).reshape(2, 8 * 64))
    shared["b_w_in"] = f(b_w_in); shared["b_w_out"] = f(b_w_out)
    shared["ffn_w1"] = f(ffn_w1); shared["ffn_w2"] = f(ffn_w2)
    percore = [dict() for _ in range(8)]
    if _do_b and _do_samp:
        ckv = f(cache_k).reshape(2, NPOOL * 8, 16 * 256); cvv = f(cache_v).reshape(2, NPOOL * 8, 16 * 256); ckiv = f(cache_kidx).reshape(2, NPOOL * 8, 16 * 64)
        if USE_AG:
            SHR = NPOOL * 8 // 8
            for c in range(8):
                percore[c]["ck"] = np.ascontiguousarray(ckv[:, c * SHR:(c + 1) * SHR]).reshape(2 * SHR, 16 * 256)
                percore[c]["cv"] = np.ascontiguousarray(cvv[:, c * SHR:(c + 1) * SHR]).reshape(2 * SHR, 16 * 256)
                percore[c]["cki"] = np.ascontiguousarray(ckiv[:, c * SHR:(c + 1) * SHR]).reshape(2 * SHR, 16 * 64)
        else:
            shared["ck"] = ckv.reshape(2 * NPOOL * 8, 16 * 256)
            shared["cv"] = cvv.reshape(2 * NPOOL * 8, 16 * 256)
            shared["cki"] = ckiv.reshape(2 * NPOOL * 8, 16 * 64)
    ptab = np.asarray(page_table).astype(np.int32)
    in_maps = []
    for c in range(8):
        m = dict(shared)
        m.update(percore[c])
        m["xp"] = x_prompt[c]
        m["xs"] = np.ascontiguousarray(x_sample[c * NB:(c + 1) * NB].reshape(NS, D))
        m["pt"] = np.ascontiguousarray(ptab[c * NB:(c + 1) * NB])
        in_maps.append(m)
    key = (_n_layers, _do_b, _do_samp)
    if key not in _NC_CACHE:
        _NC_CACHE[key] = build_nc(_n_layers, _do_b, _do_samp)
    nc = _NC_CACHE[key]
    res = run_bass_kernel_spmd(nc, in_maps, core_ids=list(range(8)))
    R = res.results
    cat = lambda name: np.stack([R[c][name] for c in range(8)])
    y_prompt = cat("y_p")
    y_sample = cat("y_s").reshape(128, 4, D)
    nk_p = cat("nk_p").transpose(1, 0, 2, 3).reshape(2, 8, SEQ, 2, 128)
    nv_p = cat("nv_p").transpose(1, 0, 2, 3).reshape(2, 8, SEQ, 2, 128)
    nki_p = cat("nki_p").transpose(1, 0, 2, 3).reshape(2, 8, SEQ, 64)
    nk_s = cat("nk_s").transpose(1, 0, 2, 3).reshape(2, 128, 4, 2, 128)
    nv_s = cat("nv_s").transpose(1, 0, 2, 3).reshape(2, 128, 4, 2, 128)
    nki_s = cat("nki_s").transpose(1, 0, 2, 3).reshape(2, 128, 4, 64)
    cvp = cat("cv_p").transpose(1, 0, 2, 3).reshape(2, 8, 128, D)
    cvs = cat("cv_s").transpose(1, 0, 2, 3).reshape(2, 128, 4, D)
    return (y_prompt, y_sample, np.ascontiguousarray(nk_p), np.ascontiguousarray(nv_p), np.ascontiguousarray(nki_p),
            np.ascontiguousarray(nk_s), np.ascontiguousarray(nv_s), np.ascontiguousarray(nki_s),
            np.ascontiguousarray(cvp), np.ascontiguousarray(cvs))
```

```python
import math
import numpy as np
from contextlib import ExitStack
import concourse.bass as bass
import concourse.mybir as mybir
from concourse.bass_utils import run_bass_kernel_spmd

F32 = mybir.dt.float32
BF16 = mybir.dt.bfloat16
I32 = mybir.dt.int32
AF = mybir.ActivationFunctionType
ALU = mybir.AluOpType
AX = mybir.AxisListType

D = 1024
KC = 8
SEQ = 2048
NS = 64
NT = SEQ + NS
NB = 16
BP = 2120
EPS = 1e-6
ATTN_SCALE = float(128 ** -0.5)
NEG = -1.0e30
NITER = 20
TOPK = 256
NPOOL = 2560
USE_AG = False

ENGS = ("tensor", "vector", "scalar", "gpsimd", "sync")
SEM_LIMIT = 24000
NDMA_SEM = 10


class Prog:
    def __init__(self, nc):
        self.nc = nc
        self.ops = []

    def op(self, eng, fn, reads=(), writes=(), dma=False, semkey=None, nobar=False, inc=16):
        self.ops.append(dict(eng=eng, fn=fn, reads=tuple(reads), writes=tuple(writes), dma=dma, semkey=semkey, nobar=nobar, inc=inc))

    def barrier(self):
        self.ops.append(dict(eng=None, barrier=True))

    def finalize(self, stack):
        nc = self.nc
        ops = self.ops
        n = len(ops)
        last_writer = {}
        readers = {}
        deps = [None] * n
        needed = [False] * n
        last_on_eng = {}
        outstanding_dma = []
        bar_deps = []
        fresh = {}
        for i, o in enumerate(ops):
            if o.get("barrier"):
                bar_deps = list(last_on_eng.values()) + list(outstanding_dma)
                outstanding_dma = []
                last_writer = {k: v for k, v in last_writer.items() if k.startswith("G:")}
                readers = {k: v for k, v in readers.items() if k.startswith("G:")}
                fresh = {e: True for e in ENGS}
                continue
            d = set()
            for k in o["reads"]:
                if k in last_writer:
                    d.add(last_writer[k])
            for k in o["writes"]:
                if k in last_writer:
                    d.add(last_writer[k])
                for r in readers.get(k, ()):
                    d.add(r)
            if bar_deps and fresh.get(o["eng"], False):
                d.update(bar_deps)
                fresh[o["eng"]] = False
            d.discard(i)
            if o["eng"] == "tensor" and not o["dma"]:
                d = {j for j in d if not (ops[j]["eng"] == "tensor" and not ops[j]["dma"])}
            deps[i] = d
            for j in d:
                needed[j] = True
            for k in o["reads"]:
                readers.setdefault(k, []).append(i)
            for k in o["writes"]:
                last_writer[k] = i
                readers[k] = []
            if o["dma"]:
                if not o.get("nobar"):
                    outstanding_dma.append(i)
                needed[i] = True
            else:
                last_on_eng[o["eng"]] = i

        def newsem(name):
            return stack.enter_context(nc.semaphore(name))
        eng_sems = {e: [newsem(f"s_{e}_0")] for e in ENGS}
        eng_cnt = {e: 0 for e in ENGS}
        dma_sems = {e: [newsem(f"d_{e}_{k}") for k in range(NDMA_SEM)] for e in ("sync", "gpsimd", "scalar")}
        dma_cnt = {e: [0] * NDMA_SEM for e in dma_sems}
        dma_rr = {e: 0 for e in dma_sems}
        dma_last = {e: [None] * NDMA_SEM for e in dma_sems}
        event = [None] * n
        extra_dep = [None] * n
        ded_sems = {}
        for i, o in enumerate(ops):
            if o.get("barrier"):
                continue
            e = o["eng"]
            if o["dma"] and o.get("semkey"):
                sk = o["semkey"]
                if sk not in ded_sems:
                    ded_sems[sk] = [newsem("g_" + sk.replace(".", "_")), 0]
                ded_sems[sk][1] += o["inc"]
                event[i] = (ded_sems[sk][0], ded_sems[sk][1])
                o["sem"] = ded_sems[sk][0]
            elif o["dma"]:
                k = dma_rr[e]
                dma_rr[e] = (k + 1) % NDMA_SEM
                if dma_last[e][k] is not None:
                    extra_dep[i] = dma_last[e][k]
                dma_cnt[e][k] += 16
                event[i] = (dma_sems[e][k], dma_cnt[e][k])
                dma_last[e][k] = event[i]
                o["sem"] = dma_sems[e][k]
            elif needed[i]:
                if eng_cnt[e] >= SEM_LIMIT:
                    eng_sems[e].append(newsem(f"s_{e}_{len(eng_sems[e])}"))
                    eng_cnt[e] = 0
                eng_cnt[e] += 1
                event[i] = (eng_sems[e][-1], eng_cnt[e])
                o["sem"] = eng_sems[e][-1]
        seen = {e: {} for e in ENGS}
        for i, o in enumerate(ops):
            if o.get("barrier"):
                continue
            e = o["eng"]
            evs = [event[j] for j in deps[i]]
            if extra_dep[i] is not None:
                evs.append(extra_dep[i])
            best = {}
            for (s, v) in evs:
                key = id(s)
                if v > seen[e].get(key, 0) and v > best.get(key, (None, 0))[1]:
                    best[key] = (s, v)
            o["waits"] = list(best.values())
            for key, (s, v) in best.items():
                seen[e][key] = v
        self.final_events = [(v[0], v[1]) for v in ded_sems.values()]
        for e in dma_sems:
            for k in range(NDMA_SEM):
                if dma_last[e][k] is not None:
                    self.final_events.append(dma_last[e][k])

    def emit(self, block):
        ops = self.ops
        final_events = self.final_events

        def run(engname, eng):
            for o in ops:
                if o.get("barrier") or o["eng"] != engname:
                    continue
                for (s, v) in o["waits"]:
                    eng.wait_ge(s, v)
                ins = o["fn"](eng)
                if o["dma"] and o["inc"] == 1:
                    ins.then_inc(o["sem"])
                elif o["dma"]:
                    ins.then_inc(o["sem"], 16)
                elif "sem" in o:
                    ins.then_inc(o["sem"], 1)
            if engname == "sync":
                for (s, v) in final_events:
                    eng.wait_ge(s, v)

        @block.tensor
        def _(t):
            run("tensor", t)

        @block.vector
        def _(v):
            run("vector", v)

        @block.scalar
        def _(a):
            run("scalar", a)

        @block.gpsimd
        def _(g):
            run("gpsimd", g)

        @block.sync
        def _(s):
            run("sync", s)


class Arena:
    def __init__(self, t, nwords, base=0):
        self.t = t
        self.n = nwords
        self.off = base
        self.hi = 0

    def alloc(self, shape, dt=F32):
        free = 1
        for s in shape[1:]:
            free *= s
        words = free if dt != BF16 else (free + 1) // 2
        v = self.t[0:shape[0], self.off:self.off + words]
        if dt == BF16:
            v = v.bitcast(BF16)
            if free != 2 * words:
                v = v[:, 0:free]
        elif dt == I32:
            v = v.bitcast(I32)
        self.off += words
        self.hi = max(self.hi, self.off)
        assert self.off <= self.n, f"arena overflow {self.off} > {self.n}"
        if len(shape) == 3:
            v = v.rearrange("p (a b) -> p a b", a=shape[1])
        elif len(shape) == 4:
            v = v.rearrange("p (a b c) -> p a b c", a=shape[1], b=shape[2])
        return v

    def mark(self):
        return self.off

    def reset(self, m):
        self.off = m


DEBUG = {}
GROUPS = [(0, 512), (512, 512), (1024, 512), (1536, 512), (2048, 64)]


def tkeys(prefix, t0, n):
    return [f"{prefix}.{tt}" for tt in range(t0 // 128, (t0 + n + 127) // 128)]


def build_nc(n_layers=4, do_b=True, do_samp=True):
    nc = bass.Bass("TRN2", target_bir_lowering=False)
    st = ExitStack()
    I = {}
    O = {}

    def inp(name, shape, dt=F32):
        I[name] = nc.dram_tensor(name, list(shape), dt, kind="ExternalInput").ap()
        return I[name]

    def outp(name, shape, dt=F32):
        O[name] = nc.dram_tensor(name, list(shape), dt, kind="ExternalOutput").ap()
        return O[name]

    xp = inp("xp", [SEQ, D]); xs = inp("xs", [NS, D])
    SHR = NPOOL * 8 // 8
    if do_b and do_samp:
        if USE_AG:
            ck_sh = inp("ck", [2 * SHR, 16 * 256]); cv_sh = inp("cv", [2 * SHR, 16 * 256]); cki_sh = inp("cki", [2 * SHR, 16 * 64])
            pool_in = {}; pool_full = {}
            POOLS = [("cki", 0, 1024, 0)] + [(nm, hf, 2048, hf * 2048) for nm in ("ck", "cv") for hf in range(2)]
            for nm, hf, w, c0 in POOLS:
                for jj in range(2):
                    pool_in[nm, hf, jj] = nc.dram_tensor(f"{nm}{hf}_in{jj}", [SHR, w], F32, kind="Internal").ap()
                    pool_full[nm, hf, jj] = nc.dram_tensor(f"{nm}{hf}_full{jj}", [NPOOL * 8, w], F32, kind="Internal").ap()
        else:
            ck = inp("ck", [2 * NPOOL * 8, 16 * 256]); cv = inp("cv", [2 * NPOOL * 8, 16 * 256]); cki = inp("cki", [2 * NPOOL * 8, 16 * 64])
    pt = inp("pt", [NB, 16], I32)
    gmix = inp("gmix", [128, 32]); gffn = inp("gffn", [128, 32]); gfin = inp("gfin", [128, 8])
    a_w_in = inp("a_w_in", [2, D, 2048]); a_v_gain = inp("a_v_gain", [2, D])
    a_wsT = inp("a_wsT", [2, 128, 8 * 128]); a_wsTs = inp("a_wsTs", [2, 64, 8 * 64])
    a_bs = inp("a_bs", [2, 8 * 128]); a_bss = inp("a_bss", [2, 8 * 64])
    a_w_out = inp("a_w_out", [2, D, D])
    b_w_in = inp("b_w_in", [2, D, BP]); b_w_out = inp("b_w_out", [2, D, D])
    ffn_w1 = inp("ffn_w1", [4, D, 4096]); ffn_w2 = inp("ffn_w2", [4, 4096, D])
    c_ident = inp("c_ident", [128, 128]); c_tri01 = inp("c_tri01", [128, 128]); c_tribias = inp("c_tribias", [128, 128])
    c_blk01 = inp("c_blk01", [64, 64]); c_blkbias = inp("c_blkbias", [64, 64])
    c_ropeP = inp("c_ropeP", [128, 16 * 48]); c_ropeS = inp("c_ropeS", [64, 48])
    c_q8 = inp("c_q8", [128, 1], I32); c_pow2 = inp("c_pow2", [128, NITER + 1])
    c_rowmask = inp("c_rowmask", [64, 16])

    y_p = outp("y_p", [SEQ, D]); y_s = outp("y_s", [NS, D])
    nk_p = outp("nk_p", [2, SEQ, 256]); nv_p = outp("nv_p", [2, SEQ, 256]); nki_p = outp("nki_p", [2, SEQ, 64])
    nk_s = outp("nk_s", [2, NS, 256]); nv_s = outp("nv_s", [2, NS, 256]); nki_s = outp("nki_s", [2, NS, 64])
    cv_p = outp("cv_p", [2, 128, D]); cv_s = outp("cv_s", [2, NS, D])

    with st:
        ARW = 53100
        arena_t = st.enter_context(nc.sbuf_tensor("arena", [128, ARW], F32))
        A = Arena(arena_t, ARW)
        psf = [st.enter_context(nc.psum_tensor(f"psf{i}", [128, 512], F32)) for i in range(6)]
        psb = [st.enter_context(nc.psum_tensor(f"psb{i}", [128, 1024], BF16)) for i in range(2)]
        P = Prog(nc)
        rr = {"f": 0, "b": 0, "eng": 0}

        def PS(pool=None):
            k = rr["f"]
            rr["f"] = (k + 1) % 4
            return psf[k], f"psf{k}"

        def PSB():
            k = rr["b"]
            rr["b"] = (k + 1) % DEBUG.get("npsb", 2)
            return psb[k], f"psb{k}"

        def alt(*engs):
            rr["eng"] += 1
            return engs[rr["eng"] % len(engs)]

        def copy_op(eng, out, in_, reads, writes):
            if eng == "scalar":
                P.op("scalar", lambda e: e.activation(out=out, in_=in_, func=AF.Copy), reads, writes)
            else:
                P.op(eng, lambda e: e.tensor_copy(out, in_), reads, writes)

        xT = A.alloc([128, KC, NT])
        xn_off = A.mark()
        xnT = A.alloc([128, KC, NT], BF16)
        xn_end = A.mark()
        identf = A.alloc([128, 128]); identb = A.alloc([128, 128], BF16); onesb = A.alloc([128, 128], BF16)
        tri01 = A.alloc([128, 128], BF16); tribias = A.alloc([128, 128])
        blk01 = A.alloc([64, 64], BF16); blkbias = A.alloc([64, 64])
        ropeP = A.alloc([128, 16, 48]); ropeS = A.alloc([64, 48])
        gmix_t = A.alloc([128, 32]); gffn_t = A.alloc([128, 32]); gfin_t = A.alloc([128, 8])
        q8 = A.alloc([128, 1], I32); pow2 = A.alloc([128, NITER + 1])
        rowmask = A.alloc([64, 16])
        stage = A.alloc([128, 128])

        def load(dst, src, key, eng="sync", **kw):
            P.op(eng, lambda e: e.dma_start(out=dst, in_=src, **kw), writes=[key], dma=True)

        load(identf, c_ident[:, :], "identf"); load(tribias, c_tribias[:, :], "tribias")
        load(blkbias, c_blkbias[:, :], "blkbias")
        load(ropeP, c_ropeP.rearrange("p (a b) -> p a b", a=16), "rope"); load(ropeS, c_ropeS[:, :], "rope")
        load(gmix_t, gmix[:, :], "gains"); load(gffn_t, gffn[:, :], "gains"); load(gfin_t, gfin[:, :], "gains")
        load(q8, c_q8[:, :], "q8"); load(pow2, c_pow2[:, :], "pow2")
        load(rowmask, c_rowmask[:, :], "rowmask")
        load(identb, c_ident[:, :], "identb", eng="gpsimd")
        load(tri01, c_tri01[:, :], "tri01", eng="gpsimd")
        load(blk01, c_blk01[:, :], "blk01", eng="gpsimd")
        P.op("vector", lambda e: e.memset(onesb, 1.0), writes=["onesb"])

        NCH = 4
        def pool_copies():
            if not (do_b and do_samp and USE_AG):
                return
            srcs = {"cki": cki_sh, "ck": ck_sh, "cv": cv_sh}
            for jj in range(2):
                for nm, hf, w, c0 in POOLS:
                    rows = SHR // NCH
                    for ch in range(NCH):
                        dst = pool_in[nm, hf, jj][ch * rows:(ch + 1) * rows, :]
                        src = srcs[nm][jj * SHR + ch * rows:jj * SHR + (ch + 1) * rows, c0:c0 + w]
                        P.op("sync", lambda e, dst=dst, src=src: e.dma_start(out=dst, in_=src), writes=[f"G:{nm}{hf}_in{jj}.{ch}"], dma=True, semkey=f"cp.{nm}{hf}.{jj}", nobar=True)

        def pool_allgather():
            if not (do_b and do_samp and USE_AG):
                return
            for jj in range(2):
                for nm, hf, w, c0 in POOLS:
                    pin = pool_in[nm, hf, jj]; pfull = pool_full[nm, hf, jj]
                    P.op("gpsimd", lambda e, pin=pin, pfull=pfull: e.collective_compute("AllGather", ALU.bypass, replica_groups=[list(range(8))], ins=[pin.opt()], outs=[pfull.opt()]),
                         reads=[f"G:{nm}{hf}_in{jj}.{ch}" for ch in range(NCH)], writes=[f"G:{nm}{hf}_full{jj}"], dma=True, semkey=f"ag.{nm}{hf}.{jj}", nobar=True, inc=1)

        pool_copies()
        m0 = A.mark()
        xtok = [A.alloc([128, D]) for _ in range(2)]
        for tt in range(17):
            n = 128 if tt < 16 else NS
            xb = xtok[tt % 2]
            src = xp[tt * 128:(tt + 1) * 128, :] if tt < 16 else xs[:, :]
            P.op("sync", lambda e, xb=xb, src=src, n=n: e.dma_start(out=xb[0:n, :], in_=src), writes=[f"xtok{tt % 2}"], dma=True)
            for half in range(2):
                ps, pk = PS()
                for cc in range(4):
                    c = half * 4 + cc
                    P.op("tensor", lambda e, ps=ps, xb=xb, c=c, cc=cc, n=n: e.transpose(ps[:, cc * 128:cc * 128 + n], xb[0:n, c * 128:(c + 1) * 128], identf[0:n, 0:n]),
                         reads=[f"xtok{tt % 2}", "identf"], writes=[pk])
                dst = xT[:, half * 4:half * 4 + 4, tt * 128:tt * 128 + n]
                srcp = ps[:, :].rearrange("p (a b) -> p a b", a=4)[:, :, 0:n]
                copy_op(alt("vector", "scalar"), dst, srcp, [pk], [f"xT.{tt}"])
        A.reset(m0)
        P.barrier()

        def rmsnorm(gain, gcol0, dst_key="xn"):
            m = A.mark()
            sq = [A.alloc([128, KC, 512], BF16) for _ in range(1)] * 2
            rs = [A.alloc([128, 512]) for _ in range(1)] * 2
            for gi, (t0, n) in enumerate(GROUPS):
                gi = 0
                s_ = sq[gi % 2]; r_ = rs[gi % 2]
                xk = tkeys("xT", t0, n)
                P.op("scalar", lambda e, s_=s_, t0=t0, n=n: e.activation(out=s_[:, :, 0:n], in_=xT[:, :, t0:t0 + n], func=AF.Square),
                     reads=xk, writes=[f"sq{gi % 2}"])
                ps, pk = PS()
                for c in range(KC):
                    P.op("tensor", lambda e, ps=ps, s_=s_, c=c, n=n: e.matmul(ps[:, 0:n], onesb, s_[:, c, 0:n], start=(c == 0), stop=(c == KC - 1)),
                         reads=[f"sq{gi % 2}", "onesb"], writes=[pk])
                P.op("scalar", lambda e, ps=ps, r_=r_, n=n: e.activation(out=r_[:, 0:n], in_=ps[:, 0:n], func=AF.Sqrt, scale=1.0 / D, bias=epsb[:, 0:1]),
                     reads=[pk, "epsb"], writes=[f"rs{gi % 2}"])
                P.op("vector", lambda e, r_=r_, n=n: e.reciprocal(r_[:, 0:n], r_[:, 0:n]), reads=[f"rs{gi % 2}"], writes=[f"rs{gi % 2}"])
                for c in range(KC):
                    eng = "vector"
                    P.op(eng, lambda e, c=c, t0=t0, n=n, r_=r_: e.scalar_tensor_tensor(out=xnT[:, c, t0:t0 + n], in0=xT[:, c, t0:t0 + n], scalar=gain[:, gcol0 + c:gcol0 + c + 1], in1=r_[:, 0:n], op0=ALU.mult, op1=ALU.mult),
                         reads=xk + [f"rs{gi % 2}", "gains"], writes=tkeys(dst_key, t0, n))
            A.reset(m)

        epsb = A.alloc([128, 1])
        P.op("vector", lambda e: e.memset(epsb, EPS), writes=["epsb"])

        wload_eng = "gpsimd"

        def ffn(l):
            m = A.mark()
            w1g = [A.alloc([128, KC, 1024], BF16) for _ in range(2)]
            w2g = [A.alloc([128, KC, 1024], BF16) for _ in range(2)]
            hT = [A.alloc([128, KC, 512], BF16) for _ in range(2)]
            rl = [A.alloc([128, 512], BF16) for _ in range(2)]
            cnt = 0

            def wload(fg):
                sl = fg % 2
                for kc in range(KC):
                    P.op(wload_eng, lambda e, sl=sl, kc=kc, fg=fg: e.dma_start(out=w1g[sl][:, kc, :], in_=ffn_w1[l, kc * 128:(kc + 1) * 128, fg * 1024:(fg + 1) * 1024]),
                         writes=[f"w1g{sl}.{kc}"], dma=True)
                for fc in range(KC):
                    P.op(wload_eng, lambda e, sl=sl, fc=fc, fg=fg: e.dma_start(out=w2g[sl][:, fc, :], in_=ffn_w2[l, fg * 1024 + fc * 128:fg * 1024 + (fc + 1) * 128, :]),
                         writes=[f"w2g{sl}.{fc}"], dma=True)

            wload(0)
            wload(1)
            rmsnorm(gffn_t, l * 8)
            items = [(fg, gi) for fg in range(4) for gi in range(len(GROUPS))]

            def Hph(i):
                fg, gi = items[i]; sl = fg % 2; hs = i % 2; (t0, n) = GROUPS[gi]
                for fc in range(KC):
                    ps, pk = PS()
                    for kc in range(KC):
                        P.op("tensor", lambda e, ps=ps, sl=sl, kc=kc, fc=fc, t0=t0, n=n: e.matmul(ps[:, 0:n], w1g[sl][:, kc, fc * 128:(fc + 1) * 128], xnT[:, kc, t0:t0 + n], start=(kc == 0), stop=(kc == KC - 1)),
                             reads=[f"w1g{sl}.{kc}"] + tkeys("xn", t0, n), writes=[pk])
                    rb = rl[fc % 2]
                    P.op("scalar", lambda e, ps=ps, rb=rb, n=n: e.activation(out=rb[:, 0:n], in_=ps[:, 0:n], func=AF.Relu), reads=[pk], writes=[f"rl{fc % 2}"])
                    P.op("gpsimd", lambda e, rb=rb, hs=hs, fc=fc, n=n: e.tensor_tensor(out=hT[hs][:, fc, 0:n], in0=rb[:, 0:n], in1=rb[:, 0:n], op=ALU.mult),
                         reads=[f"rl{fc % 2}"], writes=[f"hT{hs}.{fc}"])

            def Yph(i):
                fg, gi = items[i]; sl = fg % 2; hs = i % 2; (t0, n) = GROUPS[gi]
                for dmc in range(KC):
                    ps, pk = PS()
                    for fc in range(KC):
                        P.op("tensor", lambda e, ps=ps, sl=sl, fc=fc, dmc=dmc, hs=hs, n=n: e.matmul(ps[:, 0:n], w2g[sl][:, fc, dmc * 128:(dmc + 1) * 128], hT[hs][:, fc, 0:n], start=(fc == 0), stop=(fc == KC - 1)),
                             reads=[f"w2g{sl}.{fc}", f"hT{hs}.{fc}"], writes=[pk])
                    P.op("vector", lambda e, ps=ps, dmc=dmc, t0=t0, n=n: e.tensor_tensor(out=xT[:, dmc, t0:t0 + n], in0=xT[:, dmc, t0:t0 + n], in1=ps[:, 0:n], op=ALU.add),
                         reads=[pk] + tkeys("xT", t0, n), writes=tkeys("xT", t0, n))

            Hph(0)
            for i in range(len(items)):
                fg, gi = items[i]
                if gi == 0 and fg >= 1 and fg + 1 < 4:
                    wload(fg + 1)
                if i + 1 < len(items):
                    Hph(i + 1)
                Yph(i)
            A.reset(m)
            P.barrier()

        def layer_a(j, l):
            m = A.mark()
            w_in = A.alloc([128, KC, 2048], BF16)
            w_out = A.alloc([128, KC, D], BF16)
            wsT = A.alloc([128, 8, 128], BF16); wsTs = A.alloc([64, 8, 64], BF16)
            bsb = A.alloc([128, 8, 128]); bsbs = A.alloc([128, 8, 64])
            vgain = A.alloc([128, D])
            uT = A.alloc([128, KC, 512], BF16)
            umT = uT
            vg = A.alloc([128, D]); vn = vg; vnb = A.alloc([128, D], BF16)
            junk = A.alloc([128, D], BF16)
            stat = A.alloc([128, 8])
            mixs = [A.alloc([128, 128]) for _ in range(2)]
            for kc in range(KC):
                P.op(wload_eng, lambda e, kc=kc: e.dma_start(out=w_in[:, kc, :], in_=a_w_in[j, kc * 128:(kc + 1) * 128, :]), writes=[f"a_w_in.{kc}"], dma=True)
            for kc in range(KC):
                P.op(wload_eng, lambda e, kc=kc: e.dma_start(out=w_out[:, kc, :], in_=a_w_out[j, kc * 128:(kc + 1) * 128, :]), writes=[f"a_w_out.{kc}"], dma=True)
            load(wsT, a_wsT[j].rearrange("p (a b) -> p a b", a=8), "wsT", eng="gpsimd")
            load(wsTs, a_wsTs[j].rearrange("p (a b) -> p a b", a=8), "wsTs", eng="gpsimd")
            load(bsb, a_bs[j].partition_broadcast(128).rearrange("p (a b) -> p a b", a=8), "bsb")
            load(bsbs, a_bss[j].partition_broadcast(128).rearrange("p (a b) -> p a b", a=8), "bsbs")
            load(vgain, a_v_gain[j].partition_broadcast(128), "vgain")
            P.op("vector", lambda e: e.tensor_tensor(out=wsT, in0=wsT, in1=tri01.unsqueeze(1).to_broadcast([128, 8, 128]), op=ALU.mult), reads=["wsT", "tri01"], writes=["wsT"])
            P.op("vector", lambda e: e.tensor_tensor(out=wsTs, in0=wsTs, in1=blk01.unsqueeze(1).to_broadcast([64, 8, 64]), op=ALU.mult), reads=["wsTs", "blk01"], writes=["wsTs"])
            rmsnorm(gmix_t, l * 8)
            for (t0, n) in GROUPS:
                for gc in range(8):
                    ps, pk = PS()
                    for kc in range(KC):
                        P.op("tensor", lambda e, ps=ps, kc=kc, gc=gc, t0=t0, n=n: e.matmul(ps[:, 0:n], w_in[:, kc, gc * 128:(gc + 1) * 128], xnT[:, kc, t0:t0 + n], start=(kc == 0), stop=(kc == KC - 1)),
                             reads=[f"a_w_in.{kc}"] + tkeys("xn", t0, n), writes=[pk])
                    P.op("scalar", lambda e, ps=ps, gc=gc, n=n: e.activation(out=uT[:, gc, 0:n], in_=ps[:, 0:n], func=AF.Gelu_apprx_tanh), reads=[pk], writes=[f"uT.{gc}"])
                ntile = (n + 127) // 128
                for ti in range(ntile):
                    tt = t0 // 128 + ti
                    tn = min(128, n)
                    c0 = ti * 128
                    P.op("vector", lambda e: e.memset(stat, 0.0), writes=["stat"])
                    for half in range(2):
                        ps, pk = PS()
                        for kc in range(KC):
                            P.op("tensor", lambda e, ps=ps, kc=kc, half=half, tt=tt, tn=tn: e.matmul(ps[0:tn, :], xnT[:, kc, tt * 128:tt * 128 + tn], w_in[:, kc, 1024 + half * 512:1024 + (half + 1) * 512], start=(kc == 0), stop=(kc == KC - 1)),
                                 reads=[f"a_w_in.{kc}", f"xn.{tt}"], writes=[pk])
                        P.op("scalar", lambda e, ps=ps, half=half, tn=tn: e.activation(out=vg[0:tn, half * 512:(half + 1) * 512], in_=ps[0:tn, :], func=AF.Gelu_apprx_tanh, accum_out=stat[0:tn, half:half + 1]),
                             reads=[pk, "stat"], writes=[f"vg.{half}", "stat"])
                    P.op("scalar", lambda e, tn=tn: e.activation(out=junk[0:tn, :], in_=vg[0:tn, :], func=AF.Square, accum_out=stat[0:tn, 2:3]), reads=["vg.0", "vg.1", "stat"], writes=["junk", "stat"])
                    P.op("vector", lambda e, tn=tn: e.tensor_tensor(out=stat[0:tn, 3:4], in0=stat[0:tn, 0:1], in1=stat[0:tn, 1:2], op=ALU.add), reads=["stat"], writes=["stat"])
                    P.op("vector", lambda e, tn=tn: e.tensor_scalar(stat[0:tn, 3:4], stat[0:tn, 3:4], 1.0 / D, None, ALU.mult), reads=["stat"], writes=["stat"])
                    P.op("vector", lambda e, tn=tn: e.tensor_tensor(out=stat[0:tn, 4:5], in0=stat[0:tn, 3:4], in1=stat[0:tn, 3:4], op=ALU.mult), reads=["stat"], writes=["stat"])
                    P.op("vector", lambda e, tn=tn: e.scalar_tensor_tensor(out=stat[0:tn, 5:6], in0=stat[0:tn, 2:3], scalar=1.0 / D, in1=stat[0:tn, 4:5], op0=ALU.mult, op1=ALU.subtract), reads=["stat"], writes=["stat"])
                    P.op("scalar", lambda e, tn=tn: e.activation(out=stat[0:tn, 6:7], in_=stat[0:tn, 5:6], func=AF.Sqrt, bias=epsb[0:tn, 0:1]), reads=["stat", "epsb"], writes=["stat"])
                    P.op("vector", lambda e, tn=tn: e.reciprocal(stat[0:tn, 6:7], stat[0:tn, 6:7]), reads=["stat"], writes=["stat"])
                    P.op("vector", lambda e, tn=tn: e.tensor_scalar(vn[0:tn, :], vg[0:tn, :], stat[0:tn, 3:4], stat[0:tn, 6:7], ALU.subtract, ALU.mult), reads=["vg.0", "vg.1", "stat"], writes=["vg.0", "vg.1"])
                    P.op("gpsimd", lambda e, tn=tn: e.tensor_tensor(out=vn[0:tn, :], in0=vn[0:tn, :], in1=vgain[0:tn, :], op=ALU.mult), reads=["vg.0", "vg.1", "vgain"], writes=["vg.0", "vg.1"])
                    P.op("vector", lambda e, tn=tn: e.tensor_copy(vnb[0:tn, :], vn[0:tn, :]), reads=["vg.0", "vg.1"], writes=["vnb"])
                    if tt == 15:
                        P.op("sync", lambda e: e.dma_start(out=cv_p[j, :, :], in_=vn[:, :]), reads=["vg.0", "vg.1"], dma=True)
                    if tt == 16:
                        P.op("sync", lambda e: e.dma_start(out=cv_s[j, :, :], in_=vn[0:NS, :]), reads=["vg.0", "vg.1"], dma=True)
                    for gc in range(8):
                        ps, pk = PS()
                        if tt < 16:
                            P.op("tensor", lambda e, ps=ps, gc=gc: e.matmul(ps[:, 0:128], vnb[:, gc * 128:(gc + 1) * 128], wsT[:, gc, :], start=True, stop=True), reads=["vnb", "wsT"], writes=[pk])
                            bias = bsb[:, gc, :]
                        else:
                            P.op("tensor", lambda e, ps=ps, gc=gc: e.matmul(ps[:, 0:NS], vnb[0:NS, gc * 128:(gc + 1) * 128], wsTs[:, gc, :], start=True, stop=True), reads=["vnb", "wsTs"], writes=[pk])
                            bias = bsbs[:, gc, :]
                        mx = mixs[gc % 2]
                        P.op("vector", lambda e, ps=ps, mx=mx, bias=bias, tn=tn: e.tensor_tensor(out=mx[:, 0:tn], in0=ps[:, 0:tn], in1=bias, op=ALU.add), reads=[pk, "bsb", "bsbs"], writes=[f"mix{gc % 2}"])
                        P.op("gpsimd", lambda e, mx=mx, gc=gc, c0=c0, tn=tn: e.tensor_tensor(out=umT[:, gc, c0:c0 + tn], in0=uT[:, gc, c0:c0 + tn], in1=mx[:, 0:tn], op=ALU.mult), reads=[f"mix{gc % 2}", f"uT.{gc}"], writes=[f"uT.{gc}"])
                for dmc in range(KC):
                    ps, pk = PS()
                    for gc in range(8):
                        P.op("tensor", lambda e, ps=ps, gc=gc, dmc=dmc, n=n: e.matmul(ps[:, 0:n], w_out[:, gc, dmc * 128:(dmc + 1) * 128], umT[:, gc, 0:n], start=(gc == 0), stop=(gc == 7)),
                             reads=[f"a_w_out.{gc}", f"uT.{gc}"], writes=[pk])
                    P.op("vector", lambda e, ps=ps, dmc=dmc, t0=t0, n=n: e.tensor_tensor(out=xT[:, dmc, t0:t0 + n], in0=xT[:, dmc, t0:t0 + n], in1=ps[:, 0:n], op=ALU.add),
                         reads=[pk] + tkeys("xT", t0, n), writes=tkeys("xT", t0, n))
            A.reset(m)
            P.barrier()

        def final_norm_out():
            m = A.mark()
            sq = [A.alloc([128, KC, 512], BF16) for _ in range(2)]
            rs = [A.alloc([128, 512]) for _ in range(2)]
            yT = [A.alloc([128, KC, 512]) for _ in range(2)]
            ytok = [A.alloc([128, D]) for _ in range(2)]
            cnt = 0
            for gi, (t0, n) in enumerate(GROUPS):
                s_ = sq[gi % 2]; r_ = rs[gi % 2]; y_ = yT[gi % 2]
                xk = tkeys("xT", t0, n)
                P.op("scalar", lambda e, s_=s_, t0=t0, n=n: e.activation(out=s_[:, :, 0:n], in_=xT[:, :, t0:t0 + n], func=AF.Square), reads=xk, writes=[f"sq{gi % 2}"])
                ps, pk = PS()
                for c in range(KC):
                    P.op("tensor", lambda e, ps=ps, s_=s_, c=c, n=n: e.matmul(ps[:, 0:n], onesb, s_[:, c, 0:n], start=(c == 0), stop=(c == KC - 1)), reads=[f"sq{gi % 2}", "onesb"], writes=[pk])
                P.op("scalar", lambda e, ps=ps, r_=r_, n=n: e.activation(out=r_[:, 0:n], in_=ps[:, 0:n], func=AF.Sqrt, scale=1.0 / D, bias=epsb[:, 0:1]), reads=[pk, "epsb"], writes=[f"rs{gi % 2}"])
                P.op("vector", lambda e, r_=r_, n=n: e.reciprocal(r_[:, 0:n], r_[:, 0:n]), reads=[f"rs{gi % 2}"], writes=[f"rs{gi % 2}"])
                for c in range(KC):
                    eng = "vector"
                    P.op(eng, lambda e, c=c, t0=t0, n=n, r_=r_, y_=y_: e.scalar_tensor_tensor(out=y_[:, c, 0:n], in0=xT[:, c, t0:t0 + n], scalar=gfin_t[:, c:c + 1], in1=r_[:, 0:n], op0=ALU.mult, op1=ALU.mult),
                         reads=xk + [f"rs{gi % 2}", "gains"], writes=[f"yT{gi % 2}"])
                for ti in range((n + 127) // 128):
                    tt = t0 // 128 + ti
                    tn = min(128, n)
                    yb = ytok[cnt % 2]
                    yk = f"ytok{cnt % 2}"
                    cnt += 1
                    for half in range(2):
                        ps, pk = PS()
                        for cc in range(4):
                            c = half * 4 + cc
                            P.op("tensor", lambda e, ps=ps, y_=y_, c=c, cc=cc, ti=ti, tn=tn: e.transpose(ps[0:tn, cc * 128:(cc + 1) * 128], y_[:, c, ti * 128:ti * 128 + tn], identf[:, :]),
                                 reads=[f"yT{gi % 2}", "identf"], writes=[pk])
                        copy_op(alt("vector", "scalar"), yb[0:tn, half * 512:(half + 1) * 512], ps[0:tn, :], [pk], [yk])
                    dst = y_p[tt * 128:(tt + 1) * 128, :] if tt < 16 else y_s[:, :]
                    P.op("sync", lambda e, dst=dst, yb=yb, tn=tn: e.dma_start(out=dst, in_=yb[0:tn, :]), reads=[yk], dma=True)
            A.reset(m)


        def layer_b(j, l):
            m = A.mark()
            S = Arena(arena_t, xn_end, base=xn_off)
            w_in = A.alloc([128, KC, BP], BF16)
            w_out = A.alloc([128, KC, D], BF16)
            proj = A.alloc([128, BP])
            Isc = A.alloc([128, SEQ + NS])
            Rr = A.alloc([128, 8, 512], BF16)
            junk = Rr.rearrange("p a b -> p (a b)")
            kT = S.alloc([128, 2, SEQ], BF16)
            vtok = S.alloc([128, 16, 256], BF16)
            kiT2 = S.alloc([128, SEQ], BF16)
            mask = S.alloc([128, SEQ + NS], BF16)
            maskT = S.alloc([128, 17, 128], BF16)
            xn_t = A.alloc([128, KC, 128], BF16)
            sq = A.alloc([128, KC, 128], BF16)
            rs = A.alloc([128, 128])
            qb = A.alloc([128, D], BF16)
            kb = A.alloc([128, 256], BF16)
            qib = A.alloc([128, 640], BF16)
            bis = A.alloc([128, 4 * (NITER + 2)])
            steps = bis[:, 0:NITER + 1]; mids = bis[:, NITER + 2:2 * NITER + 3]; cnts = bis[:, 2 * NITER + 4:3 * NITER + 4]
            incs = bis[:, 3 * NITER + 5:4 * NITER + 5]
            sc = A.alloc([128, 8])
            mB = A.mark()
            qT = A.alloc([128, 8, 128], BF16)
            qiT2 = A.alloc([128, 4, 128], BF16)
            Dw = A.alloc([128, 8, 128], BF16)
            Eb = [A.alloc([128, 512], BF16) for _ in range(2)]
            PTb = [A.alloc([128, 512], BF16) for _ in range(2)]
            rden = A.alloc([128, 512])
            oTn = A.alloc([128, 8, 128], BF16)
            for kc in range(KC):
                P.op(wload_eng, lambda e, kc=kc: e.dma_start(out=w_in[:, kc, :], in_=b_w_in[j, kc * 128:(kc + 1) * 128, :]), writes=[f"b_w_in.{kc}"], dma=True)
            for kc in range(KC):
                P.op(wload_eng, lambda e, kc=kc: e.dma_start(out=w_out[:, kc, :], in_=b_w_out[j, kc * 128:(kc + 1) * 128, :]), writes=[f"b_w_out.{kc}"], dma=True)
            P.op("vector", lambda e: e.memset(sc[:, 3:4], -1.0e29), writes=["thr0"])

            def norm_tile(t0, n):
                xk = tkeys("xT", t0, n)
                P.op("scalar", lambda e: e.activation(out=sq[:, :, 0:n], in_=xT[:, :, t0:t0 + n], func=AF.Square), reads=xk, writes=["sqb"])
                ps, pk = PS()
                for c in range(KC):
                    P.op("tensor", lambda e, ps=ps, c=c: e.matmul(ps[:, 0:n], onesb, sq[:, c, 0:n], start=(c == 0), stop=(c == KC - 1)), reads=["sqb", "onesb"], writes=[pk])
                P.op("scalar", lambda e, ps=ps: e.activation(out=rs[:, 0:n], in_=ps[:, 0:n], func=AF.Sqrt, scale=1.0 / D, bias=epsb[:, 0:1]), reads=[pk, "epsb"], writes=["rsb"])
                P.op("vector", lambda e: e.reciprocal(rs[:, 0:n], rs[:, 0:n]), reads=["rsb"], writes=["rsb"])
                for c in range(KC):
                    P.op("vector", lambda e, c=c: e.scalar_tensor_tensor(out=xn_t[:, c, 0:n], in0=xT[:, c, t0:t0 + n], scalar=gmix_t[:, l * 8 + c:l * 8 + c + 1], in1=rs[:, 0:n], op0=ALU.mult, op1=ALU.mult),
                         reads=xk + ["rsb", "gains"], writes=["xn_t"])

            def project(n):
                for c0 in range(0, BP, 512):
                    w = min(512, BP - c0)
                    ps, pk = PS()
                    for kc in range(KC):
                        P.op("tensor", lambda e, ps=ps, kc=kc, c0=c0, w=w: e.matmul(ps[0:n, 0:w], xn_t[:, kc, 0:n], w_in[:, kc, c0:c0 + w], start=(kc == 0), stop=(kc == KC - 1)),
                             reads=["xn_t", f"b_w_in.{kc}"], writes=[pk])
                    copy_op(alt("vector", "scalar"), proj[0:n, c0:c0 + w], ps[0:n, 0:w], [pk], ["proj"])

            PROJK = ["proj", "projr.0", "projr.1", "projr.2", "projr.3"]
            rts = {}
            for si, (H_, hf_) in enumerate(((8, 16), (2, 16), (8, 8), (1, 8))):
                rts[si] = [A.alloc([128, H_, hf_]) for _ in range(4)]

            def rope(n, tab):
                secs = []
                for si, (o, H, Dh, half, co, so, eng) in enumerate(((0, 8, 128, 16, 0, 16, "vector"), (1024, 2, 128, 16, 0, 16, "gpsimd"), (1536, 8, 64, 8, 32, 40, "gpsimd"), (2048, 1, 64, 8, 32, 40, "vector"))):
                    sec = proj[0:n, o:o + H * Dh].rearrange("p (h d) -> p h d", h=H)
                    x1 = sec[:, :, 0:half]; x2 = sec[:, :, half:2 * half]
                    cb = tab[:, co:co + half].unsqueeze(1).to_broadcast([n, H, half])
                    sb_ = tab[:, so:so + half].unsqueeze(1).to_broadcast([n, H, half])
                    t = [r[0:n, :, :] for r in rts[si]]
                    secs.append((si, eng, x1, x2, cb, sb_, t))
                for (si, eng, x1, x2, cb, sb_, t) in secs:
                    P.op(eng, lambda e, t=t, x1=x1, cb=cb: e.tensor_tensor(out=t[0], in0=x1, in1=cb, op=ALU.mult), reads=["proj", "rope"], writes=[f"rt{si}.0"])
                    P.op(eng, lambda e, t=t, x2=x2, sb_=sb_: e.tensor_tensor(out=t[1], in0=x2, in1=sb_, op=ALU.mult), reads=["proj", "rope"], writes=[f"rt{si}.1"])
                    P.op(eng, lambda e, t=t, x2=x2, cb=cb: e.tensor_tensor(out=t[2], in0=x2, in1=cb, op=ALU.mult), reads=["proj", "rope"], writes=[f"rt{si}.2"])
                    P.op(eng, lambda e, t=t, x1=x1, sb_=sb_: e.tensor_tensor(out=t[3], in0=x1, in1=sb_, op=ALU.mult), reads=["proj", "rope"], writes=[f"rt{si}.3"])
                for (si, eng, x1, x2, cb, sb_, t) in secs:
                    rk = [f"rt{si}.{i}" for i in range(4)]
                    P.op(eng, lambda e, t=t, x1=x1: e.tensor_tensor(out=x1, in0=t[0], in1=t[1], op=ALU.subtract), reads=rk, writes=[f"projr.{si}"])
                    P.op(eng, lambda e, t=t, x2=x2: e.tensor_tensor(out=x2, in0=t[2], in1=t[3], op=ALU.add), reads=rk, writes=[f"projr.{si}"])

            def bisect(np_, L, lo_cols, rkeys=("R",)):
                rkeys = list(rkeys)
                P.op("vector", lambda e: e.tensor_reduce(out=sc[0:np_, 0:1], in_=Isc[0:np_, 0:L], axis=AX.X, op=ALU.max), reads=["I"], writes=["sc"])
                P.op("vector", lambda e: e.tensor_reduce(out=sc[0:np_, 1:2], in_=Isc[0:np_, 0:lo_cols], axis=AX.X, op=ALU.min), reads=["I"], writes=["sc"])
                P.op("vector", lambda e: e.tensor_tensor(out=sc[0:np_, 2:3], in0=sc[0:np_, 0:1], in1=sc[0:np_, 1:2], op=ALU.subtract), reads=["sc"], writes=["sc"])
                P.op("vector", lambda e: e.tensor_scalar(steps[0:np_, :], pow2[0:np_, :], sc[0:np_, 2:3], None, ALU.mult), reads=["sc", "pow2"], writes=["steps"])
                P.op("vector", lambda e: e.tensor_tensor(out=mids[0:np_, 0:1], in0=sc[0:np_, 1:2], in1=steps[0:np_, 0:1], op=ALU.add), reads=["sc", "steps"], writes=["mids"])
                P.op("vector", lambda e: e.memset(cnts[0:np_, :], 0.0), writes=["cnts"])
                for k in range(NITER):
                    P.op("vector", lambda e, k=k: e.tensor_scalar(junk[0:np_, 0:L], Isc[0:np_, 0:L], mids[0:np_, k:k + 1], None, ALU.is_ge, ALU.add, accum_out=cnts[0:np_, k:k + 1]),
                         reads=["I", "mids", "cnts"], writes=rkeys + ["cnts"])
                    P.op("vector", lambda e, k=k: e.tensor_scalar(incs[0:np_, k:k + 1], cnts[0:np_, k:k + 1], float(TOPK), steps[0:np_, k:k + 1], ALU.is_ge, ALU.mult), reads=["cnts", "steps"], writes=["incs"])
                    P.op("vector", lambda e, k=k: e.scalar_tensor_tensor(out=mids[0:np_, k + 1:k + 2], in0=mids[0:np_, k:k + 1], scalar=steps[0:np_, k + 1:k + 2], in1=incs[0:np_, k:k + 1], op0=ALU.subtract, op1=ALU.add),
                         reads=["mids", "steps", "incs"], writes=["mids"])

            ACC0, ACC1 = psf[4], psf[5]
            qT2 = A.alloc([128, 8, 128], BF16)
            qTs = [qT, qT2]
            RKP = [f"R.{h}" for h in range(8)]

            def front(qt):
                t0 = qt * 128
                L = (qt + 1) * 128
                qTc = qTs[qt % 2]; qk = f"qT{qt % 2}"
                norm_tile(t0, 128)
                project(128)
                rope(128, ropeP[:, qt, :])
                P.op("sync", lambda e: e.dma_start(out=nk_p[j, t0:t0 + 128, :], in_=proj[:, 1024:1280]), reads=PROJK, dma=True)
                P.op("sync", lambda e: e.dma_start(out=nv_p[j, t0:t0 + 128, :], in_=proj[:, 1280:1536]), reads=PROJK, dma=True)
                P.op("sync", lambda e: e.dma_start(out=nki_p[j, t0:t0 + 128, :], in_=proj[:, 2048:2112]), reads=PROJK, dma=True)
                P.op("scalar", lambda e: e.activation(out=qb, in_=proj[:, 0:1024], func=AF.Copy), reads=PROJK, writes=["qb"])
                P.op("gpsimd", lambda e: e.tensor_copy(kb, proj[:, 1024:1280]), reads=PROJK, writes=["kb"])
                P.op("gpsimd", lambda e: e.tensor_copy(vtok[:, qt, :], proj[:, 1280:1536]), reads=PROJK, writes=[f"vtok.{qt}"])
                P.op("scalar", lambda e: e.activation(out=qib[:, 0:576], in_=proj[:, 1536:2112], func=AF.Copy), reads=PROJK, writes=["qib"])
                P.op("gpsimd", lambda e: e.tensor_copy(qib[:, 576:640], proj[:, 2048:2112]), reads=PROJK, writes=["qib"])
                for h in range(8):
                    P.op("gpsimd", lambda e, h=h: e.tensor_scalar(Dw[:, h, :], identb, proj[:, 2112 + h:2113 + h], None, ALU.mult), reads=PROJK + ["identb"], writes=["Dw"])
                pb, pbk = PSB()
                for h in range(8):
                    P.op("tensor", lambda e, pb=pb, h=h: e.transpose(pb[:, h * 128:(h + 1) * 128], qb[:, h * 128:(h + 1) * 128], identb), reads=["qb", "identb"], writes=[pbk])
                copy_op("vector", qTc.rearrange("p a b -> p (a b)"), pb[:, :], [pbk], [qk])
                pb, pbk = PSB()
                for g in range(2):
                    P.op("tensor", lambda e, pb=pb, g=g: e.transpose(pb[:, g * 128:(g + 1) * 128], kb[:, g * 128:(g + 1) * 128], identb), reads=["kb", "identb"], writes=[pbk])
                for pr in range(5):
                    P.op("tensor", lambda e, pb=pb, pr=pr: e.transpose(pb[:, (2 + pr) * 128:(3 + pr) * 128], qib[:, pr * 128:(pr + 1) * 128], identb), reads=["qib", "identb"], writes=[pbk])
                copy_op("vector", kT[:, :, t0:t0 + 128], pb[:, 0:256].rearrange("p (a b) -> p a b", a=2), [pbk], [f"kT.{qt}"])
                copy_op("vector", qiT2.rearrange("p a b -> p (a b)"), pb[:, 256:768], [pbk], ["qiT2"])
                copy_op("vector", kiT2[:, t0:t0 + 128], pb[:, 768:896], [pbk], [f"kiT2.{qt}"])
                for s0 in range(0, L, 512):
                    w = min(512, L - s0)
                    kk = [f"kiT2.{q}" for q in range(s0 // 128, (s0 + w) // 128)]
                    for h in range(8):
                        pr, hh = h // 2, h % 2
                        ps, pk = PS()
                        P.op("tensor", lambda e, ps=ps, pr=pr, hh=hh, s0=s0, w=w: e.matmul(ps[:, 0:w], qiT2[hh * 64:(hh + 1) * 64, pr, :], kiT2[hh * 64:(hh + 1) * 64, s0:s0 + w], start=True, stop=True),
                             reads=["qiT2"] + kk, writes=[pk])
                        if h % 4 != 3:
                            P.op("scalar", lambda e, ps=ps, h=h, w=w: e.activation(out=Rr[:, h, 0:w], in_=ps[:, 0:w], func=AF.Relu), reads=[pk], writes=[RKP[h]])
                        else:
                            P.op("vector", lambda e, ps=ps, h=h, w=w: e.tensor_scalar_max(Rr[:, h, 0:w], ps[:, 0:w], 0.0), reads=[pk], writes=[RKP[h]])
                    ps, pk = PS()
                    for h in range(8):
                        P.op("tensor", lambda e, ps=ps, h=h, w=w: e.matmul(ps[:, 0:w], Dw[:, h, :], Rr[:, h, 0:w], start=(h == 0), stop=(h == 7)), reads=["Dw", RKP[h]], writes=[pk])
                    copy_op("scalar", Isc[:, s0:s0 + w], ps[:, 0:w], [pk], ["I"])
                P.op("gpsimd", lambda e: e.tensor_tensor(out=Isc[:, t0:t0 + 128], in0=Isc[:, t0:t0 + 128], in1=tribias, op=ALU.add), reads=["I", "tribias"], writes=["I"])

            def bis(qt):
                if qt >= 2:
                    bisect(128, (qt + 1) * 128, qt * 128, rkeys=RKP)

            def msk_a(qt):
                L = (qt + 1) * 128
                thr = mids[:, NITER:NITER + 1] if qt >= 2 else sc[:, 3:4]
                P.op("vector", lambda e: e.tensor_scalar(mask[:, 0:L], Isc[:, 0:L], thr, None, ALU.is_ge), reads=["I", "mids", "thr0"], writes=["mask"])

            def msk_b(qt):
                for k0 in range(0, qt + 1, 8):
                    nk = min(8, qt + 1 - k0)
                    pb, pbk = PSB()
                    for kt in range(k0, k0 + nk):
                        P.op("tensor", lambda e, pb=pb, kt=kt, k0=k0: e.transpose(pb[:, (kt - k0) * 128:(kt - k0 + 1) * 128], mask[:, kt * 128:(kt + 1) * 128], identb), reads=["mask", "identb"], writes=[pbk])
                    copy_op("vector", maskT[:, k0:k0 + nk, :].rearrange("p a b -> p (a b)"), pb[:, 0:nk * 128], [pbk], ["maskT"])

            def back(qt):
                t0 = qt * 128
                qTc = qTs[qt % 2]; qk = f"qT{qt % 2}"
                items = [(g, kt) for g in range(2) for kt in range(qt + 1)]

                def s_mm(i):
                    g, kt = items[i]
                    ps, pk = PS()
                    P.op("tensor", lambda e, ps=ps, g=g, kt=kt: e.matmul(ps[:, 0:512], kT[:, g, kt * 128:(kt + 1) * 128], qTc[:, g * 4:(g + 1) * 4, :].rearrange("p a b -> p (a b)"), start=True, stop=True),
                         reads=[qk, f"kT.{kt}"], writes=[pk])
                    return ps, pk

                nxt = s_mm(0)
                for i, (g, kt) in enumerate(items):
                    ps, pk = nxt
                    if i + 1 < len(items):
                        nxt = s_mm(i + 1)
                    E = Eb[i % 2]; PT = PTb[i % 2]
                    P.op("scalar", lambda e, ps=ps, E=E: e.activation(out=E, in_=ps[:, 0:512], func=AF.Exp, scale=ATTN_SCALE), reads=[pk], writes=[f"E{i % 2}"])
                    P.op("gpsimd", lambda e, E=E, PT=PT, kt=kt: e.tensor_tensor(out=PT.rearrange("p (a b) -> p a b", a=4), in0=E.rearrange("p (a b) -> p a b", a=4), in1=maskT[:, kt, :].unsqueeze(1).to_broadcast([128, 4, 128]), op=ALU.mult),
                         reads=[f"E{i % 2}", "maskT"], writes=[f"PT{i % 2}"])
                    P.op("tensor", lambda e, g=g, kt=kt, PT=PT: e.matmul(ACC0[:, 0:512], vtok[:, kt, g * 128:(g + 1) * 128], PT, start=(kt == 0), stop=(kt == qt)), reads=[f"PT{i % 2}", f"vtok.{kt}"], writes=["acc0"])
                    P.op("tensor", lambda e, kt=kt, PT=PT: e.matmul(ACC1[:, 0:512], onesb, PT, start=(kt == 0), stop=(kt == qt)), reads=[f"PT{i % 2}", "onesb"], writes=["acc1"])
                    if kt == qt:
                        P.op("vector", lambda e: e.reciprocal(rden, ACC1[:, 0:512]), reads=["acc1"], writes=["rden"])
                        P.op("vector", lambda e, g=g: e.tensor_tensor(out=oTn[:, g * 4:(g + 1) * 4, :].rearrange("p a b -> p (a b)"), in0=ACC0[:, 0:512], in1=rden, op=ALU.mult), reads=["acc0", "rden"], writes=["oTn"])
                for dmc in range(KC):
                    ps, pk = PS()
                    for h in range(8):
                        P.op("tensor", lambda e, ps=ps, h=h, dmc=dmc: e.matmul(ps[:, 0:128], w_out[:, h, dmc * 128:(dmc + 1) * 128], oTn[:, h, :], start=(h == 0), stop=(h == 7)), reads=[f"b_w_out.{h}", "oTn"], writes=[pk])
                    P.op("vector", lambda e, ps=ps, dmc=dmc: e.tensor_tensor(out=xT[:, dmc, t0:t0 + 128], in0=xT[:, dmc, t0:t0 + 128], in1=ps[:, 0:128], op=ALU.add), reads=[pk, f"xT.{qt}"], writes=[f"xT.{qt}"])

            NQT = DEBUG.get('nqt', 16)
            if DEBUG.get("nopipe"):
                for qt in range(NQT):
                    front(qt); bis(qt); msk_a(qt); msk_b(qt); back(qt)
            else:
                for qt in range(NQT):
                    front(qt)
                    if qt > 0:
                        msk_b(qt - 1)
                    bis(qt)
                    if qt > 0:
                        back(qt - 1)
                    msk_a(qt)
                msk_b(NQT - 1)
                back(NQT - 1)
            if do_samp:
                P.barrier()
                A.reset(mB)
                S2 = Arena(arena_t, xn_end, base=xn_off)
                k_b = S2.alloc([128, 16, 256], BF16)
                v_b = S2.alloc([128, 16, 256], BF16)
                kT_b = S2.alloc([128, 2, 16, 128], BF16)
                kiT_b = S2.alloc([64, 16, 128], BF16)
                kg = [S2.alloc([128, 1024], BF16) for _ in range(2)]
                mask_s = A.alloc([64, SEQ + NS], BF16)
                maskT_all = A.alloc([128, 16, 64], BF16)
                maskT_new = A.alloc([64, 64], BF16)
                qT_s = A.alloc([128, 2, 64, 4], BF16)
                kT_new = A.alloc([128, 2, 64], BF16)
                qiT_s = A.alloc([64, 8, 64], BF16)
                kiT_new = A.alloc([64, 64], BF16)
                Dw_s = A.alloc([64, 8, 64], BF16)
                vb_new = A.alloc([64, 256], BF16)
                E_s = [A.alloc([128, 256], BF16) for _ in range(2)]
                PT_s = [A.alloc([128, 256], BF16) for _ in range(2)]
                rden_s = A.alloc([128, 256])
                oT_s = A.alloc([128, 8, 64], BF16)
                idr = A.alloc([128, 16], I32)
                idx = A.alloc([128, 16], I32)
                rr2 = {"k": 0}

                def PS2():
                    k = 4 + rr2["k"]
                    rr2["k"] = (rr2["k"] + 1) % 2
                    return psf[k], f"psf{k}"

                RK = [f"R.{h}" for h in range(8)]
                norm_tile(SEQ, NS)
                project(NS)
                rope(NS, ropeS[0:NS, :])
                P.op("sync", lambda e: e.dma_start(out=nk_s[j, :, :], in_=proj[0:NS, 1024:1280]), reads=PROJK, dma=True)
                P.op("sync", lambda e: e.dma_start(out=nv_s[j, :, :], in_=proj[0:NS, 1280:1536]), reads=PROJK, dma=True)
                P.op("sync", lambda e: e.dma_start(out=nki_s[j, :, :], in_=proj[0:NS, 2048:2112]), reads=PROJK, dma=True)
                P.op("scalar", lambda e: e.activation(out=qb[0:NS, :], in_=proj[0:NS, 0:1024], func=AF.Copy), reads=PROJK, writes=["qb"])
                P.op("gpsimd", lambda e: e.tensor_copy(kb[0:NS, :], proj[0:NS, 1024:1280]), reads=PROJK, writes=["kb"])
                P.op("gpsimd", lambda e: e.tensor_copy(vb_new[0:NS, :], proj[0:NS, 1280:1536]), reads=PROJK, writes=["vb_new"])
                P.op("vector", lambda e: e.tensor_copy(qib[0:NS, 0:576], proj[0:NS, 1536:2112]), reads=PROJK, writes=["qib"])
                for h in range(8):
                    P.op("gpsimd", lambda e, h=h: e.tensor_scalar(Dw_s[:, h, :], identb[0:NS, 0:NS], proj[0:NS, 2112 + h:2113 + h], None, ALU.mult), reads=PROJK + ["identb"], writes=["Dw_s"])
                pb, pbk = PSB()
                for h in range(8):
                    P.op("tensor", lambda e, pb=pb, h=h: e.transpose(pb[:, h * 64:(h + 1) * 64], qb[0:NS, h * 128:(h + 1) * 128], identb[0:NS, 0:NS]), reads=["qb", "identb"], writes=[pbk])
                for g in range(2):
                    P.op("vector", lambda e, pb=pb, g=g: e.tensor_copy(qT_s[:, g, :, :].rearrange("p t h -> p h t"), pb[:, g * 256:(g + 1) * 256].rearrange("p (h t) -> p h t", h=4)), reads=[pbk], writes=["qT_s"])
                pb, pbk = PSB()
                for g in range(2):
                    P.op("tensor", lambda e, pb=pb, g=g: e.transpose(pb[:, g * 64:(g + 1) * 64], kb[0:NS, g * 128:(g + 1) * 128], identb[0:NS, 0:NS]), reads=["kb", "identb"], writes=[pbk])
                for h in range(9):
                    P.op("tensor", lambda e, pb=pb, h=h: e.transpose(pb[0:64, 128 + h * 64:128 + (h + 1) * 64], qib[0:NS, h * 64:(h + 1) * 64], identb[0:NS, 0:NS]), reads=["qib", "identb"], writes=[pbk])
                P.op("vector", lambda e, pb=pb: e.tensor_copy(kT_new.rearrange("p a b -> p (a b)"), pb[:, 0:128]), reads=[pbk], writes=["kT_new"])
                P.op("vector", lambda e, pb=pb: e.tensor_copy(qiT_s.rearrange("p a b -> p (a b)"), pb[0:64, 128:640]), reads=[pbk], writes=["qiT_s"])
                P.op("vector", lambda e, pb=pb: e.tensor_copy(kiT_new, pb[0:64, 640:704]), reads=[pbk], writes=["kiT_new"])
                for jj in range(16):
                    src = bass.AP(pt.tensor, jj, [[0, 8], [16, 16]])
                    P.op("sync", lambda e, src=src, jj=jj: e.dma_start(out=idr[jj * 8:(jj + 1) * 8, :], in_=src, allow_slow_non_contiguous=True), writes=["idr"], dma=True)
                P.op("vector", lambda e: e.tensor_scalar(idx, idr, 8, q8[:, 0:1], ALU.mult, ALU.add), reads=["idr", "q8"], writes=["idx"])
                if USE_AG:
                    r0 = 0
                    cki_src = pool_full["cki", 0, j]; ck_src = [pool_full["ck", hf, j] for hf in range(2)]; cv_src = [pool_full["cv", hf, j] for hf in range(2)]
                else:
                    r0 = j * NPOOL * 8
                    cki_src = cki; ck_src = [ck, ck]; cv_src = [cv, cv]
                IACC = [(psf[i], f"psf{i}") for i in range(4)]
                for b in range(NB):
                    g_ = kg[b % 2]; gk = f"kg{b % 2}"
                    P.op("gpsimd", lambda e, g_=g_, b=b: e.indirect_dma_start(out=g_, out_offset=None, in_=cki_src, in_offset=bass.IndirectOffsetOnAxis(ap=idx[:, b:b + 1], axis=0), element_offset=r0 * 1024), reads=["idx", f"G:cki0_full{j}"], writes=[gk], dma=True)
                    for r8 in range(2):
                        pb, pbk = PSB()
                        for rr_ in range(8):
                            r = r8 * 8 + rr_
                            P.op("tensor", lambda e, pb=pb, g_=g_, r=r, rr_=rr_: e.transpose(pb[0:64, rr_ * 128:(rr_ + 1) * 128], g_[:, r * 64:(r + 1) * 64], identb), reads=[gk, "identb"], writes=[pbk])
                        P.op("vector", lambda e, pb=pb, r8=r8: e.tensor_copy(kiT_b[:, r8 * 8:(r8 + 1) * 8, :].rearrange("p a b -> p (a b)"), pb[0:64, :]), reads=[pbk], writes=[f"kiT_b.{r8}"])
                    for blk in range(4):
                        for h in range(8):
                            ps, pk = PS2()
                            P.op("tensor", lambda e, ps=ps, h=h, blk=blk: e.matmul(ps[0:64, 0:512], qiT_s[:, h, :], kiT_b[:, blk * 4:(blk + 1) * 4, :].rearrange("p a b -> p (a b)"), start=True, stop=True),
                                 reads=["qiT_s", f"kiT_b.{blk // 2}"], writes=[pk])
                            if h % 2 == 0:
                                P.op("scalar", lambda e, ps=ps, h=h, b=b: e.activation(out=Rr[0:64, h, :], in_=ps[0:64, 0:512], func=AF.Relu, scale=rowmask[:, b:b + 1]), reads=[pk, "rowmask"], writes=[RK[h]])
                            else:
                                P.op("vector", lambda e, ps=ps, h=h, b=b: e.tensor_scalar(Rr[0:64, h, :], ps[0:64, 0:512], rowmask[:, b:b + 1], 0.0, ALU.mult, ALU.max), reads=[pk, "rowmask"], writes=[RK[h]])
                        ia, iak = IACC[blk]
                        for h in range(8):
                            P.op("tensor", lambda e, ia=ia, h=h, b=b: e.matmul(ia[0:64, 0:512], Dw_s[:, h, :], Rr[0:64, h, :], start=(b == 0 and h == 0), stop=(b == NB - 1 and h == 7)), reads=["Dw_s", RK[h]], writes=[iak])
                ps, pk = PS2()
                for h in range(8):
                    P.op("tensor", lambda e, ps=ps, h=h: e.matmul(ps[0:64, h * 64:(h + 1) * 64], qiT_s[:, h, :], kiT_new, start=True, stop=True), reads=["qiT_s", "kiT_new"], writes=[pk])
                P.op("scalar", lambda e, ps=ps: e.activation(out=Rr[0:64, :, 0:64], in_=ps[0:64, 0:512].rearrange("p (h s) -> p h s", h=8), func=AF.Relu), reads=[pk], writes=RK)
                ps2, pk2 = PS2()
                for h in range(8):
                    P.op("tensor", lambda e, ps2=ps2, h=h: e.matmul(ps2[0:64, 0:64], Dw_s[:, h, :], Rr[0:64, h, 0:64], start=(h == 0), stop=(h == 7)), reads=["Dw_s"] + RK, writes=[pk2])
                P.op("vector", lambda e, ps2=ps2: e.tensor_tensor(out=Isc[0:64, SEQ:SEQ + NS], in0=ps2[0:64, 0:64], in1=blkbias, op=ALU.add), reads=[pk2, "blkbias"], writes=["I"])
                for blk in range(4):
                    ia, iak = IACC[blk]
                    copy_op("scalar" if blk % 2 == 0 else "vector", Isc[0:64, blk * 512:(blk + 1) * 512], ia[0:64, 0:512], [iak], ["I"])
                bisect(64, SEQ + NS, SEQ, rkeys=RK)
                P.op("vector", lambda e: e.tensor_scalar(mask_s[:, :], Isc[0:64, :], mids[0:64, NITER:NITER + 1], None, ALU.is_ge), reads=["I", "mids"], writes=["mask_s"])
                for r8 in range(2):
                    pb, pbk = PSB()
                    for rr_ in range(8):
                        r = r8 * 8 + rr_
                        P.op("tensor", lambda e, pb=pb, r=r, rr_=rr_: e.transpose(pb[:, rr_ * 64:(rr_ + 1) * 64], mask_s[:, r * 128:(r + 1) * 128], identb[0:64, 0:64]), reads=["mask_s", "identb"], writes=[pbk])
                    P.op("vector", lambda e, pb=pb, r8=r8: e.tensor_copy(maskT_all[:, r8 * 8:(r8 + 1) * 8, :].rearrange("p a b -> p (a b)"), pb[:, 0:512]), reads=[pbk], writes=["maskT_all"])
                pb, pbk = PSB()
                P.op("tensor", lambda e, pb=pb: e.transpose(pb[0:64, 0:64], mask_s[:, SEQ:SEQ + NS], identb[0:64, 0:64]), reads=["mask_s", "identb"], writes=[pbk])
                P.op("vector", lambda e, pb=pb: e.tensor_copy(maskT_new, pb[0:64, 0:64]), reads=[pbk], writes=["maskT_new"])
                OACC = [(psf[0], "psf0"), (psf[2], "psf2")]
                DACC = [(psf[1], "psf1"), (psf[3], "psf3")]
                ecnt = 0
                for g in range(2):
                    ps, pk = PS2()
                    P.op("tensor", lambda e, ps=ps, g=g: e.matmul(ps[0:64, 0:256], kT_new[:, g, :], qT_s[:, g, :, :].rearrange("p t h -> p (t h)"), start=True, stop=True), reads=["kT_new", "qT_s"], writes=[pk])
                    E = E_s[ecnt % 2]; PT = PT_s[ecnt % 2]; ek = f"E_s{ecnt % 2}"; ptk = f"PT_s{ecnt % 2}"
                    ecnt += 1
                    P.op("scalar", lambda e, ps=ps, E=E: e.activation(out=E[0:64, :], in_=ps[0:64, 0:256], func=AF.Exp, scale=ATTN_SCALE), reads=[pk], writes=[ek])
                    P.op("vector", lambda e, E=E, PT=PT: e.tensor_tensor(out=PT[0:64, :].rearrange("p (t h) -> p t h", h=4), in0=E[0:64, :].rearrange("p (t h) -> p t h", h=4), in1=maskT_new.unsqueeze(2).to_broadcast([64, 64, 4]), op=ALU.mult),
                         reads=[ek, "maskT_new"], writes=[ptk])
                    oa, oak = OACC[g]; da, dak = DACC[g]
                    P.op("tensor", lambda e, oa=oa, g=g, PT=PT: e.matmul(oa[:, 0:256], vb_new[:, g * 128:(g + 1) * 128], PT[0:64, :], start=True, stop=False), reads=["vb_new", ptk], writes=[oak])
                    P.op("tensor", lambda e, da=da, PT=PT: e.matmul(da[:, 0:256], onesb[0:64, :], PT[0:64, :], start=True, stop=False), reads=["onesb", ptk], writes=[dak])
                for b in range(NB):
                    for hf in range(2):
                        P.op("gpsimd", lambda e, b=b, hf=hf: e.indirect_dma_start(out=k_b[:, hf * 8:(hf + 1) * 8, :].rearrange("p a b -> p (a b)"), out_offset=None, in_=ck_src[hf], in_offset=bass.IndirectOffsetOnAxis(ap=idx[:, b:b + 1], axis=0), element_offset=(0 if USE_AG else r0 * 4096 + hf * 2048)),
                             reads=["idx", f"G:ck{hf}_full{j}"], writes=[f"k_b.{hf}"], dma=True)
                    for hf in range(2):
                        P.op("gpsimd", lambda e, b=b, hf=hf: e.indirect_dma_start(out=v_b[:, hf * 8:(hf + 1) * 8, :].rearrange("p a b -> p (a b)"), out_offset=None, in_=cv_src[hf], in_offset=bass.IndirectOffsetOnAxis(ap=idx[:, b:b + 1], axis=0), element_offset=(0 if USE_AG else r0 * 4096 + hf * 2048)),
                             reads=["idx", f"G:cv{hf}_full{j}"], writes=[f"v_b.{hf}"], dma=True)
                    for g in range(2):
                        for r8 in range(2):
                            pb, pbk = PSB()
                            for rr_ in range(8):
                                r = r8 * 8 + rr_
                                P.op("tensor", lambda e, pb=pb, g=g, r=r, rr_=rr_: e.transpose(pb[:, rr_ * 128:(rr_ + 1) * 128], k_b[:, r, g * 128:(g + 1) * 128], identb), reads=[f"k_b.{r8}", "identb"], writes=[pbk])
                            P.op("vector", lambda e, pb=pb, g=g, r8=r8: e.tensor_copy(kT_b[:, g, r8 * 8:(r8 + 1) * 8, :].rearrange("p a b -> p (a b)"), pb[:, :]), reads=[pbk], writes=[f"kT_b.{g}.{r8}"])
                    for g in range(2):
                        ps, pk = PS2()
                        for r in range(16):
                            P.op("tensor", lambda e, ps=ps, g=g, r=r, b=b: e.matmul(ps[:, r * 16:(r + 1) * 16], kT_b[:, g, r, :], qT_s[:, g, b * 4:(b + 1) * 4, :].rearrange("p t h -> p (t h)"), start=True, stop=True),
                                 reads=[f"kT_b.{g}.{r // 8}", "qT_s"], writes=[pk])
                        E = E_s[ecnt % 2]; PT = PT_s[ecnt % 2]; ek = f"E_s{ecnt % 2}"; ptk = f"PT_s{ecnt % 2}"
                        ecnt += 1
                        P.op("scalar", lambda e, ps=ps, E=E: e.activation(out=E, in_=ps[:, 0:256], func=AF.Exp, scale=ATTN_SCALE), reads=[pk], writes=[ek])
                        P.op("vector", lambda e, E=E, PT=PT, b=b: e.tensor_tensor(out=PT.rearrange("p (r t h) -> p r t h", r=16, t=4), in0=E.rearrange("p (r t h) -> p r t h", r=16, t=4), in1=maskT_all[:, :, b * 4:(b + 1) * 4].unsqueeze(3).to_broadcast([128, 16, 4, 4]), op=ALU.mult),
                             reads=[ek, "maskT_all"], writes=[ptk])
                        oa, oak = OACC[g]; da, dak = DACC[g]
                        for r in range(16):
                            last = (b == NB - 1 and r == 15)
                            P.op("tensor", lambda e, oa=oa, g=g, r=r, b=b, PT=PT, last=last: e.matmul(oa[:, b * 16:(b + 1) * 16], v_b[:, r, g * 128:(g + 1) * 128], PT[:, r * 16:(r + 1) * 16], start=False, stop=last), reads=[f"v_b.{r // 8}", ptk], writes=[oak])
                            P.op("tensor", lambda e, da=da, r=r, b=b, PT=PT, last=last: e.matmul(da[:, b * 16:(b + 1) * 16], onesb, PT[:, r * 16:(r + 1) * 16], start=False, stop=last), reads=["onesb", ptk], writes=[dak])
                for g in range(2):
                    oa, oak = OACC[g]; da, dak = DACC[g]
                    P.op("vector", lambda e, da=da: e.reciprocal(rden_s, da[:, 0:256]), reads=[dak], writes=["rden_s"])
                    P.op("vector", lambda e, oa=oa, g=g: e.tensor_tensor(out=oT_s[:, g * 4:(g + 1) * 4, :], in0=oa[:, 0:256].rearrange("p (t h) -> p h t", h=4), in1=rden_s.rearrange("p (t h) -> p h t", h=4), op=ALU.mult), reads=[oak, "rden_s"], writes=["oT_s"])
                for dmc in range(KC):
                    ps, pk = PS2()
                    for h in range(8):
                        P.op("tensor", lambda e, ps=ps, h=h, dmc=dmc: e.matmul(ps[:, 0:NS], w_out[:, h, dmc * 128:(dmc + 1) * 128], oT_s[:, h, :], start=(h == 0), stop=(h == 7)), reads=[f"b_w_out.{h}", "oT_s"], writes=[pk])
                    P.op("vector", lambda e, ps=ps, dmc=dmc: e.tensor_tensor(out=xT[:, dmc, SEQ:SEQ + NS], in0=xT[:, dmc, SEQ:SEQ + NS], in1=ps[:, 0:NS], op=ALU.add), reads=[pk, "xT.16"], writes=["xT.16"])
            A.reset(m)
            P.barrier()

        for l in range(n_layers):
            j = l // 2
            if l % 2 == 0:
                if not DEBUG.get("skip_a"):
                    layer_a(j, l)
                if l == 0:
                    pool_allgather()
            else:
                if do_b:
                    layer_b(j, l)
            if not DEBUG.get("skip_ffn"):
                ffn(l)
        final_norm_out()
        print("arena hi words", A.hi, "ops", len(P.ops))
        P.finalize(st)
        with nc.Block() as block:
            P.emit(block)
    return nc


def host_consts():
    c = {}
    c["c_ident"] = np.eye(128, dtype=np.float32)
    s = np.arange(128)[:, None]; t = np.arange(128)[None, :]
    c["c_tri01"] = (s <= t).astype(np.float32)
    c["c_tribias"] = np.where(t <= s, 0.0, NEG).astype(np.float32)
    b = np.arange(64) // 4; o = np.arange(64) % 4
    same = b[:, None] == b[None, :]
    c["c_blk01"] = (same & (o[:, None] <= o[None, :])).astype(np.float32)
    c["c_blkbias"] = np.where(same & (o[None, :] <= o[:, None]), 0.0, NEG).astype(np.float32)
    def tab(pos):
        inv16 = 500000.0 ** (-np.arange(16, dtype=np.float32) * 2.0 / 32)
        inv8 = 500000.0 ** (-np.arange(8, dtype=np.float32) * 2.0 / 16)
        a16 = pos.astype(np.float32)[:, None] * inv16[None, :]
        a8 = pos.astype(np.float32)[:, None] * inv8[None, :]
        return np.concatenate([np.cos(a16), np.sin(a16), np.cos(a8), np.sin(a8)], axis=1).astype(np.float32)
    tp = tab(np.arange(SEQ))
    c["c_ropeP"] = np.ascontiguousarray(tp.reshape(16, 128, 48).transpose(1, 0, 2).reshape(128, 16 * 48))
    c["c_ropeS"] = tab(2048 + (np.arange(64) % 4))
    c["c_q8"] = (np.arange(128) % 8).astype(np.int32).reshape(128, 1)
    c["c_rowmask"] = (np.arange(64)[:, None] // 4 == np.arange(16)[None, :]).astype(np.float32)
    p2 = (0.5 ** (np.arange(NITER + 1) + 1)).astype(np.float32)
    p2[NITER] = p2[NITER - 1]
    c["c_pow2"] = np.tile(p2[None, :], (128, 1))
    return c


_NC_CACHE = {}


def kernel(x_prompt, x_sample, cache_k, cache_v, cache_kidx, page_table, norm_mix, norm_ffn,
           a_w_in, a_v_gain, a_w_s, a_b_s, a_w_out, b_w_in, b_w_out, ffn_w1, ffn_w2, norm_final,
           _n_layers=4, _do_b=True, _do_samp=True):
    f = lambda a: np.ascontiguousarray(np.asarray(a, dtype=np.float32))
    x_prompt = f(x_prompt); x_sample = f(x_sample)
    consts = host_consts()
    shared = dict(consts)
    gl = lambda g: np.ascontiguousarray(f(g).reshape(4, 8, 128).transpose(2, 0, 1).reshape(128, 32))
    shared["gmix"] = gl(norm_mix); shared["gffn"] = gl(norm_ffn)
    shared["gfin"] = np.ascontiguousarray(f(norm_final).reshape(8, 128).T)
    shared["a_w_in"] = f(a_w_in); shared["a_v_gain"] = f(a_v_gain); shared["a_w_out"] = f(a_w_out)
    ws = f(a_w_s)
    shared["a_wsT"] = np.ascontiguousarray(ws.transpose(0, 3, 1, 2).reshape(2, 128, 8 * 128))
    w4 = ws[:, :, :4, :4]
    blk = np.zeros((2, 64, 8, 64), np.float32)
    for b in range(16):
        blk[:, b * 4:(b + 1) * 4, :, b * 4:(b + 1) * 4] = w4.transpose(0, 3, 1, 2)
    shared["a_wsTs"] = blk.reshape(2, 64, 8 * 64)
    bs = f(a_b_s)
    shared["a_bs"] = np.ascontiguousarray(bs.reshape(2, 8 * 128))
    shared["a_bss"] = np.ascontiguousarray(np.tile(bs[:, :, :4], (1, 1, 16)).reshape(2, 8 * 64))
    shared["b_w_in"] = f(b_w_in); shared["b_w_out"] = f(b_w_out)
    shared["ffn_w1"] = f(ffn_w1); shared["ffn_w2"] = f(ffn_w2)
    percore = [dict() for _ in range(8)]
    if _do_b and _do_samp:
        ckv = f(cache_k).reshape(2, NPOOL * 8, 16 * 256); cvv = f(cache_v).reshape(2, NPOOL * 8, 16 * 256); ckiv = f(cache_kidx).reshape(2, NPOOL * 8, 16 * 64)
        if USE_AG:
            SHR = NPOOL * 8 // 8
            for c in range(8):
                percore[c]["ck"] = np.ascontiguousarray(ckv[:, c * SHR:(c + 1) * SHR]).reshape(2 * SHR, 16 * 256)
                percore[c]["cv"] = np.ascontiguousarray(cvv[:, c * SHR:(c + 1) * SHR]).reshape(2 * SHR, 16 * 256)
                percore[c]["cki"] = np.ascontiguousarray(ckiv[:, c * SHR:(c + 1) * SHR]).reshape(2 * SHR, 16 * 64)
        else:
            shared["ck"] = ckv.reshape(2 * NPOOL * 8, 16 * 256)
            shared["cv"] = cvv.reshape(2 * NPOOL * 8, 16 * 256)
            shared["cki"] = ckiv.reshape(2 * NPOOL * 8, 16 * 64)
    ptab = np.asarray(page_table).astype(np.int32)
    in_maps = []
    for c in range(8):
        m = dict(shared)
        m.update(percore[c])
        m["xp"] = x_prompt[c]
        m["xs"] = np.ascontiguousarray(x_sample[c * NB:(c + 1) * NB].reshape(NS, D))
        m["pt"] = np.ascontiguousarray(ptab[c * NB:(c + 1) * NB])
        in_maps.append(m)
    key = (_n_layers, _do_b, _do_samp)
    if key not in _NC_CACHE:
        _NC_CACHE[key] = build_nc(_n_layers, _do_b, _do_samp)
    nc = _NC_CACHE[key]
    res = run_bass_kernel_spmd(nc, in_maps, core_ids=list(range(8)))
    R = res.results
    cat = lambda name: np.stack([R[c][name] for c in range(8)])
    y_prompt = cat("y_p")
    y_sample = cat("y_s").reshape(128, 4, D)
    nk_p = cat("nk_p").transpose(1, 0, 2, 3).reshape(2, 8, SEQ, 2, 128)
    nv_p = cat("nv_p").transpose(1, 0, 2, 3).reshape(2, 8, SEQ, 2, 128)
    nki_p = cat("nki_p").transpose(1, 0, 2, 3).reshape(2, 8, SEQ, 64)
    nk_s = cat("nk_s").transpose(1, 0, 2, 3).reshape(2, 128, 4, 2, 128)
    nv_s = cat("nv_s").transpose(1, 0, 2, 3).reshape(2, 128, 4, 2, 128)
    nki_s = cat("nki_s").transpose(1, 0, 2, 3).reshape(2, 128, 4, 64)
    cvp = cat("cv_p").transpose(1, 0, 2, 3).reshape(2, 8, 128, D)
    cvs = cat("cv_s").transpose(1, 0, 2, 3).reshape(2, 128, 4, D)
    return (y_prompt, y_sample, np.ascontiguousarray(nk_p), np.ascontiguousarray(nv_p), np.ascontiguousarray(nki_p),
            np.ascontiguousarray(nk_s), np.ascontiguousarray(nv_s), np.ascontiguousarray(nki_s),
            np.ascontiguousarray(cvp), np.ascontiguousarray(cvs))
```
